# Optimizing a Trainium2 kernel written in Bass

```python
import math
import jax, jax.numpy as jnp
from jax import lax
import numpy as np

D_MODEL = 1024
BATCH = 32
SEQ = 2048
DEPTH = 2

GRID_W = 64
CTX_LEN = 256
N_EVEN = (DEPTH + 1) // 2
N_ODD = DEPTH // 2

MLA_HEADS = 8
MLA_NOPE = 64
MLA_ROPE = 32
MLA_V = 64
MLA_Q_RANK = 384
MLA_KV_RANK = 256
DIFF_HEADS = 4
DIFF_HEAD_DIM = 64
DIFF_V_DIM = 2 * DIFF_HEAD_DIM
WIN_Q_HEADS = 8
WIN_KV_HEADS = 2
WIN_GROUP = WIN_Q_HEADS // WIN_KV_HEADS
WIN_HEAD_DIM = 64
WINDOW = 128
BLOCK = 128
HYENA_CH = 512
HYENA_ORDER = 2
HYENA_BANDS = 16
HYENA_EMB = 1 + 2 * HYENA_BANDS
HYENA_HIDDEN = 64
HYENA_DECAY_TARGET = 1e-2
HYENA_FAST_PCT = 0.3
HYENA_SLOW_PCT = 1.5
D_FF = 2816

ROPE_BASE = 10000.0
NORM_EPS = 1e-6
NEG_INF = -1e30
Q_BLOCK = 128

EVEN_SPLIT = (MLA_Q_RANK, MLA_KV_RANK, MLA_ROPE, DIFF_HEADS * 2 * DIFF_HEAD_DIM, DIFF_HEADS * 2 * DIFF_HEAD_DIM, DIFF_HEADS * DIFF_V_DIM)
ODD_SPLIT = (WIN_Q_HEADS * WIN_HEAD_DIM, WIN_KV_HEADS * WIN_HEAD_DIM, WIN_KV_HEADS * WIN_HEAD_DIM, (HYENA_ORDER + 1) * HYENA_CH)
EVEN_IN = sum(EVEN_SPLIT)
ODD_IN = sum(ODD_SPLIT)
MIX_WIDTH = MLA_HEADS * MLA_V + DIFF_HEADS * DIFF_V_DIM

kernel_name = "hybrid_mla_diff_swa_hyena_prefix_block"


def split_cols(t, widths):
    return jnp.split(t, [int(i) for i in np.cumsum(widths)[:-1]], axis=-1)


def rms_norm(x, g):
    x32 = x.astype(jnp.float32)
    y = x32 * lax.rsqrt(jnp.mean(x32 * x32, axis=-1, keepdims=True) + NORM_EPS)
    return (y * g.astype(jnp.float32)).astype(x.dtype)


def modulate(h, shift, scale):
    return h * (1.0 + scale) + shift


def dwconv3(x, w, b):
    xp = jnp.pad(x, ((0, 0), (1, 1), (0, 0)))
    return xp[:, :-2] * w[0] + xp[:, 1:-1] * w[1] + xp[:, 2:] * w[2] + b


def axial_rope(length, rot_dim):
    rows = length // GRID_W
    row = jnp.repeat(jnp.arange(rows), GRID_W).astype(jnp.float32)
    col = jnp.tile(jnp.arange(GRID_W), rows).astype(jnp.float32)
    quarter = rot_dim // 4
    inv = ROPE_BASE ** (-jnp.arange(quarter, dtype=jnp.float32) / quarter)
    ang = jnp.concatenate([row[:, None] * inv, col[:, None] * inv], axis=-1)
    return jnp.cos(ang), jnp.sin(ang)


def apply_rope(x, cos, sin):
    half = x.shape[-1] // 2
    extra = x.ndim - 3
    cos = cos.reshape(cos.shape[0], *([1] * extra), half).astype(x.dtype)
    sin = sin.reshape(sin.shape[0], *([1] * extra), half).astype(x.dtype)
    x1, x2 = x[..., :half], x[..., half:]
    return jnp.concatenate([x1 * cos - x2 * sin, x1 * sin + x2 * cos], axis=-1)


def attend(q, k, v, scale):
    s = jnp.einsum("bqhd,bkhd->bhqk", q, k).astype(jnp.float32) * scale
    p = jax.nn.softmax(s, axis=-1).astype(v.dtype)
    return jnp.einsum("bhqk,bkhd->bqhd", p, v)


def map_query_blocks(fn, *qs):
    b, length = qs[0].shape[:2]
    nb = length // Q_BLOCK
    blocks = tuple(jnp.swapaxes(q.reshape(b, nb, Q_BLOCK, *q.shape[2:]), 0, 1) for q in qs)
    out = lax.map(lambda args: fn(*args), blocks)
    return jnp.swapaxes(out, 0, 1).reshape(b, length, *out.shape[3:])


def gqa_sink_attend(q, k, v, sink, mask):
    s = jnp.einsum("bqhgd,bkhd->bhgqk", q, k).astype(jnp.float32) * (WIN_HEAD_DIM ** -0.5)
    if mask is not None:
        s = jnp.where(mask, s, NEG_INF)
    sink_col = jnp.broadcast_to(sink.astype(jnp.float32)[None, :, :, None, None], s.shape[:-1] + (1,))
    p = jax.nn.softmax(jnp.concatenate([s, sink_col], axis=-1), axis=-1)[..., :-1].astype(v.dtype)
    return jnp.einsum("bhgqk,bkhd->bqhgd", p, v)


def window_attention(q, k, v, k_ctx, v_ctx, sink):
    b, length = q.shape[:2]
    nb = length // BLOCK
    n_ctx = k_ctx.shape[1]

    def to_blocks(t):
        return jnp.swapaxes(t.reshape(b, nb, BLOCK, *t.shape[2:]), 0, 1)

    def windows(t):
        tp = jnp.pad(t, ((0, 0), (BLOCK, BLOCK), (0, 0), (0, 0))).reshape(b, nb + 2, BLOCK, *t.shape[2:])
        w = jnp.concatenate([tp[:, :-2], tp[:, 1:-1], tp[:, 2:]], axis=2)
        return jnp.swapaxes(w, 0, 1)

    q_pos = jnp.arange(length).reshape(nb, BLOCK)
    k_pos = (jnp.arange(nb)[:, None] - 1) * BLOCK + jnp.arange(3 * BLOCK)[None, :]
    kp = k_pos[:, None, :]
    in_win = (jnp.abs(q_pos[:, :, None] - kp) <= WINDOW) & (kp >= 0) & (kp < length)
    mask = jnp.concatenate([jnp.ones((nb, BLOCK, n_ctx), bool), in_win], axis=-1)

    def block_fn(args):
        qb, kb, vb, mb = args
        kk = jnp.concatenate([k_ctx, kb], axis=1)
        vv = jnp.concatenate([v_ctx, vb], axis=1)
        return gqa_sink_attend(qb, kk, vv, sink, mb)

    out = lax.map(block_fn, (to_blocks(q), windows(k), windows(v), mask))
    return jnp.swapaxes(out, 0, 1).reshape(b, length, *q.shape[2:])


def hyena_filters(length, w1, b1, w2, b2, w3, b3, freq):
    t = jnp.arange(length, dtype=jnp.float32)
    tn = t / max(length - 1, 1)
    bands = jnp.linspace(1e-4, HYENA_BANDS - 1, HYENA_BANDS, dtype=jnp.float32)
    ang = 2.0 * math.pi * bands[None, :] * t[:, None] / length
    feats = jnp.concatenate([tn[:, None], jnp.cos(ang), -jnp.sin(ang)], axis=-1).astype(w1.dtype)
    hid = jnp.sin(freq * (feats @ w1 + b1))
    hid = jnp.sin(freq * (hid @ w2 + b2))
    h = hid @ w3 + b3
    min_decay = math.log(HYENA_DECAY_TARGET) / HYENA_SLOW_PCT
    max_decay = math.log(HYENA_DECAY_TARGET) / HYENA_FAST_PCT
    deltas = jnp.abs(jnp.linspace(min_decay, max_decay, HYENA_CH, dtype=jnp.float32))
    decay = jnp.exp(-tn[:, None] * deltas[None, :]).astype(h.dtype)
    return h.reshape(length, 2, HYENA_ORDER, HYENA_CH) * decay[:, None, None, :]


def bidir_long_conv(z, h_fwd, h_bwd):
    length, ch = z.shape[1], z.shape[2]
    n = 2 * length
    taps = jnp.concatenate([h_fwd, jnp.zeros((1, ch), h_fwd.dtype), h_bwd[1:][::-1]], axis=0).astype(jnp.float32)
    zf = jnp.fft.rfft(z.astype(jnp.float32), n=n, axis=1)
    tf = jnp.fft.rfft(taps, n=n, axis=0)
    y = jnp.fft.irfft(zf * tf[None], n=n, axis=1)[:, :length]
    return y.astype(z.dtype)


def hyena_mix(u, conv_w, conv_b, filt, bias):
    parts = jnp.split(dwconv3(u, conv_w, conv_b), HYENA_ORDER + 1, axis=-1)
    z = parts[0]
    for n in range(HYENA_ORDER):
        z = parts[n + 1] * (bidir_long_conv(z, filt[:, 0, n], filt[:, 1, n]) + bias[n] * z)
    return z


def conv_ffn(h, w_up, conv_w, conv_b, w_down):
    u = dwconv3(h @ w_up, conv_w, conv_b)
    a, g = jnp.split(u, 2, axis=-1)
    return (jax.nn.silu(g) * a) @ w_down


def even_mixer(h_lat, h_ctx, need_ctx, lambda_init, ropes, w_in, q_norm_g, w_uq, kv_norm_g, w_ukv, diff_lambda, diff_subln_g, w_out):
    def project(h, rope):
        b, length = h.shape[:2]
        cq, ckv, k_rope, dq, dk, dv = split_cols(h @ w_in, EVEN_SPLIT)
        q = (rms_norm(cq, q_norm_g) @ w_uq).reshape(b, length, MLA_HEADS, MLA_NOPE + MLA_ROPE)
        kv = (rms_norm(ckv, kv_norm_g) @ w_ukv).reshape(b, length, MLA_HEADS, MLA_NOPE + MLA_V)
        q_nope, q_rope = q[..., :MLA_NOPE], q[..., MLA_NOPE:]
        k_nope, v_mla = kv[..., :MLA_NOPE], kv[..., MLA_NOPE:]
        dq = dq.reshape(b, length, DIFF_HEADS, 2, DIFF_HEAD_DIM)
        dk = dk.reshape(b, length, DIFF_HEADS, 2, DIFF_HEAD_DIM)
        dv = dv.reshape(b, length, DIFF_HEADS, DIFF_V_DIM)
        if rope is not None:
            (cos_m, sin_m), (cos_d, sin_d) = rope
            q_rope = apply_rope(q_rope, cos_m, sin_m)
            k_rope = apply_rope(k_rope, cos_m, sin_m)
            dq = apply_rope(dq, cos_d, sin_d)
            dk = apply_rope(dk, cos_d, sin_d)
        k_rope = jnp.broadcast_to(k_rope[:, :, None, :], (b, length, MLA_HEADS, MLA_ROPE))
        q_mla = jnp.concatenate([q_nope, q_rope], axis=-1)
        k_mla = jnp.concatenate([k_nope, k_rope], axis=-1)
        return (q_mla, k_mla, v_mla, dq[:, :, :, 0], dq[:, :, :, 1], dk[:, :, :, 0], dk[:, :, :, 1], dv)

    lq1, lk1, lq2, lk2 = diff_lambda.astype(jnp.float32)
    lam = jnp.exp(jnp.sum(lq1 * lk1)) - jnp.exp(jnp.sum(lq2 * lk2)) + lambda_init
    mla_scale = (MLA_NOPE + MLA_ROPE) ** -0.5
    diff_scale = DIFF_HEAD_DIM ** -0.5

    def diff_attend(q1, q2, k1, k2, v):
        return attend(q1, k1, v, diff_scale) - lam.astype(v.dtype) * attend(q2, k2, v, diff_scale)

    def merge(o_mla, o_diff):
        b, length = o_mla.shape[:2]
        o_diff = rms_norm(o_diff, diff_subln_g) * (1.0 - lambda_init)
        return jnp.concatenate([o_mla.reshape(b, length, -1), o_diff.reshape(b, length, -1)], axis=-1) @ w_out

    cq_, ck_, cv_, cdq1, cdq2, cdk1, cdk2, cdv = project(h_ctx, None)
    lq_, lk_, lv_, ldq1, ldq2, ldk1, ldk2, ldv = project(h_lat, ropes)

    def cat(a, b):
        return jnp.concatenate([a, b], axis=1)

    k_all, v_all = cat(ck_, lk_), cat(cv_, lv_)
    dk1_all, dk2_all, dv_all = cat(cdk1, ldk1), cat(cdk2, ldk2), cat(cdv, ldv)
    o_mla = map_query_blocks(lambda qb: attend(qb, k_all, v_all, mla_scale), lq_)
    o_diff = map_query_blocks(lambda q1b, q2b: diff_attend(q1b, q2b, dk1_all, dk2_all, dv_all), ldq1, ldq2)
    y_lat = merge(o_mla, o_diff)
    y_ctx = None
    if need_ctx:
        y_ctx = merge(attend(cq_, ck_, cv_, mla_scale), diff_attend(cdq1, cdq2, cdk1, cdk2, cdv))
    return y_lat, y_ctx


def odd_mixer(h_lat, h_ctx, need_ctx, rope, w_in, sink, conv_w, conv_b, f_w1, f_b1, f_w2, f_b2, f_w3, f_b3, f_freq, hy_bias, w_out):
    sink = sink.reshape(WIN_KV_HEADS, WIN_GROUP)

    def project(h, rope_tab):
        b, length = h.shape[:2]
        q, k, v, u = split_cols(h @ w_in, ODD_SPLIT)
        q = q.reshape(b, length, WIN_KV_HEADS, WIN_GROUP, WIN_HEAD_DIM)
        k = k.reshape(b, length, WIN_KV_HEADS, WIN_HEAD_DIM)
        v = v.reshape(b, length, WIN_KV_HEADS, WIN_HEAD_DIM)
        if rope_tab is not None:
            q = apply_rope(q, *rope_tab)
            k = apply_rope(k, *rope_tab)
        return q, k, v, u

    def hyena(u):
        filt = hyena_filters(u.shape[1], f_w1, f_b1, f_w2, f_b2, f_w3, f_b3, f_freq)
        return hyena_mix(u, conv_w, conv_b, filt, hy_bias)

    def merge(o_win, o_hy):
        b, length = o_win.shape[:2]
        return jnp.concatenate([o_win.reshape(b, length, -1), o_hy], axis=-1) @ w_out

    cq, ck, cv, cu = project(h_ctx, None)
    lq, lk, lv, lu = project(h_lat, rope)
    y_lat = merge(window_attention(lq, lk, lv, ck, cv, sink), hyena(lu))
    y_ctx = None
    if need_ctx:
        y_ctx = merge(gqa_sink_attend(cq, ck, cv, sink, None), hyena(cu))
    return y_lat, y_ctx


def setup_inputs(seed: int = 0) -> dict:
    key = jax.random.key(seed)
    ks = iter(jax.random.split(key, 40))

    def nrm(shape, scale):
        return jax.random.normal(next(ks), shape, jnp.float32) * scale

    def gain(shape):
        return 1.0 + nrm(shape, 0.05)

    return {
        "x": nrm((BATCH, SEQ, D_MODEL), 1.0),
        "c": nrm((BATCH, D_MODEL), 1.0),
        "ctx": nrm((BATCH, CTX_LEN, D_MODEL), 1.0),
        "c_ctx": nrm((D_MODEL,), 1.0),
        "ada_w": nrm((DEPTH, D_MODEL, 6 * D_MODEL), 0.5 * D_MODEL ** -0.5),
        "ada_b": nrm((DEPTH, 6 * D_MODEL), 0.02),
        "norm_g": gain((DEPTH, 4, D_MODEL)),
        "mix_w_out": nrm((DEPTH, MIX_WIDTH, D_MODEL), MIX_WIDTH ** -0.5),
        "ffn_w_up": nrm((DEPTH, D_MODEL, 2 * D_FF), D_MODEL ** -0.5),
        "ffn_conv_w": nrm((DEPTH, 3, 2 * D_FF), 3 ** -0.5),
        "ffn_conv_b": nrm((DEPTH, 2 * D_FF), 0.02),
        "ffn_w_down": nrm((DEPTH, D_FF, D_MODEL), D_FF ** -0.5),
        "even_w_in": nrm((N_EVEN, D_MODEL, EVEN_IN), D_MODEL ** -0.5),
        "mla_q_norm_g": gain((N_EVEN, MLA_Q_RANK)),
        "mla_w_uq": nrm((N_EVEN, MLA_Q_RANK, MLA_HEADS * (MLA_NOPE + MLA_ROPE)), MLA_Q_RANK ** -0.5),
        "mla_kv_norm_g": gain((N_EVEN, MLA_KV_RANK)),
        "mla_w_ukv": nrm((N_EVEN, MLA_KV_RANK, MLA_HEADS * (MLA_NOPE + MLA_V)), MLA_KV_RANK ** -0.5),
        "diff_lambda": nrm((N_EVEN, 4, DIFF_HEAD_DIM), 0.1),
        "diff_subln_g": gain((N_EVEN, DIFF_V_DIM)),
        "odd_w_in": nrm((N_ODD, D_MODEL, ODD_IN), D_MODEL ** -0.5),
        "win_sink": nrm((N_ODD, WIN_Q_HEADS), 0.5),
        "hy_conv_w": nrm((N_ODD, 3, (HYENA_ORDER + 1) * HYENA_CH), 3 ** -0.5),
        "hy_conv_b": nrm((N_ODD, (HYENA_ORDER + 1) * HYENA_CH), 0.02),
        "hy_f_w1": nrm((N_ODD, HYENA_EMB, HYENA_HIDDEN), HYENA_EMB ** -0.5),
        "hy_f_b1": nrm((N_ODD, HYENA_HIDDEN), 0.1),
        "hy_f_w2": nrm((N_ODD, HYENA_HIDDEN, HYENA_HIDDEN), HYENA_HIDDEN ** -0.5),
        "hy_f_b2": nrm((N_ODD, HYENA_HIDDEN), 0.1),
        "hy_f_w3": nrm((N_ODD, HYENA_HIDDEN, 2 * HYENA_ORDER * HYENA_CH), 0.1 * HYENA_HIDDEN ** -0.5),
        "hy_f_b3": nrm((N_ODD, 2 * HYENA_ORDER * HYENA_CH), 0.01),
        "hy_f_freq": 1.0 + nrm((N_ODD, HYENA_HIDDEN), 0.1),
        "hy_bias": nrm((N_ODD, HYENA_ORDER, HYENA_CH), 0.5),
    }


def reference(x, c, ctx, c_ctx, ada_w, ada_b, norm_g, mix_w_out, ffn_w_up, ffn_conv_w, ffn_conv_b, ffn_w_down, even_w_in, mla_q_norm_g, mla_w_uq, mla_kv_norm_g, mla_w_ukv, diff_lambda, diff_subln_g, odd_w_in, win_sink, hy_conv_w, hy_conv_b, hy_f_w1, hy_f_b1, hy_f_w2, hy_f_b2, hy_f_w3, hy_f_b3, hy_f_freq, hy_bias):
    seq = x.shape[1]
    rope_mla = axial_rope(seq, MLA_ROPE)
    rope_diff = axial_rope(seq, DIFF_HEAD_DIM)
    rope_win = axial_rope(seq, WIN_HEAD_DIM)
    sc = jax.nn.silu(c)
    sc_ctx = jax.nn.silu(c_ctx)
    xl, xc = x, ctx
    for layer in range(DEPTH):
        need_ctx = layer < DEPTH - 1
        mod_l = jnp.split((sc @ ada_w[layer] + ada_b[layer])[:, None, :], 6, axis=-1)
        mod_c = jnp.split((sc_ctx @ ada_w[layer] + ada_b[layer])[None, None, :], 6, axis=-1)
        g = norm_g[layer]
        h_l = modulate(rms_norm(xl, g[0]), mod_l[0], mod_l[1])
        h_c = modulate(rms_norm(xc, g[0]), mod_c[0], mod_c[1])
        i = layer // 2
        if layer % 2 == 0:
            lambda_init = 0.8 - 0.6 * math.exp(-0.3 * layer)
            y_l, y_c = even_mixer(h_l, h_c, need_ctx, lambda_init, (rope_mla, rope_diff), even_w_in[i], mla_q_norm_g[i], mla_w_uq[i], mla_kv_norm_g[i], mla_w_ukv[i], diff_lambda[i], diff_subln_g[i], mix_w_out[layer])
        else:
            y_l, y_c = odd_mixer(h_l, h_c, need_ctx, rope_win, odd_w_in[i], win_sink[i], hy_conv_w[i], hy_conv_b[i], hy_f_w1[i], hy_f_b1[i], hy_f_w2[i], hy_f_b2[i], hy_f_w3[i], hy_f_b3[i], hy_f_freq[i], hy_bias[i], mix_w_out[layer])
        ffn_args = (ffn_w_up[layer], ffn_conv_w[layer], ffn_conv_b[layer], ffn_w_down[layer])
        xl = xl + mod_l[2] * rms_norm(y_l, g[1])
        h_l = modulate(rms_norm(xl, g[2]), mod_l[3], mod_l[4])
        xl = xl + mod_l[5] * rms_norm(conv_ffn(h_l, *ffn_args), g[3])
        if need_ctx:
            xc = xc + mod_c[2] * rms_norm(y_c, g[1])
            h_c2 = modulate(rms_norm(xc, g[2]), mod_c[3], mod_c[4])
            xc = xc + mod_c[5] * rms_norm(conv_ffn(h_c2, *ffn_args), g[3])
    return xl
```

```python
import contextlib
import math
import numpy as np
import ml_dtypes
import concourse.bass as bass
import concourse.mybir as mybir
from concourse.bass_utils import run_bass_kernel_spmd

F32 = mybir.dt.float32
BF = mybir.dt.bfloat16
AF = mybir.ActivationFunctionType
ALU = mybir.AluOpType
ENGS = ("pe", "act", "dve", "pool", "sp")

D = 1024
L = 2048
C = 256
NT = L + C
DFF = 2816
EPS = 1e-6


def _box(ap):
    t = ap.tensor
    dims = list(ap.ap)
    off = int(ap.offset)
    if t.__class__.__name__.startswith("DRam"):
        hi = off + sum((int(c) - 1) * abs(int(s)) for s, c in dims) + 1
        return (t.name, 0, 1, off, hi)
    row = 1
    for s in list(t.shape)[1:]:
        row *= int(s)
    p0 = off // row
    f0 = off % row
    pstep, pcnt = int(dims[0][0]), int(dims[0][1])
    npart = pcnt if (pstep == row or pcnt == 1) else 128 - p0
    ext = sum((int(c) - 1) * abs(int(s)) for s, c in dims[1:]) + 1
    if t.__class__.__name__.startswith("PSum"):
        return (t.name, 0, 128, 0, row)
    return (t.name, p0, p0 + npart, f0, f0 + ext)


def _overlap(a, b):
    return a[1] < b[2] and b[1] < a[2] and a[3] < b[4] and b[3] < a[4]


def _covers(a, b):
    return a[1] <= b[1] and a[2] >= b[2] and a[3] <= b[3] and a[4] >= b[4]


class Op:
    __slots__ = ("eng", "fn", "deps", "sig", "signaled", "is_dma", "slot", "slotn", "clock", "mm", "tag")

    def __init__(self, eng, fn, is_dma=False, mm=False):
        self.eng = eng
        self.fn = fn
        self.deps = []
        self.sig = 0
        self.signaled = False
        self.is_dma = is_dma
        self.slot = -1
        self.slotn = 0
        self.clock = None
        self.mm = mm


class Sched:
    def __init__(self, nc, n_dma_slots=32):
        self.nc = nc
        self.ops = []
        self.recs = {}
        self.nslots = n_dma_slots
        self.eobj = {"pe": nc.tensor, "act": nc.scalar, "dve": nc.vector, "pool": nc.gpsimd, "sp": nc.sync}
        self.last = {e: None for e in ENGS}
        self.dmas_since = []
        self.bar = {e: None for e in ENGS}
        self.stack = contextlib.ExitStack()
        self.esem = {e: self.stack.enter_context(nc.semaphore("s_" + e)) for e in ENGS}
        self.dsem = [self.stack.enter_context(nc.semaphore("d_%d" % i)) for i in range(n_dma_slots)]
        self.cnt = {e: 0 for e in ENGS}
        self.rr = 0
        self.slot_cnt = [0] * n_dma_slots
        self.slot_last = [None] * n_dma_slots
        self.known = {e: ({x: 0 for x in ENGS}, [0] * n_dma_slots) for e in ENGS}
        self.n_emitted = 0
        self.tag = ""
        self.names = None

    def add(self, eng, fn, reads=(), writes=(), is_dma=False, mm=False):
        op = Op(eng, fn, is_dma, mm)
        op.tag = self.tag
        deps = {}
        rb = [_box(a) for a in reads]
        wb = [_box(a) for a in writes]
        for b in rb:
            lst = self.recs.get(b[0])
            if lst:
                psum = b[0].startswith("ps")
                for r in lst:
                    if _overlap(r[0], b) and (r[2] or (psum and r[1].eng != eng)):
                        deps[id(r[1])] = r[1]
        for b in wb:
            lst = self.recs.get(b[0])
            if lst:
                for r in lst:
                    if _overlap(r[0], b):
                        if mm and r[2] and r[1].mm:
                            continue
                        deps[id(r[1])] = r[1]
        if self.bar[eng] is not None:
            for d in self.bar[eng]:
                deps[id(d)] = d
            self.bar[eng] = None
        op.deps = list(deps.values())
        for b in wb:
            lst = self.recs.setdefault(b[0], [])
            lst[:] = [r for r in lst if not _covers(b, r[0])]
            lst.append([b, op, True])
        for b in rb:
            lst = self.recs.setdefault(b[0], [])
            if not is_dma:
                lst[:] = [r for r in lst if not ((not r[2]) and r[0] == b and r[1].eng == eng and not r[1].is_dma)]
            lst.append([b, op, False])
        self.ops.append(op)
        if is_dma:
            self.dmas_since.append(op)
        else:
            self.last[eng] = op
        return op

    def barrier(self):
        for o in self.last.values():
            if o is not None:
                o.signaled = True
        self.flush()
        b = [o for o in self.last.values() if o is not None] + list(self.dmas_since)
        self.dmas_since = []
        for e in ENGS:
            self.bar[e] = list(b) + (self.bar[e] or [])
        self.recs = {}

    def mm(self, out, lhsT, rhs, start=True, stop=True):
        nc = self.nc
        return self.add("pe", lambda: nc.tensor.matmul(out, lhsT, rhs, start=start, stop=stop),
                        reads=[lhsT, rhs], writes=[out], mm=True)

    def mmacc(self, out, pairs):
        n = len(pairs)
        for i, (l, r) in enumerate(pairs):
            self.mm(out, l, r, start=(i == 0), stop=(i == n - 1))

    def transpose(self, out, in_, ident):
        nc = self.nc
        return self.add("pe", lambda: nc.tensor.transpose(out, in_, ident), reads=[in_, ident], writes=[out], mm=True)

    def act(self, out, in_, func, bias=None, scale=None, accum_out=None):
        nc = self.nc
        kw = {}
        rd = [in_]
        wr = [out]
        if bias is not None:
            kw["bias"] = bias
            if not isinstance(bias, (int, float)):
                rd.append(bias)
        if scale is not None:
            kw["scale"] = scale
            if not isinstance(scale, (int, float)):
                rd.append(scale)
        if accum_out is not None:
            kw["accum_out"] = accum_out
            wr.append(accum_out)
        return self.add("act", lambda: nc.scalar.activation(out, in_, func, **kw), reads=rd, writes=wr)

    def tt(self, out, in0, in1, op, eng="dve"):
        e = self.eobj[eng]
        return self.add(eng, lambda: e.tensor_tensor(out, in0, in1, op), reads=[in0, in1], writes=[out])

    def ts(self, out, in0, s1, s2, op0, op1=None, eng="dve"):
        e = self.eobj[eng]
        rd = [in0] + [s for s in (s1, s2) if s is not None and not isinstance(s, (int, float))]
        if op1 is None:
            return self.add(eng, lambda: e.tensor_scalar(out, in0, s1, None, op0), reads=rd, writes=[out])
        return self.add(eng, lambda: e.tensor_scalar(out, in0, s1, s2, op0, op1), reads=rd, writes=[out])

    def stt(self, out, in0, scalar, in1, op0, op1, eng="dve"):
        e = self.eobj[eng]
        rd = [in0, in1] + ([] if isinstance(scalar, (int, float)) else [scalar])
        return self.add(eng, lambda: e.scalar_tensor_tensor(out, in0, scalar, in1, op0, op1), reads=rd, writes=[out])

    def copy(self, out, in_, eng="dve"):
        e = self.eobj[eng]
        if eng == "act":
            return self.add(eng, lambda: e.copy(out, in_), reads=[in_], writes=[out])
        return self.add(eng, lambda: e.tensor_copy(out, in_), reads=[in_], writes=[out])

    def memset(self, ap, val, eng="dve"):
        e = self.eobj[eng]
        return self.add(eng, lambda: e.memset(ap, val), reads=[], writes=[ap])

    def recip(self, out, in_):
        nc = self.nc
        return self.add("dve", lambda: nc.vector.reciprocal(out, in_), reads=[in_], writes=[out])

    def dma(self, out, in_, q="sp", **kw):
        e = self.eobj[q]
        return self.add(q, lambda: e.dma_start(out, in_, **kw), reads=[in_], writes=[out], is_dma=True)

    def flush(self):
        ops = self.ops
        self.ops = []
        ns = self.nslots
        for op in ops:
            for d in op.deps:
                d.signaled = True
        for op in ops:
            if op.is_dma:
                op.slot = self.rr
                self.rr = (self.rr + 1) % ns
                self.slot_cnt[op.slot] += 1
                op.slotn = self.slot_cnt[op.slot]
            elif op.signaled:
                self.cnt[op.eng] += 1
                op.sig = self.cnt[op.eng]
        esem, dsem, slot_last = self.esem, self.dsem, self.slot_last
        for op in ops:
            e = self.eobj[op.eng]
            ke, kd = self.known[op.eng]
            need = op.deps
            if op.is_dma and slot_last[op.slot] is not None:
                need = need + [slot_last[op.slot]]
            for d in need:
                if d.is_dma:
                    if kd[d.slot] >= d.slotn:
                        continue
                    e.wait_ge(dsem[d.slot], 16 * d.slotn)
                else:
                    if ke[d.eng] >= d.sig:
                        continue
                    e.wait_ge(esem[d.eng], d.sig)
                ce, cd = d.clock
                for x in ENGS:
                    if ce[x] > ke[x]:
                        ke[x] = ce[x]
                for i in range(ns):
                    if cd[i] > kd[i]:
                        kd[i] = cd[i]
            inst = op.fn()
            if self.names is not None:
                self.names[inst.ins.name] = op.tag
            if op.is_dma:
                inst.then_inc(dsem[op.slot], 16)
                slot_last[op.slot] = op
                cd2 = list(kd)
                cd2[op.slot] = max(cd2[op.slot], op.slotn)
                op.clock = (dict(ke), cd2)
            elif op.signaled:
                inst.then_inc(esem[op.eng], 1)
                ce2 = dict(ke)
                ce2[op.eng] = max(ce2[op.eng], op.sig)
                op.clock = (ce2, list(kd))
            op.fn = None
            op.deps = None
        self.n_emitted += len(ops)

    def finish(self):
        self.barrier()
        for i in range(self.nslots):
            if self.slot_cnt[i]:
                self.nc.sync.wait_ge(self.dsem[i], 16 * self.slot_cnt[i])
        self.stack.close()


class RR:
    def __init__(self, items):
        self.items = items
        self.i = 0

    def get(self):
        x = self.items[self.i % len(self.items)]
        self.i += 1
        return x


def _rope_tables(rot_dim):
    rows = L // 64
    row = np.repeat(np.arange(rows), 64).astype(np.float32)
    col = np.tile(np.arange(64), rows).astype(np.float32)
    quarter = rot_dim // 4
    inv = (np.float32(10000.0) ** (-np.arange(quarter, dtype=np.float32) / quarter)).astype(np.float32)
    ang = np.concatenate([row[:, None] * inv, col[:, None] * inv], axis=-1).astype(np.float32)
    cos = np.cos(ang).astype(np.float32).T
    sin = np.sin(ang).astype(np.float32).T
    cos2 = np.concatenate([cos, cos], 0)
    sin2 = np.concatenate([-sin, sin], 0)
    return cos2, sin2


_CONST = {}


def _consts():
    if _CONST:
        return _CONST
    cm, sm = _rope_tables(32)
    ropeM = np.zeros((128, 2, L), np.float32)
    ropeM[64:96, 0] = cm
    ropeM[64:96, 1] = sm
    cd, sd = _rope_tables(64)
    ropeD = np.zeros((128, 2, L), np.float32)
    ropeD[0:64, 0] = cd
    ropeD[64:128, 0] = cd
    ropeD[0:64, 1] = sd
    ropeD[64:128, 1] = sd
    k = np.arange(128)[:, None]
    q = np.arange(128)[None, :]
    m_prev = (q <= k).astype(np.float32)
    m_next = (k <= q).astype(np.float32)
    wmask = np.stack([np.tile(m_prev, (1, 4)), np.tile(m_next, (1, 4))], 1).astype(ml_dtypes.bfloat16)
    t = np.arange(L, dtype=np.int64)
    f = np.arange(L, dtype=np.int64)
    m = (t[:, None] * f[None, :]) % (2 * L)
    th = 2.0 * np.pi * m.astype(np.float64) / (2 * L)
    Cm = np.cos(th)
    Sm = np.sin(th)
    alt = np.where(t % 2 == 0, 1.0, -1.0)
    FM = np.concatenate([Cm, -Sm], axis=1)
    FM[:, L] = alt
    N = 2 * L
    IMr = 2.0 * Cm.T / N
    IMr[0, :] = 1.0 / N
    IMi = -2.0 * Sm.T / N
    IMi[0, :] = alt / N
    IM = np.concatenate([IMr, IMi], axis=0)
    tf = np.arange(L, dtype=np.float32)
    tn = tf / np.float32(L - 1)
    bands = np.linspace(1e-4, 15, 16, dtype=np.float32)
    ang = (np.float32(2.0 * math.pi) * bands[None, :] * tf[:, None] / np.float32(L)).astype(np.float32)
    feats = np.concatenate([tn[:, None], np.cos(ang), -np.sin(ang)], axis=-1).astype(np.float32)
    min_decay = math.log(1e-2) / 1.5
    max_decay = math.log(1e-2) / 0.3
    deltas = np.abs(np.linspace(min_decay, max_decay, 512, dtype=np.float32))
    decay = np.exp(-tn[:, None] * deltas[None, :]).astype(np.float32)
    ident = np.eye(128, dtype=np.float32)
    _CONST.update(dict(ropeM=ropeM, ropeD=ropeD, wmask=wmask, FM=FM.astype(ml_dtypes.bfloat16),
                       IM=IM.astype(ml_dtypes.bfloat16), featsT=np.ascontiguousarray(feats.T), decay=decay,
                       ident=ident))
    return _CONST


def _swap_halves(w, block):
    k, n = w.shape
    return np.ascontiguousarray(w.reshape(k, n // block, 2, block // 2)[:, :, ::-1, :].reshape(k, n))


def _colsT(v, ntile):
    return np.ascontiguousarray(np.asarray(v, np.float32).reshape(ntile, 128).T)


def _host_layout(inp, NS, core):
    f = lambda a: np.ascontiguousarray(np.asarray(a, np.float32))
    cst = _consts()
    b0 = core * NS
    m = {}
    m["x"] = f(inp["x"][b0:b0 + NS])
    m["ctx"] = f(inp["ctx"][b0:b0 + NS])
    rows = np.concatenate([f(inp["c"][b0:b0 + NS]), f(inp["c_ctx"])[None, :]], 0)
    m["cT"] = np.ascontiguousarray(rows.reshape(NS + 1, 8, 128).transpose(2, 1, 0))
    m["ada_w"] = f(inp["ada_w"])
    m["ada_bT"] = np.ascontiguousarray(f(inp["ada_b"]).reshape(2, 48, 128).transpose(2, 0, 1))
    m["norm_gT"] = np.ascontiguousarray(f(inp["norm_g"]).reshape(2, 4, 8, 128).transpose(3, 0, 1, 2))
    m["w_out"] = f(inp["mix_w_out"])
    m["w_up"] = f(inp["ffn_w_up"])
    m["w_down"] = f(inp["ffn_w_down"])
    m["fcw"] = np.ascontiguousarray(f(inp["ffn_conv_w"]).reshape(2, 3, 44, 128).transpose(3, 0, 2, 1))
    m["fcb"] = np.ascontiguousarray(f(inp["ffn_conv_b"]).reshape(2, 44, 128).transpose(2, 0, 1))
    we = f(inp["even_w_in"][0])
    cq, ckv, kr, dq, dk, dv = np.split(we, np.cumsum([384, 256, 32, 512, 512])[:], axis=1)
    m["w0in"] = np.ascontiguousarray(np.concatenate(
        [cq, ckv, dq, _swap_halves(dq, 64), dk, _swap_halves(dk, 64), dv, kr, _swap_halves(kr, 32)], 1))
    uq = f(inp["mla_w_uq"][0]).reshape(384, 8, 96)
    uq_rot = _swap_halves(np.ascontiguousarray(uq[:, :, 64:]).reshape(384, 256), 32)
    m["w_uq"] = np.ascontiguousarray(np.concatenate([uq.reshape(384, 768), uq_rot], 1))
    ukv = f(inp["mla_w_ukv"][0]).reshape(256, 8, 128)
    m["w_ukv"] = np.ascontiguousarray(np.concatenate([ukv[:, :, :64].reshape(256, 512), ukv[:, :, 64:].reshape(256, 512)], 1))
    m["qngT"] = _colsT(inp["mla_q_norm_g"][0], 3)
    m["kvngT"] = _colsT(inp["mla_kv_norm_g"][0], 2)
    m["subgT"] = _colsT(inp["diff_subln_g"][0], 1)
    m["dlam"] = f(inp["diff_lambda"][0]).reshape(1, 256)
    wo = f(inp["odd_w_in"][0])
    q, k_, v_, u = np.split(wo, np.cumsum([512, 128, 128]), axis=1)
    m["w1in"] = np.ascontiguousarray(np.concatenate([q, _swap_halves(q, 64), k_, _swap_halves(k_, 64), v_, u], 1))
    m["sink"] = f(inp["win_sink"][0]).reshape(1, 8)
    m["hcw"] = np.ascontiguousarray(f(inp["hy_conv_w"][0]).reshape(3, 12, 128).transpose(2, 1, 0))
    m["hcb"] = _colsT(inp["hy_conv_b"][0], 12)
    m["hbias"] = np.ascontiguousarray(f(inp["hy_bias"][0]).reshape(2, 4, 128).transpose(2, 0, 1))
    m["hw1"] = f(inp["hy_f_w1"][0])
    m["hb1"] = f(inp["hy_f_b1"][0]).reshape(64, 1)
    m["hw2"] = f(inp["hy_f_w2"][0])
    m["hb2"] = f(inp["hy_f_b2"][0]).reshape(64, 1)
    m["hw3"] = f(inp["hy_f_w3"][0])
    m["hb3"] = f(inp["hy_f_b3"][0]).reshape(1, 2048)
    m["hfreq"] = f(inp["hy_f_freq"][0]).reshape(64, 1)
    for kk in ("ropeM", "ropeD", "wmask", "FM", "IM", "featsT", "decay", "ident"):
        m[kk] = cst[kk]
    return m


def build_program(NS, dbg=None, stop=None, names=None):
    R = NS + 1
    nc = bass.Bass("TRN2", target_bir_lowering=False)
    S = Sched(nc)
    S.names = names
    uid = [0]

    def din(name, shape, dt=F32):
        return nc.dram_tensor(name, list(shape), dt, kind="ExternalInput").ap()

    def dscr(name, shape, dt):
        return nc.dram_tensor(name, list(shape), dt, kind="Internal").ap()

    I = {}
    I["x"] = din("x", [NS, L, D])
    I["ctx"] = din("ctx", [NS, C, D])
    I["cT"] = din("cT", [128, 8, R])
    I["ada_w"] = din("ada_w", [2, D, 6 * D])
    I["ada_bT"] = din("ada_bT", [128, 2, 48])
    I["norm_gT"] = din("norm_gT", [128, 2, 4, 8])
    I["w_out"] = din("w_out", [2, D, D])
    I["w_up"] = din("w_up", [2, D, 2 * DFF])
    I["w_down"] = din("w_down", [2, DFF, D])
    I["fcw"] = din("fcw", [128, 2, 44, 3])
    I["fcb"] = din("fcb", [128, 2, 44])
    I["w0in"] = din("w0in", [D, 3264])
    I["w_uq"] = din("w_uq", [384, 1024])
    I["w_ukv"] = din("w_ukv", [256, 1024])
    I["qngT"] = din("qngT", [128, 3])
    I["kvngT"] = din("kvngT", [128, 2])
    I["subgT"] = din("subgT", [128, 1])
    I["dlam"] = din("dlam", [1, 256])
    I["w1in"] = din("w1in", [D, 2944])
    I["sink"] = din("sink", [1, 8])
    I["hcw"] = din("hcw", [128, 12, 3])
    I["hcb"] = din("hcb", [128, 12])
    I["hbias"] = din("hbias", [128, 2, 4])
    I["hw1"] = din("hw1", [33, 64])
    I["hb1"] = din("hb1", [64, 1])
    I["hw2"] = din("hw2", [64, 64])
    I["hb2"] = din("hb2", [64, 1])
    I["hw3"] = din("hw3", [64, 2048])
    I["hb3"] = din("hb3", [1, 2048])
    I["hfreq"] = din("hfreq", [64, 1])
    I["ropeM"] = din("ropeM", [128, 2, L])
    I["ropeD"] = din("ropeD", [128, 2, L])
    I["wmask"] = din("wmask", [128, 2, 512], BF)
    I["FM"] = din("FM", [L, 2 * L], BF)
    I["IM"] = din("IM", [2 * L, L], BF)
    I["featsT"] = din("featsT", [33, L])
    I["decay"] = din("decay", [L, 512])
    I["ident"] = din("ident", [128, 128])
    OUT = nc.dram_tensor("y", [NS, L, D], F32, kind="ExternalOutput").ap()
    DBG = {}
    if dbg:
        for name, shape in dbg.items():
            DBG[name] = nc.dram_tensor("dbg_" + name, list(shape), F32, kind="ExternalOutput").ap()

    XD = dscr("XD", [NS, NT, D], F32)
    GSC = dscr("GSC", [2, R, 2, D], F32)
    SPEC = dscr("SPEC", [2, 3, L, 512], F32)
    X12 = dscr("X12", [2, 512, L], F32)
    WB = {}
    for nm, shp in (("w0in", [D, 3264]), ("w_uq", [384, 1024]), ("w_ukv", [256, 1024]), ("w1in", [D, 2944]),
                    ("w_out", [2 * D, D]), ("w_up", [2 * D, 2 * DFF]), ("w_down", [2 * DFF, D])):
        WB[nm] = dscr("wb_" + nm, shp, BF)

    es = contextlib.ExitStack()

    @contextlib.contextmanager
    def phase():
        st_ = contextlib.ExitStack()
        try:
            yield st_
            S.barrier()
        finally:
            st_.close()


    live = [0, 0]
    SB_LIMIT = 229344 - 16481 - 4096

    def sb(shape, dt, name=None, stack=None):
        uid[0] += 1
        nb = 1
        for d_ in shape[1:]:
            nb *= int(d_)
        nb *= 2 if dt == BF else 4
        nb = (nb + 31) // 32 * 32
        live[0] += nb
        live[1] = max(live[1], live[0])
        assert live[0] <= SB_LIMIT, ("SBUF over budget", name, live[0])
        stk = stack or es

        def _rel():
            live[0] -= nb
        stk.callback(_rel)
        return stk.enter_context(nc.sbuf_tensor("%s_%d" % (name or "t", uid[0]), list(shape), dt))

    with es:
        PS = [es.enter_context(nc.psum_tensor("ps%d" % i, [128, 512], F32)) for i in range(8)]
        ident = sb([128, 128], F32, "ident")
        ones_bf = sb([128, 128], BF, "ones")
        modA = sb([128, 2, 4, R, 8], F32, "modA")
        fcw = sb([128, 2, 44, 3], F32, "fcw")
        fcb = sb([128, 2, 44], F32, "fcb")
        qng = sb([128, 3], F32, "qng")
        kvng = sb([128, 2], F32, "kvng")
        subg = sb([128, 1], F32, "subg")
        nlam = sb([128, 1], F32, "nlam")
        esink = sb([128, 8], F32, "esink")
        hcw = sb([128, 12, 3], F32, "hcw")
        hcb = sb([128, 12], F32, "hcb")
        hbias = sb([128, 2, 4], F32, "hbias")
        small = sb([128, 64], F32, "small")
        smallrr = [0]

        def scol():
            smallrr[0] = (smallrr[0] + 1) % 64
            return small[:, smallrr[0]:smallrr[0] + 1]

        S.dma(ident[:], I["ident"])
        S.memset(ones_bf[:], 1.0)
        S.dma(fcw[:], I["fcw"])
        S.dma(fcb[:], I["fcb"])
        S.dma(qng[:], I["qngT"])
        S.dma(kvng[:], I["kvngT"])
        S.dma(subg[:], I["subgT"])
        S.dma(hcw[:], I["hcw"])
        S.dma(hcb[:], I["hcb"])
        S.dma(hbias[:], I["hbias"])

        def rstd_from(ss, n_feat, np_=128):
            a = scol()
            S.act(a[0:np_], ss, AF.Sqrt, bias=EPS, scale=1.0 / n_feat)
            b = scol()
            S.recip(b[0:np_], a[0:np_])
            return b

        def prologue_cast():
            S.tag = "cast"
            with phase() as ps_:
                CH = 4096
                fbuf = RR([sb([128, CH], F32, "cf", ps_) for _ in range(3)])
                bbuf = RR([sb([128, CH], BF, "cb", ps_) for _ in range(3)])
                engs = RR(["dve", "act", "pool"])
                srcs = [("w0in", I["w0in"]), ("w_uq", I["w_uq"]), ("w_ukv", I["w_ukv"]), ("w1in", I["w1in"]),
                        ("w_out", I["w_out"].rearrange("l k n -> (l k) n")),
                        ("w_up", I["w_up"].rearrange("l k n -> (l k) n")),
                        ("w_down", I["w_down"].rearrange("l k n -> (l k) n"))]
                for nm, src in srcs:
                    rows, cols = src.shape
                    tot = rows * cols
                    per = tot // 128
                    sflat = src.rearrange("(p a) n -> p (a n)", p=128)
                    dflat = WB[nm].rearrange("(p a) n -> p (a n)", p=128)
                    o = 0
                    while o < per:
                        n = min(CH, per - o)
                        fb = fbuf.get()
                        bb = bbuf.get()
                        S.dma(fb[:, 0:n], sflat[:, o:o + n])
                        S.copy(bb[:, 0:n], fb[:, 0:n], eng=engs.get())
                        S.dma(dflat[:, o:o + n], bb[:, 0:n])
                        o += n
            S.barrier()

        def prologue_mod():
            S.tag = "mod"
            with phase() as ps_:
                scT = sb([128, 8, R], F32, "scT", ps_)
                gT = sb([128, 2, 4, 8], F32, "gT", ps_)
                abT = sb([128, 2, 48], F32, "abT", ps_)
                modT = sb([128, 48, R], F32, "modT", ps_)
                PG = sb([128, 2, 2, R, 8], F32, "PG", ps_)
                rowt = sb([8, 128], F32, "rowt", ps_)
                S.dma(scT[:], I["cT"])
                S.act(scT[:], scT[:], AF.Silu)
                S.dma(gT[:], I["norm_gT"])
                S.dma(abT[:], I["ada_bT"])
                awb = RR([sb([128, 8, 512], F32, "aw", ps_) for _ in range(2)])
                for l in range(2):
                    for cc in range(12):
                        aw = awb.get()
                        S.dma(aw[:], I["ada_w"][l, :, cc * 512:(cc + 1) * 512].rearrange("(k p) n -> p k n", p=128))
                        for ct in range(4):
                            j = cc * 4 + ct
                            ps = PS[j % 2]
                            S.mmacc(ps[:, 0:R], [(aw[:, k, ct * 128:(ct + 1) * 128], scT[:, k, :]) for k in range(8)])
                            S.ts(modT[:, j, :], ps[:, 0:R], abT[:, l, j:j + 1], None, ALU.add)
                    for r in range(R):
                        def m_(i):
                            return modT[:, i * 8:(i + 1) * 8, r]
                        S.stt(modA[:, l, 0, r, :], m_(1), 1.0, gT[:, l, 0, :], ALU.add, ALU.mult)
                        S.copy(modA[:, l, 1, r, :], m_(0))
                        S.stt(modA[:, l, 2, r, :], m_(4), 1.0, gT[:, l, 2, :], ALU.add, ALU.mult)
                        S.copy(modA[:, l, 3, r, :], m_(3))
                        S.tt(PG[:, l, 0, r, :], m_(2), gT[:, l, 1, :], ALU.mult)
                        S.tt(PG[:, l, 1, r, :], m_(5), gT[:, l, 3, :], ALU.mult)
                        for w_ in range(2):
                            ps = PS[2 + (w_ % 2)]
                            S.transpose(ps[0:8, 0:128], PG[:, l, w_, r, :], ident[:])
                            S.copy(rowt[:], ps[0:8, 0:128])
                            S.dma(GSC[l, r, w_, :].rearrange("(k p) -> k p", p=128), rowt[:])
                dl = sb([128, 256], F32, "dl", ps_)
                S.dma(dl[:], I["dlam"].partition_broadcast(128))
                pr = sb([128, 128], F32, "pr", ps_)
                S.tt(pr[:, 0:64], dl[:, 0:64], dl[:, 64:128], ALU.mult)
                S.tt(pr[:, 64:128], dl[:, 128:192], dl[:, 192:256], ALU.mult)
                s1 = scol(); s2 = scol()
                S.act(pr[:, 0:64], pr[:, 0:64], AF.Identity, accum_out=s1)
                S.act(pr[:, 64:128], pr[:, 64:128], AF.Identity, accum_out=s2)
                e1 = scol(); e2 = scol()
                S.act(e1, s1, AF.Exp)
                S.act(e2, s2, AF.Exp)
                S.tt(e2, e2, e1, ALU.subtract)
                S.ts(nlam[:], e2, -0.2, None, ALU.add)
                S.ts(subg[:], subg[:], 0.8, None, ALU.mult)
                sk = sb([128, 8], F32, "sk", ps_)
                S.dma(sk[:], I["sink"].partition_broadcast(128))
                S.act(esink[:], sk[:], AF.Exp)
            S.barrier()

        def sin_safe(out, pre, np_, n, tmp1, tmp2):
            S.act(tmp1, pre, AF.Sin, scale=0.25)
            S.act(tmp2, pre, AF.Sin, scale=0.25, bias=halfpi[0:np_, :])
            S.tt(tmp2, tmp1, tmp2, ALU.mult)
            S.tt(tmp1, tmp1, tmp1, ALU.mult)
            S.ts(tmp1, tmp1, -8.0, 4.0, ALU.mult, ALU.add)
            S.tt(out, tmp1, tmp2, ALU.mult)

        halfpi = sb([128, 1], F32, "halfpi")
        S.memset(halfpi[:], math.pi / 2)

        def prologue_hyena():
            S.tag = "hyp"
            with phase() as ps_:
                ft = sb([33, L], F32, "ft", ps_)
                w1 = sb([33, 64], F32, "w1", ps_)
                w2 = sb([64, 64], F32, "w2", ps_)
                w3 = sb([65, 2048], F32, "w3", ps_)
                b1 = sb([64, 1], F32, "b1", ps_)
                b2 = sb([64, 1], F32, "b2", ps_)
                fr = sb([64, 1], F32, "fr", ps_)
                h1 = sb([64, L], F32, "h1", ps_)
                h2 = sb([65, L], F32, "h2", ps_)
                t1 = sb([64, 512], F32, "t1", ps_)
                t2 = sb([64, 512], F32, "t2", ps_)
                t3 = sb([64, 512], F32, "t3", ps_)
                S.dma(ft[:], I["featsT"])
                S.dma(w1[:], I["hw1"])
                S.dma(w2[:], I["hw2"])
                S.dma(w3[0:64, :], I["hw3"])
                S.dma(w3[64:65, :], I["hb3"])
                S.dma(b1[:], I["hb1"])
                S.dma(b2[:], I["hb2"])
                S.dma(fr[:], I["hfreq"])
                S.tt(b1[:], b1[:], fr[:], ALU.mult)
                S.tt(b2[:], b2[:], fr[:], ALU.mult)
                S.memset(h2[64:65, :], 1.0)
                for c in range(4):
                    cs = slice(c * 512, (c + 1) * 512)
                    ps = PS[c % 2]
                    S.mm(ps[0:64, :], w1[:, :], ft[:, cs])
                    S.ts(t3[:], ps[0:64, :], fr[:, 0:1], b1[:, 0:1], ALU.mult, ALU.add)
                    sin_safe(h1[:, cs], t3[:], 64, 512, t1[:], t2[:])
                for c in range(4):
                    cs = slice(c * 512, (c + 1) * 512)
                    ps = PS[2 + c % 2]
                    S.mm(ps[0:64, :], w2[:, :], h1[:, cs])
                    S.ts(t3[:], ps[0:64, :], fr[:, 0:1], b2[:, 0:1], ALU.mult, ALU.add)
                    sin_safe(h2[0:64, cs], t3[:], 64, 512, t1[:], t2[:])
                dec = sb([128, 16, 512], F32, "dec", ps_)
                S.dma(dec[:], I["decay"].rearrange("(a p) n -> p a n", p=128))
                Pm = [sb([128, 16, 512], BF, "Pm", ps_) for _ in range(2)]
                Mm = [sb([128, 16, 512], BF, "Mm", ps_) for _ in range(2)]
                hf = sb([128, 512], F32, "hf", ps_)
                hb = sb([128, 512], F32, "hb", ps_)
                for n in range(2):
                    for a in range(16):
                        psf = PS[4]
                        psb = PS[5]
                        S.mm(psf[:, :], h2[:, a * 128:(a + 1) * 128], w3[:, n * 512:(n + 1) * 512])
                        S.mm(psb[:, :], h2[:, a * 128:(a + 1) * 128], w3[:, 1024 + n * 512:1024 + (n + 1) * 512])
                        S.tt(hf[:], psf[:, :], dec[:, a, :], ALU.mult)
                        S.tt(hb[:], psb[:, :], dec[:, a, :], ALU.mult)
                        if a == 0:
                            S.memset(hb[0:1, :], 0.0)
                        S.tt(Pm[n][:, a, :], hf[:], hb[:], ALU.add)
                        S.tt(Mm[n][:, a, :], hf[:], hb[:], ALU.subtract)
                fmb = RR([sb([128, 16, 128], BF, "fm", ps_) for _ in range(3)])
                ob = RR([sb([128, 512], F32, "so", ps_) for _ in range(3)])
                FMv = I["FM"].rearrange("(a p) f -> p a f", p=128)
                for n in range(2):
                    for ftile in range(32):
                        fm = fmb.get()
                        S.dma(fm[:], FMv[:, :, ftile * 128:(ftile + 1) * 128])
                        src = Pm[n] if ftile < 16 else Mm[n]
                        ps = PS[6 + ftile % 2]
                        S.mmacc(ps[:, :], [(fm[:, a, :], src[:, a, :]) for a in range(16)])
                        o = ob.get()
                        S.copy(o[:], ps[:, :], eng="act")
                        if ftile < 16:
                            S.dma(SPEC[n, 0, ftile * 128:(ftile + 1) * 128, :], o[:])
                            if ftile > 0:
                                S.dma(SPEC[n, 2, ftile * 128:(ftile + 1) * 128, :], o[:])
                            else:
                                S.dma(SPEC[n, 2, 1:128, :], o[1:128, :])
                        else:
                            if ftile == 16:
                                ps2 = PS[4]
                                S.mmacc(ps2[:, :], [(fm[:, a, :], Pm[n][:, a, :]) for a in range(16)])
                                o2 = ob.get()
                                S.copy(o2[0:1, :], ps2[0:1, :], eng="act")
                                S.dma(SPEC[n, 2, 0:1, :], o2[0:1, :])
                                S.memset(o[0:1, :], 0.0)
                            S.dma(SPEC[n, 1, (ftile - 16) * 128:(ftile - 15) * 128, :], o[:])
            S.barrier()

        def xtile_src(layer, s, tt):
            if layer == 0:
                if tt < 2:
                    return I["ctx"][s, tt * 128:(tt + 1) * 128, :]
                return I["x"][s, (tt - 2) * 128:(tt - 1) * 128, :]
            return XD[s, tt * 128:(tt + 1) * 128, :]

        def norm_transpose(src_fn, tts, acol, bcol_, HT, stack, tok_base=0):
            S.tag = "norm"
            xb = RR([sb([128, D], F32, "xb", stack) for _ in range(3)])
            xnb = RR([sb([128, D], F32, "xn", stack) for _ in range(8)])
            junk = sb([128, D], BF, "junk", stack)
            psr = RR([PS[0], PS[1], PS[2], PS[3]])
            for g0 in range(0, len(tts), 4):
                grp = tts[g0:g0 + 4]
                xns = []
                for tt in grp:
                    x = xb.get()
                    S.dma(x[:], src_fn(tt))
                    ss = scol()
                    S.act(junk[:], x[:], AF.Square, accum_out=ss)
                    r = rstd_from(ss, D)
                    xn = xnb.get()
                    S.ts(xn[:], x[:], r, None, ALU.mult)
                    xns.append(xn)
                for k in range(8):
                    ps = psr.get()
                    for i, xn in enumerate(xns):
                        S.transpose(ps[:, i * 128:(i + 1) * 128], xn[:, k * 128:(k + 1) * 128], ident[:])
                    i = 0
                    while i < len(grp):
                        j = i
                        while j + 1 < len(grp) and (grp[j + 1] < 2) == (grp[i] < 2):
                            j += 1
                        t0 = grp[i] * 128 - tok_base
                        S.act(HT[:, k, t0:t0 + (j - i + 1) * 128], ps[:, i * 128:(j + 1) * 128], AF.Identity,
                              scale=acol(grp[i])[:, k:k + 1], bias=bcol_(grp[i])[:, k:k + 1])
                        i = j + 1

        def load_w(dst, wb_ap, r0, nk, c0, ncols, q="sp"):
            S.dma(dst, wb_ap[r0:r0 + nk * 128, c0:c0 + ncols].rearrange("(k p) n -> p k n", p=128), q=q)

        CHUNKS0 = [(0, 256)] + [(256 + 512 * i, 512) for i in range(4)]

        def rope_apply(dst, ps, psr, tab, prange, tok0, n, tmpa, tmpb):
            p0, p1 = prange
            S.tt(tmpa[p0:p1, 0:n], ps[p0:p1, 0:n], tab[p0:p1, 0, tok0:tok0 + n], ALU.mult)
            S.tt(tmpb[p0:p1, 0:n], psr[p0:p1, 0:n], tab[p0:p1, 1, tok0:tok0 + n], ALU.mult)
            S.tt(dst, tmpa[p0:p1, 0:n], tmpb[p0:p1, 0:n], ALU.add, eng="pool")

        def residual_update(s, layer, which, tt, psy, xsrc, gtile, dst, stack_bufs):
            xb, tb, junk = stack_bufs
            s1 = scol(); s2 = scol()
            S.act(junk[:, 0:512], psy[0][:, :], AF.Square, accum_out=s1)
            S.act(junk[:, 512:1024], psy[1][:, :], AF.Square, accum_out=s2)
            S.tt(s1, s1, s2, ALU.add)
            r = rstd_from(s1, D)
            x = xb.get()
            S.dma(x[:], xsrc)
            t = tb.get()
            for h in range(2):
                hs = slice(h * 512, (h + 1) * 512)
                S.stt(t[:, hs], psy[h][:, :], r, gtile[:, hs], ALU.mult, ALU.mult)
                S.tt(t[:, hs], t[:, hs], x[:, hs], ALU.add, eng="pool")
            S.dma(dst, t[:])

        def mixer0(s, HT, OT, stack):
            r_l, r_c = s, NS
            with phase() as st:
                CQN = sb([128, 3, NT], BF, "CQN", st)
                CKVN = sb([128, 2, NT], BF, "CKVN", st)
                KRT = sb([128, NT], BF, "KRT", st)
                ropeM = sb([128, 2, L], F32, "ropeM", st)
                S.dma(ropeM[64:96], I["ropeM"][64:96])
                tmpa = sb([128, 512], F32, "tmpa", st)
                tmpb = sb([128, 512], F32, "tmpb", st)
                S.tag = "lat"
                with phase() as st2:
                    wl = sb([128, 8, 640], BF, "wl", st2)
                    wkr = sb([128, 8, 64], BF, "wkr", st2)
                    load_w(wl[:], WB["w0in"], 0, 8, 0, 640)
                    load_w(wkr[:], WB["w0in"], 0, 8, 3200, 64)
                    cf = [sb([128, 512], F32, "cf", st2) for _ in range(5)]
                    sq = [sb([128, 512], BF, "sq", st2) for _ in range(5)]
                    rs = sb([128, 512], F32, "rs", st2)
                    for (c0, n) in CHUNKS0:
                        for (base, nt_, gcol, dst, nf) in ((0, 3, qng, CQN, 384), (3, 2, kvng, CKVN, 256)):
                            for ct in range(nt_):
                                ps = PS[ct]
                                S.mmacc(ps[:, 0:n], [(wl[:, k, (base + ct) * 128:(base + ct + 1) * 128], HT[:, k, c0:c0 + n]) for k in range(8)])
                                S.copy(cf[base + ct][:, 0:n], ps[:, 0:n], eng="act")
                                S.tt(sq[base + ct][:, 0:n], cf[base + ct][:, 0:n], cf[base + ct][:, 0:n], ALU.mult, eng="pool")
                            pss = PS[3]
                            S.mmacc(pss[:, 0:n], [(ones_bf[:, :], sq[base + ct][:, 0:n]) for ct in range(nt_)])
                            S.act(rs[:, 0:n], pss[:, 0:n], AF.Sqrt, bias=EPS, scale=1.0 / nf)
                            S.recip(rs[:, 0:n], rs[:, 0:n])
                            for ct in range(nt_):
                                S.stt(dst[:, ct, c0:c0 + n], cf[base + ct][:, 0:n], gcol[:, ct:ct + 1], rs[:, 0:n], ALU.mult, ALU.mult)
                        pk = PS[4]; pkr = PS[5]
                        S.mmacc(pk[64:96, 0:n], [(wkr[:, k, 0:32], HT[:, k, c0:c0 + n]) for k in range(8)])
                        if c0 == 0:
                            S.copy(KRT[64:96, 0:n], pk[64:96, 0:n], eng="act")
                        else:
                            S.mmacc(pkr[64:96, 0:n], [(wkr[:, k, 32:64], HT[:, k, c0:c0 + n]) for k in range(8)])
                            rope_apply(KRT[64:96, c0:c0 + n], pk, pkr, ropeM, (64, 96), c0 - C, n, tmpa, tmpb)
                S.tag = "mla"
                with phase() as st2:
                    wuq = sb([128, 3, 1024], BF, "wuq", st2)
                    wukv = sb([128, 2, 1024], BF, "wukv", st2)
                    load_w(wuq[:], WB["w_uq"], 0, 3, 0, 1024)
                    load_w(wukv[:], WB["w_ukv"], 0, 2, 0, 1024)
                    QHb = RR([sb([128, NT], BF, "QH", st2) for _ in range(2)])
                    KHb = RR([sb([128, NT], BF, "KH", st2) for _ in range(2)])
                    VHb = [sb([128, 18, 128], BF, "VH", st2) for _ in range(2)]
                    for v in VHb:
                        S.memset(v[:, :, 64:128], 1.0)
                    VHr = RR(VHb)
                    PTb = RR([sb([128, 512], BF, "PT", st2) for _ in range(6)])
                    dtmp = RR([sb([64, 512], F32, "dtmp", st2) for _ in range(2)])
                    scale = 96 ** -0.5
                    hb_ = {}

                    def mla_proj(h):
                        QH = QHb.get(); KH = KHb.get(); VH = VHr.get()
                        hb_[h] = (QH, KH, VH)
                        for (c0, n) in CHUNKS0:
                            pq = PS[0]; pqr = PS[1]; pk = PS[2]
                            S.mmacc(pq[0:96, 0:n], [(wuq[:, k, h * 96:(h + 1) * 96], CQN[:, k, c0:c0 + n]) for k in range(3)])
                            S.mmacc(pk[0:64, 0:n], [(wukv[:, k, h * 64:(h + 1) * 64], CKVN[:, k, c0:c0 + n]) for k in range(2)])
                            S.copy(KH[0:64, c0:c0 + n], pk[0:64, 0:n], eng="act")
                            if c0 == 0:
                                S.copy(QH[0:96, 0:n], pq[0:96, 0:n], eng="act")
                            else:
                                S.mmacc(pqr[64:96, 0:n], [(wuq[:, k, 768 + h * 32:768 + (h + 1) * 32], CQN[:, k, c0:c0 + n]) for k in range(3)])
                                S.copy(QH[0:64, c0:c0 + n], pq[0:64, 0:n], eng="act")
                                rope_apply(QH[64:96, c0:c0 + n], pq, pqr, ropeM, (64, 96), c0 - C, n, tmpa, tmpb)
                        S.copy(KH[64:96, :], KRT[64:96, :], eng="pool")
                        for t8 in range(0, 18, 8):
                            nn = min(8, 18 - t8)
                            psv = PS[1]
                            for i in range(nn):
                                tt = t8 + i
                                S.mmacc(psv[:, i * 64:(i + 1) * 64], [(CKVN[:, k, tt * 128:(tt + 1) * 128], wukv[:, k, 512 + h * 64:512 + (h + 1) * 64]) for k in range(2)])
                            S.copy(VH[:, t8:t8 + nn, 0:64], psv[:, 0:nn * 64].rearrange("p (a b) -> p a b", b=64))

                    def mla_attn(h):
                        QH, KH, VH = hb_.pop(h)
                        units = []
                        for ci, (c0, n) in enumerate(CHUNKS0):
                            nk = 2 if c0 == 0 else 18
                            for kt in range(nk):
                                units.append((ci, c0, n, kt, nk))
                        psS = RR([PS[3], PS[4], PS[5]])
                        psO = RR([PS[6], PS[7]])
                        cur_o = {}
                        pend = []

                        def do_pv(u, pt):
                            ci, c0, n, kt, nk = u
                            if kt == 0:
                                cur_o[ci] = psO.get()
                            po = cur_o[ci]
                            S.mm(po[:, 0:n], VH[:, kt, :], pt[:, 0:n], start=(kt == 0), stop=(kt == nk - 1))
                            if kt == nk - 1:
                                dt = dtmp.get()
                                S.copy(dt[0:64, 0:n], po[64:128, 0:n], eng="act")
                                S.recip(dt[0:64, 0:n], dt[0:64, 0:n])
                                S.tt(OT[(h % 2) * 64:(h % 2) * 64 + 64, h // 2, c0:c0 + n], po[0:64, 0:n], dt[0:64, 0:n], ALU.mult)

                        for u in units:
                            ci, c0, n, kt, nk = u
                            pss = psS.get()
                            S.mm(pss[:, 0:n], KH[0:96, kt * 128:(kt + 1) * 128], QH[0:96, c0:c0 + n])
                            pt = PTb.get()
                            S.act(pt[:, 0:n], pss[:, 0:n], AF.Exp, scale=scale)
                            pend.append((u, pt))
                            if len(pend) > 2:
                                do_pv(*pend.pop(0))
                        while pend:
                            do_pv(*pend.pop(0))
                    mla_proj(0)
                    for h in range(8):
                        if h + 1 < 8:
                            mla_proj(h + 1)
                        mla_attn(h)
            S.tag = "diff"
            with phase() as st2:
                tmpa = sb([128, 512], F32, "tmpa", st2)
                tmpb = sb([128, 512], F32, "tmpb", st2)
                ropeD = sb([128, 2, L], F32, "ropeD", st2)
                S.dma(ropeD[:], I["ropeD"])
                wdb = RR([sb([128, 8, 5, 128], BF, "wd", st2) for _ in range(2)])
                DQb = RR([sb([128, NT], BF, "DQ", st2) for _ in range(2)])
                DKb = RR([sb([128, NT], BF, "DK", st2) for _ in range(2)])
                DKzb = RR([[sb([128, NT], BF, "DKz", st2) for _ in range(2)] for _ in range(2)])
                for pair_ in DKzb.items:
                    S.memset(pair_[0][64:128, :], 0.0, eng="pool")
                    S.memset(pair_[1][0:64, :], 0.0, eng="pool")
                sqd2 = sb([128, 512], F32, "r2", st2)
                PTb6 = None
                DVb = RR([sb([128, 18, 128], BF, "DV", st2) for _ in range(2)])
                PTb = RR([sb([128, 512], BF, "PT", st2) for _ in range(6)])
                a1 = sb([128, 512], F32, "a1", st2)
                a2 = sb([128, 512], F32, "a2", st2)
                r1 = sb([128, 512], F32, "r1", st2)
                sqd = sb([128, 512], BF, "sqd", st2)
                scale = 64 ** -0.5
                dh_ = {}

                def diff_proj(h):
                    wd = wdb.get()
                    for i, cb in enumerate((640, 1152, 1664, 2176, 2688)):
                        load_w(wd[:, :, i, :], WB["w0in"], 0, 8, cb + h * 128, 128)
                    DQ = DQb.get(); DK = DKb.get(); DV = DVb.get(); DKz = DKzb.get()
                    for (c0, n) in CHUNKS0:
                        for (dst, wi) in ((DQ, 0), (DK, 2)):
                            pa = PS[0]; pb = PS[1]
                            S.mmacc(pa[:, 0:n], [(wd[:, k, wi, :], HT[:, k, c0:c0 + n]) for k in range(8)])
                            if c0 == 0:
                                S.copy(dst[:, 0:n], pa[:, 0:n], eng="act")
                            else:
                                S.mmacc(pb[:, 0:n], [(wd[:, k, wi + 1, :], HT[:, k, c0:c0 + n]) for k in range(8)])
                                rope_apply(dst[:, c0:c0 + n], pa, pb, ropeD, (0, 128), c0 - C, n, tmpa, tmpb)
                    for t4 in range(0, 18, 4):
                        ps = PS[2]
                        nn = min(4, 18 - t4)
                        for i in range(nn):
                            tt = t4 + i
                            S.mmacc(ps[:, i * 128:(i + 1) * 128], [(HT[:, k, tt * 128:(tt + 1) * 128], wd[:, k, 4, :]) for k in range(8)])
                        S.copy(DV[:, t4:t4 + nn, :], ps[:, 0:nn * 128].rearrange("p (a b) -> p a b", b=128), eng="act")
                    S.copy(DKz[0][0:64, :], DK[0:64, :], eng="pool")
                    S.copy(DKz[1][64:128, :], DK[64:128, :], eng="pool")
                    dh_[h] = (DQ, DV, DKz)

                def diff_attn(h):
                    DQ, DV, DKz = dh_.pop(h)
                    r2 = sqd2
                    for (c0, n) in CHUNKS0:
                        nk = 2 if c0 == 0 else 18
                        psS = RR([PS[3], PS[4], PS[7]])
                        po = PS[5]
                        pd = PS[6]
                        acc = [a1, a2]
                        for j in range(2):
                            pend = []

                            def do_pv(kt, pt, n=n, nk=nk):
                                S.mm(po[:, 0:n], DV[:, kt, :], pt[:, 0:n], start=(kt == 0), stop=(kt == nk - 1))
                                S.mm(pd[:, 0:n], ones_bf[:, :], pt[:, 0:n], start=(kt == 0), stop=(kt == nk - 1))

                            for kt in range(nk):
                                pss = psS.get()
                                S.mm(pss[:, 0:n], DKz[j][:, kt * 128:(kt + 1) * 128], DQ[:, c0:c0 + n])
                                pt = PTb.get()
                                S.act(pt[:, 0:n], pss[:, 0:n], AF.Exp, scale=scale)
                                pend.append((kt, pt))
                                if len(pend) > 2:
                                    do_pv(*pend.pop(0))
                            while pend:
                                do_pv(*pend.pop(0))
                            rr_ = r1 if j == 0 else r2
                            S.recip(rr_[:, 0:n], pd[:, 0:n])
                            S.tt(acc[j][:, 0:n], po[:, 0:n], rr_[:, 0:n], ALU.mult)
                        S.stt(a1[:, 0:n], a2[:, 0:n], nlam[:, 0:1], a1[:, 0:n], ALU.mult, ALU.add)
                        S.tt(sqd[:, 0:n], a1[:, 0:n], a1[:, 0:n], ALU.mult, eng="pool")
                        pss = psS.get()
                        S.mm(pss[:, 0:n], ones_bf[:, :], sqd[:, 0:n])
                        S.act(r1[:, 0:n], pss[:, 0:n], AF.Sqrt, bias=EPS, scale=1.0 / 128)
                        S.recip(r1[:, 0:n], r1[:, 0:n])
                        S.stt(OT[:, 4 + h, c0:c0 + n], a1[:, 0:n], subg[:, 0:1], r1[:, 0:n], ALU.mult, ALU.mult)

                diff_proj(0)
                for h in range(4):
                    if h + 1 < 4:
                        diff_proj(h + 1)
                    diff_attn(h)

        def wout_residual(s, layer, OT, tts, tok_base):
            S.tag = "wout"
            with phase() as st:
                wo = sb([128, 8, D], BF, "wo", st)
                load_w(wo[:], WB["w_out"], layer * D, 8, 0, D)
                G = {}
                for which, r in (("l", s), ("c", NS)):
                    if which == "c" and layer == 1:
                        continue
                    G[which] = sb([128, D], F32, "G", st)
                    S.dma(G[which][:], GSC[layer, r, 0:1, :].partition_broadcast(128))
                bufs = (RR([sb([128, D], F32, "xr", st) for _ in range(2)]), RR([sb([128, D], F32, "tr", st) for _ in range(2)]),
                        sb([128, D], BF, "junk", st))
                psr = RR([(PS[0], PS[1]), (PS[2], PS[3]), (PS[4], PS[5])])
                for tt in tts:
                    psy = psr.get()
                    t0 = tt * 128 - tok_base
                    for h in range(2):
                        S.mmacc(psy[h][:, :], [(OT[:, k, t0:t0 + 128], wo[:, k, h * 512:(h + 1) * 512]) for k in range(8)])
                    residual_update(s, layer, 0, tt, psy, xtile_src(layer, s, tt), G["c" if tt < 2 else "l"],
                                    XD[s, tt * 128:(tt + 1) * 128, :], bufs)

        def ffn(s, layer, last, chk=lambda n: None):
            tts = list(range(18)) if layer == 0 else list(range(2, 18))
            tok_base = 0 if layer == 0 else C
            ntok = NT - tok_base
            with phase() as st:
                HT = sb([128, 8, ntok], BF, "HT2", st)
                with phase() as st2:
                    norm_transpose(lambda tt: XD[s, tt * 128:(tt + 1) * 128, :], tts,
                                   lambda tt: modA[:, layer, 2, NS if tt < 2 else s, :],
                                   lambda tt: modA[:, layer, 3, NS if tt < 2 else s, :], HT, st2, tok_base)
                chk("ffn_norm")
                S.tag = "ffn"
                wdn = sb([128, 22, D], BF, "wdn", st)
                load_w(wdn[:], WB["w_down"], layer * DFF, 22, 0, D)
                G = {}
                for which, r in (("l", s), ("c", NS)):
                    if which == "c" and layer == 1:
                        continue
                    G[which] = sb([128, D], F32, "G3", st)
                    S.dma(G[which][:], GSC[layer, r, 1:2, :].partition_broadcast(128))
                GT = sb([128, 22, 1024], BF, "GT", st)
                HALO = sb([128, 44, 2], F32, "HALO", st)
                wub = RR([sb([128, 8, 2, 128], BF, "wu", st) for _ in range(3)])
                Yb = RR([sb([128, 1026], F32, "Y", st) for _ in range(4)])
                Ub = RR([sb([128, 1024], F32, "U", st) for _ in range(4)])
                bufs = (RR([sb([128, D], F32, "xr", st) for _ in range(2)]), RR([sb([128, D], F32, "tr", st) for _ in range(2)]),
                        sb([128, D], BF, "junk", st))
                scs = []
                if layer == 0:
                    scs.append((0, 256, 0))
                scs += [(256, 1024, 1), (1280, 1024, 2)]
                psU = RR([PS[0], PS[1], PS[2], PS[3]])
                psH = RR([PS[4], PS[5]])
                psD = RR([(PS[6], PS[7]), (PS[4], PS[5])])
                chk("ffn_load")
                for (t0, n, kind) in scs:
                    a0 = t0 - tok_base
                    for j in range(22):
                        wu = wub.get()
                        load_w(wu[:, :, 0, :], WB["w_up"], layer * D, 8, j * 128, 128)
                        load_w(wu[:, :, 1, :], WB["w_up"], layer * D, 8, DFF + j * 128, 128)
                        Us = []
                        for ag in range(2):
                            Y = Yb.get(); U = Ub.get()
                            jj = ag * 22 + j
                            for c in range(0, n, 512):
                                nn = min(512, n - c)
                                ps = psU.get()
                                S.mmacc(ps[:, 0:nn], [(wu[:, k, ag, :], HT[:, k, a0 + c:a0 + c + nn]) for k in range(8)])
                                S.copy(Y[:, 1 + c:1 + c + nn], ps[:, 0:nn], eng="act")
                                S.act(U[:, c:c + nn], ps[:, 0:nn], AF.Identity, scale=fcw[:, layer, jj, 1:2], bias=fcb[:, layer, jj:jj + 1])
                            if kind == 1:
                                ph = psH.get()
                                S.mmacc(ph[:, 0:2], [(wu[:, k, ag, :], HT[:, k, a0 + n - 1:a0 + n + 1]) for k in range(8)])
                                S.copy(Y[:, n + 1:n + 2], ph[:, 1:2], eng="act")
                                S.copy(HALO[:, jj, 0:1], ph[:, 0:1], eng="act")
                                S.memset(Y[:, 0:1], 0.0, eng="pool")
                            elif kind == 2:
                                S.copy(Y[:, 0:1], HALO[:, jj, 0:1], eng="pool")
                                S.memset(Y[:, n + 1:n + 2], 0.0, eng="pool")
                            else:
                                S.memset(Y[:, 0:1], 0.0, eng="pool")
                                S.memset(Y[:, n + 1:n + 2], 0.0, eng="pool")
                            S.stt(U[:, 0:n], Y[:, 0:n], fcw[:, layer, jj, 0:1], U[:, 0:n], ALU.mult, ALU.add)
                            S.stt(U[:, 0:n], Y[:, 2:n + 2], fcw[:, layer, jj, 2:3], U[:, 0:n], ALU.mult, ALU.add)
                            Us.append(U)
                        S.act(Us[1][:, 0:n], Us[1][:, 0:n], AF.Silu)
                        S.tt(GT[:, j, 0:n], Us[0][:, 0:n], Us[1][:, 0:n], ALU.mult, eng="pool")
                    chk("ffn_up")
                    for ti in range(n // 128):
                        tt = t0 // 128 + ti
                        psy = psD.get()
                        for h in range(2):
                            S.mmacc(psy[h][:, :], [(GT[:, j, ti * 128:(ti + 1) * 128], wdn[:, j, h * 512:(h + 1) * 512]) for j in range(22)])
                        if last:
                            dst = OUT[s, (tt - 2) * 128:(tt - 1) * 128, :]
                        else:
                            dst = XD[s, tt * 128:(tt + 1) * 128, :]
                        residual_update(s, layer, 1, tt, psy, XD[s, tt * 128:(tt + 1) * 128, :], G["c" if tt < 2 else "l"], dst, bufs)
                        chk("ffn_res")

        def mixer1(s, HT, OT, Z):
            S.tag = "m1hyproj"
            with phase() as st2:
                wub = RR([sb([128, 8, 128], BF, "wu1", st2) for _ in range(3)])
                Yb = RR([sb([128, L + 2], F32, "Yh", st2) for _ in range(2)])
                Ub = RR([sb([128, L], F32, "Uh", st2) for _ in range(2)])
                for j in range(12):
                    wu = wub.get()
                    load_w(wu[:], WB["w1in"], 0, 8, 1408 + j * 128, 128)
                    Y = Yb.get()
                    S.memset(Y[:, 0:1], 0.0)
                    S.memset(Y[:, L + 1:L + 2], 0.0)
                    dst = Z[:, j, :] if j < 4 else Ub.get()[:, :]
                    for c in range(4):
                        ps = PS[5 + c % 2]
                        tk = C + c * 512
                        S.mmacc(ps[:, :], [(wu[:, k, :], HT[:, k, tk:tk + 512]) for k in range(8)])
                        S.copy(Y[:, 1 + c * 512:1 + (c + 1) * 512], ps[:, :], eng="act")
                        S.ts(dst[:, c * 512:(c + 1) * 512], ps[:, :], hcw[:, j, 1:2], hcb[:, j:j + 1], ALU.mult, ALU.add)
                    S.stt(dst, Y[:, 0:L], hcw[:, j, 0:1], dst, ALU.mult, ALU.add)
                    S.stt(dst, Y[:, 2:L + 2], hcw[:, j, 2:3], dst, ALU.mult, ALU.add)
                    if j >= 4:
                        jj = j - 4
                        S.dma(X12[jj // 4, (jj % 4) * 128:(jj % 4 + 1) * 128, :], dst)
            S.tag = "m1winproj"
            with phase() as st1:
                tmpa = sb([128, 512], F32, "tmpa", st1)
                tmpb = sb([128, 512], F32, "tmpb", st1)
                QW = sb([64, 8, L], BF, "QW", st1)
                KW = sb([64, 2, NT], BF, "KW", st1)
                VW = sb([128, 18, 2, 128], BF, "VW", st1)
                S.memset(VW[:, :, :, 64:128], 1.0)
                with phase() as st2:
                    wqb = RR([sb([128, 8, 2, 64], BF, "wq", st2) for _ in range(3)])
                    wk = sb([128, 8, 384], BF, "wk", st2)
                    rtb = RR([sb([128, 2, 512], F32, "rt", st2) for _ in range(1)])
                    load_w(wk[:], WB["w1in"], 0, 8, 1024, 384)
                    for c in range(4):
                        rt = rtb.get()
                        S.dma(rt[0:64], I["ropeD"][0:64, :, c * 512:(c + 1) * 512])
                        tk = C + c * 512
                        for hd in range(8):
                            pa = PS[0]; pb = PS[1]
                            wq = wqb.get()
                            load_w(wq[:, :, 0, :], WB["w1in"], 0, 8, hd * 64, 64)
                            load_w(wq[:, :, 1, :], WB["w1in"], 0, 8, 512 + hd * 64, 64)
                            S.mmacc(pa[0:64, :], [(wq[:, k, 0, :], HT[:, k, tk:tk + 512]) for k in range(8)])
                            S.mmacc(pb[0:64, :], [(wq[:, k, 1, :], HT[:, k, tk:tk + 512]) for k in range(8)])
                            rope_apply(QW[0:64, hd, c * 512:(c + 1) * 512], pa, pb, rt, (0, 64), 0, 512, tmpa, tmpb)
                        for kh in range(2):
                            pa = PS[2]; pb = PS[3]
                            S.mmacc(pa[0:64, :], [(wk[:, k, kh * 64:(kh + 1) * 64], HT[:, k, tk:tk + 512]) for k in range(8)])
                            S.mmacc(pb[0:64, :], [(wk[:, k, 128 + kh * 64:128 + (kh + 1) * 64], HT[:, k, tk:tk + 512]) for k in range(8)])
                            rope_apply(KW[0:64, kh, tk:tk + 512], pa, pb, rt, (0, 64), 0, 512, tmpa, tmpb)
                    for kh in range(2):
                        pa = PS[2]
                        S.mmacc(pa[0:64, 0:C], [(wk[:, k, kh * 64:(kh + 1) * 64], HT[:, k, 0:C]) for k in range(8)])
                        S.copy(KW[0:64, kh, 0:C], pa[0:64, 0:C], eng="act")
                    for t4 in range(0, 18, 4):
                        ps = PS[4]
                        nn = min(4, 18 - t4)
                        for i in range(nn):
                            tt = t4 + i
                            S.mmacc(ps[:, i * 128:(i + 1) * 128], [(HT[:, k, tt * 128:(tt + 1) * 128], wk[:, k, 256:384]) for k in range(8)])
                        for kh in range(2):
                            S.copy(VW[:, t4:t4 + nn, kh, 0:64],
                                   ps[:, 0:nn * 128].rearrange("p (a b) -> p a b", b=128)[:, :, kh * 64:(kh + 1) * 64], eng="act")
                S.tag = "m1win"
                with phase() as st2:
                    wm = sb([128, 2, 512], BF, "wm", st2)
                    S.dma(wm[:], I["wmask"])
                    PTb = RR([sb([128, 512], BF, "PTw", st2) for _ in range(6)])
                    dtmp = RR([sb([64, 512], F32, "dtw", st2) for _ in range(2)])
                    psS = RR([PS[0], PS[1], PS[2], PS[3]])
                    psO = RR([PS[4], PS[5]])
                    scale = 64 ** -0.5
                    for kh in range(2):
                        for nb in range(16):
                            kts = [(0, None), (1, None)]
                            if nb > 0:
                                kts.append((2 + nb - 1, 0))
                            kts.append((2 + nb, None))
                            if nb < 15:
                                kts.append((2 + nb + 1, 1))
                            po = psO.get()
                            pts = []
                            for (kt, mk) in kts:
                                pss = psS.get()
                                S.mm(pss[:, :], KW[0:64, kh, kt * 128:(kt + 1) * 128], QW[0:64, kh * 4:(kh + 1) * 4, nb * 128:(nb + 1) * 128])
                                pt = PTb.get()
                                S.act(pt[:, :], pss[:, :], AF.Exp, scale=scale)
                                if mk is not None:
                                    S.tt(pt[:, :], pt[:, :], wm[:, mk, :], ALU.mult)
                                pts.append((kt, pt))
                            for i, (kt, pt) in enumerate(pts):
                                S.mm(po[:, :], VW[:, kt, kh, :], pt[:, :], start=(i == 0), stop=(i == len(pts) - 1))
                            dt = dtmp.get()
                            for g in range(4):
                                S.ts(dt[0:64, g * 128:(g + 1) * 128], po[64:128, g * 128:(g + 1) * 128], esink[64:128, kh * 4 + g:kh * 4 + g + 1], None, ALU.add)
                            S.recip(dt[0:64, :], dt[0:64, :])
                            for g in range(4):
                                hd = kh * 4 + g
                                S.tt(OT[(hd % 2) * 64:(hd % 2) * 64 + 64, hd // 2, nb * 128:(nb + 1) * 128], po[0:64, g * 128:(g + 1) * 128],
                                     dt[0:64, g * 128:(g + 1) * 128], ALU.mult)

        def hyena(s, Z, OT, st):
            S.tag = "hyena"
            ZT = sb([128, 16, 512], BF, "ZT", st)
            Yf = sb([128, 32, 512], BF, "Yf", st)
            fmb = RR([sb([128, 16, 128], BF, "fmh", st) for _ in range(4)])
            imb = RR([sb([128, 8, 512], BF, "imh", st) for _ in range(2)])
            spb = RR([sb([128, 3, 512], F32, "sp", st) for _ in range(2)])
            xb = RR([sb([128, 512], F32, "x12", st) for _ in range(2)])
            tA = sb([128, 512], F32, "tA", st)
            tB = sb([128, 512], F32, "tB", st)
            tC = sb([128, 512], F32, "tC", st)
            zb = sb([128, 128], BF, "zb", st)
            FMv = I["FM"].rearrange("(a p) f -> p a f", p=128)
            IMv = I["IM"].rearrange("(a p) t -> p a t", p=128)
            for n in range(2):
                for a in range(16):
                    ps = PS[a % 2]
                    for ct in range(4):
                        S.transpose(ps[:, ct * 128:(ct + 1) * 128], Z[:, ct, a * 128:(a + 1) * 128], ident[:])
                    S.copy(ZT[:, a, :], ps[:, :], eng="act")
                for i in range(16):
                    fr_ = fmb.get(); fi_ = fmb.get()
                    S.dma(fr_[:], FMv[:, :, i * 128:(i + 1) * 128])
                    S.dma(fi_[:], FMv[:, :, L + i * 128:L + (i + 1) * 128])
                    sp = spb.get()
                    S.dma(sp[:], SPEC[n, :, i * 128:(i + 1) * 128, :].rearrange("w p c -> p w c"))
                    pr = PS[2 + 2 * (i % 2)]
                    pi = PS[3 + 2 * (i % 2)]
                    S.mmacc(pr[:, :], [(fr_[:, a, :], ZT[:, a, :]) for a in range(16)])
                    S.mmacc(pi[:, :], [(fi_[:, a, :], ZT[:, a, :]) for a in range(16)])
                    S.tt(tA[:], pr[:, :], sp[:, 0, :], ALU.mult)
                    S.tt(tB[:], pi[:, :], sp[:, 1, :], ALU.mult)
                    S.tt(Yf[:, i, :], tA[:], tB[:], ALU.subtract, eng="pool")
                    S.tt(tC[:], pr[:, :], sp[:, 1, :], ALU.mult)
                    S.tt(tB[:], pi[:, :], sp[:, 2, :], ALU.mult)
                    S.tt(Yf[:, 16 + i, :], tC[:], tB[:], ALU.add, eng="pool")
                for c in range(4):
                    pss = [PS[4 + ct] for ct in range(4)]
                    for f8 in range(4):
                        im = imb.get()
                        S.dma(im[:], IMv[:, f8 * 8:(f8 + 1) * 8, c * 512:(c + 1) * 512])
                        for ct in range(4):
                            for a in range(8):
                                fa = f8 * 8 + a
                                S.mm(pss[ct][:, :], Yf[:, fa, ct * 128:(ct + 1) * 128], im[:, a, :], start=(fa == 0), stop=(fa == 31))
                    for ct in range(4):
                        x = xb.get()
                        S.dma(x[:], X12[n, ct * 128:(ct + 1) * 128, c * 512:(c + 1) * 512])
                        zs = Z[:, ct, c * 512:(c + 1) * 512]
                        S.stt(tA[:], zs, hbias[:, n, ct:ct + 1], pss[ct][:, :], ALU.mult, ALU.add)
                        if n == 0:
                            S.tt(zs, tA[:], x[:], ALU.mult)
                        else:
                            S.tt(OT[:, 4 + ct, c * 512:(c + 1) * 512], tA[:], x[:], ALU.mult)

        def main_body(chk):
            for s in range(NS):
                with phase() as st:
                    HT = sb([128, 8, NT], BF, "HT", st)
                    OT = sb([128, 8, NT], BF, "OT", st)
                    with phase() as st2:
                        norm_transpose(lambda tt: xtile_src(0, s, tt), list(range(18)),
                                       lambda tt: modA[:, 0, 0, NS if tt < 2 else s, :],
                                       lambda tt: modA[:, 0, 1, NS if tt < 2 else s, :], HT, st2)
                    chk("norm0")
                    mixer0(s, HT, OT, st)
                    chk("mixer0")
                    if dbg and "ot0" in DBG and s == 0:
                        with phase() as std:
                            otf = sb([128, 8, NT], F32, "otf", std)
                            S.copy(otf[:], OT[:])
                            S.dma(DBG["ot0"].rearrange("(k p) t -> p k t", p=128), otf[:])
                    wout_residual(s, 0, OT, list(range(18)), 0)
                    chk("wout0")
                ffn(s, 0, False, chk)
                chk("ffn0")
                S.barrier()
                if dbg and "xd0" in DBG and s == 0:
                    for tt_ in range(18):
                        S.dma(DBG["xd0"][tt_ * 128:(tt_ + 1) * 128, :], XD[0, tt_ * 128:(tt_ + 1) * 128, :])
                    S.barrier()
                with phase() as st:
                    OT = sb([128, 8, L], BF, "OT1", st)
                    Z = sb([128, 4, L], F32, "Z", st)
                    with phase() as stH:
                        HT = sb([128, 8, NT], BF, "HT1", stH)
                        with phase() as st2:
                            norm_transpose(lambda tt: xtile_src(1, s, tt), list(range(18)),
                                           lambda tt: modA[:, 1, 0, NS if tt < 2 else s, :],
                                           lambda tt: modA[:, 1, 1, NS if tt < 2 else s, :], HT, st2)
                        chk("norm1")
                        mixer1(s, HT, OT, Z)
                        chk("mixer1")
                    with phase() as stY:
                        hyena(s, Z, OT, stY)
                        chk("hyena")
                    if dbg and "ot1" in DBG and s == 0:
                        otf = sb([128, 8, L], F32, "otf", st)
                        S.copy(otf[:], OT[:])
                        S.dma(DBG["ot1"].rearrange("(k p) t -> p k t", p=128), otf[:])
                        S.barrier()
                    wout_residual(s, 1, OT, list(range(2, 18)), C)
                    S.barrier()
                ffn(s, 1, True)
                S.barrier()

        class _Stop(Exception):
            pass

        def chk(name):
            if stop == name:
                raise _Stop()

        try:
            prologue_cast()
            chk("cast")
            prologue_mod()
            chk("mod")
            prologue_hyena()
            chk("hyp")
            main_body(chk)
        except _Stop:
            pass
        S.finish()
    return nc


def kernel(**inputs):
    NCORE = 8
    NS = 4
    nc = build_program(NS)
    in_maps = [_host_layout(inputs, NS, c) for c in range(NCORE)]
    res = run_bass_kernel_spmd(nc, in_maps, core_ids=list(range(NCORE)))
    out = np.concatenate([np.asarray(r["y"], np.float32) for r in res.results], axis=0)
    return out
```

```python
import contextlib
import math
import numpy as np
import ml_dtypes
import concourse.bass as bass
import concourse.mybir as mybir
from concourse.bass_utils import run_bass_kernel_spmd

F32 = mybir.dt.float32
BF = mybir.dt.bfloat16
AF = mybir.ActivationFunctionType
ALU = mybir.AluOpType
ENGS = ("pe", "act", "dve", "pool", "sp")

D = 1024
L = 2048
C = 256
NT = L + C
DFF = 2816
EPS = 1e-6


def _box(ap):
    t = ap.tensor
    dims = list(ap.ap)
    off = int(ap.offset)
    if t.__class__.__name__.startswith("DRam"):
        hi = off + sum((int(c) - 1) * abs(int(s)) for s, c in dims) + 1
        return (t.name, 0, 1, off, hi)
    row = 1
    for s in list(t.shape)[1:]:
        row *= int(s)
    p0 = off // row
    f0 = off % row
    pstep, pcnt = int(dims[0][0]), int(dims[0][1])
    npart = pcnt if (pstep == row or pcnt == 1) else 128 - p0
    ext = sum((int(c) - 1) * abs(int(s)) for s, c in dims[1:]) + 1
    if t.__class__.__name__.startswith("PSum"):
        return (t.name, 0, 128, 0, row)
    return (t.name, p0, p0 + npart, f0, f0 + ext)


def _overlap(a, b):
    return a[1] < b[2] and b[1] < a[2] and a[3] < b[4] and b[3] < a[4]


def _covers(a, b):
    return a[1] <= b[1] and a[2] >= b[2] and a[3] <= b[3] and a[4] >= b[4]


class Op:
    __slots__ = ("eng", "fn", "deps", "sig", "signaled", "is_dma", "slot", "slotn", "clock", "mm", "tag")

    def __init__(self, eng, fn, is_dma=False, mm=False):
        self.eng = eng
        self.fn = fn
        self.deps = []
        self.sig = 0
        self.signaled = False
        self.is_dma = is_dma
        self.slot = -1
        self.slotn = 0
        self.clock = None
        self.mm = mm


class Sched:
    def __init__(self, nc, n_dma_slots=32):
        self.nc = nc
        self.ops = []
        self.recs = {}
        self.nslots = n_dma_slots
        self.eobj = {"pe": nc.tensor, "act": nc.scalar, "dve": nc.vector, "pool": nc.gpsimd, "sp": nc.sync}
        self.last = {e: None for e in ENGS}
        self.dmas_since = []
        self.bar = {e: None for e in ENGS}
        self.stack = contextlib.ExitStack()
        self.esem = {e: self.stack.enter_context(nc.semaphore("s_" + e)) for e in ENGS}
        self.dsem = [self.stack.enter_context(nc.semaphore("d_%d" % i)) for i in range(n_dma_slots)]
        self.cnt = {e: 0 for e in ENGS}
        self.rr = 0
        self.slot_cnt = [0] * n_dma_slots
        self.slot_last = [None] * n_dma_slots
        self.known = {e: ({x: 0 for x in ENGS}, [0] * n_dma_slots) for e in ENGS}
        self.n_emitted = 0
        self.tag = ""
        self.names = None

    def add(self, eng, fn, reads=(), writes=(), is_dma=False, mm=False):
        op = Op(eng, fn, is_dma, mm)
        op.tag = self.tag
        deps = {}
        rb = [_box(a) for a in reads]
        wb = [_box(a) for a in writes]
        for b in rb:
            lst = self.recs.get(b[0])
            if lst:
                psum = b[0].startswith("ps")
                for r in lst:
                    if _overlap(r[0], b) and (r[2] or (psum and r[1].eng != eng)):
                        deps[id(r[1])] = r[1]
        for b in wb:
            lst = self.recs.get(b[0])
            if lst:
                for r in lst:
                    if _overlap(r[0], b):
                        if mm and r[2] and r[1].mm:
                            continue
                        deps[id(r[1])] = r[1]
        if self.bar[eng] is not None:
            for d in self.bar[eng]:
                deps[id(d)] = d
            self.bar[eng] = None
        op.deps = list(deps.values())
        for b in wb:
            lst = self.recs.setdefault(b[0], [])
            lst[:] = [r for r in lst if not _covers(b, r[0])]
            lst.append([b, op, True])
        for b in rb:
            lst = self.recs.setdefault(b[0], [])
            if not is_dma:
                lst[:] = [r for r in lst if not ((not r[2]) and r[0] == b and r[1].eng == eng and not r[1].is_dma)]
            lst.append([b, op, False])
        self.ops.append(op)
        if is_dma:
            self.dmas_since.append(op)
        else:
            self.last[eng] = op
        return op

    def barrier(self):
        for o in self.last.values():
            if o is not None:
                o.signaled = True
        self.flush()
        b = [o for o in self.last.values() if o is not None] + list(self.dmas_since)
        self.dmas_since = []
        for e in ENGS:
            self.bar[e] = list(b) + (self.bar[e] or [])
        self.recs = {}

    def mm(self, out, lhsT, rhs, start=True, stop=True):
        nc = self.nc
        return self.add("pe", lambda: nc.tensor.matmul(out, lhsT, rhs, start=start, stop=stop),
                        reads=[lhsT, rhs], writes=[out], mm=True)

    def mmacc(self, out, pairs):
        n = len(pairs)
        for i, (l, r) in enumerate(pairs):
            self.mm(out, l, r, start=(i == 0), stop=(i == n - 1))

    def transpose(self, out, in_, ident):
        nc = self.nc
        return self.add("pe", lambda: nc.tensor.transpose(out, in_, ident), reads=[in_, ident], writes=[out], mm=True)

    def act(self, out, in_, func, bias=None, scale=None, accum_out=None):
        nc = self.nc
        kw = {}
        rd = [in_]
        wr = [out]
        if bias is not None:
            kw["bias"] = bias
            if not isinstance(bias, (int, float)):
                rd.append(bias)
        if scale is not None:
            kw["scale"] = scale
            if not isinstance(scale, (int, float)):
                rd.append(scale)
        if accum_out is not None:
            kw["accum_out"] = accum_out
            wr.append(accum_out)
        return self.add("act", lambda: nc.scalar.activation(out, in_, func, **kw), reads=rd, writes=wr)

    def tt(self, out, in0, in1, op, eng="dve"):
        e = self.eobj[eng]
        return self.add(eng, lambda: e.tensor_tensor(out, in0, in1, op), reads=[in0, in1], writes=[out])

    def ts(self, out, in0, s1, s2, op0, op1=None, eng="dve"):
        e = self.eobj[eng]
        rd = [in0] + [s for s in (s1, s2) if s is not None and not isinstance(s, (int, float))]
        if op1 is None:
            return self.add(eng, lambda: e.tensor_scalar(out, in0, s1, None, op0), reads=rd, writes=[out])
        return self.add(eng, lambda: e.tensor_scalar(out, in0, s1, s2, op0, op1), reads=rd, writes=[out])

    def stt(self, out, in0, scalar, in1, op0, op1, eng="dve"):
        e = self.eobj[eng]
        rd = [in0, in1] + ([] if isinstance(scalar, (int, float)) else [scalar])
        return self.add(eng, lambda: e.scalar_tensor_tensor(out, in0, scalar, in1, op0, op1), reads=rd, writes=[out])

    def copy(self, out, in_, eng="dve"):
        e = self.eobj[eng]
        if eng == "act":
            return self.add(eng, lambda: e.copy(out, in_), reads=[in_], writes=[out])
        return self.add(eng, lambda: e.tensor_copy(out, in_), reads=[in_], writes=[out])

    def memset(self, ap, val, eng="dve"):
        e = self.eobj[eng]
        return self.add(eng, lambda: e.memset(ap, val), reads=[], writes=[ap])

    def recip(self, out, in_):
        nc = self.nc
        return self.add("dve", lambda: nc.vector.reciprocal(out, in_), reads=[in_], writes=[out])

    def dma(self, out, in_, q="sp", **kw):
        e = self.eobj[q]
        return self.add(q, lambda: e.dma_start(out, in_, **kw), reads=[in_], writes=[out], is_dma=True)

    def flush(self):
        ops = self.ops
        self.ops = []
        ns = self.nslots
        for op in ops:
            for d in op.deps:
                d.signaled = True
        for op in ops:
            if op.is_dma:
                op.slot = self.rr
                self.rr = (self.rr + 1) % ns
                self.slot_cnt[op.slot] += 1
                op.slotn = self.slot_cnt[op.slot]
            elif op.signaled:
                self.cnt[op.eng] += 1
                op.sig = self.cnt[op.eng]
        esem, dsem, slot_last = self.esem, self.dsem, self.slot_last
        for op in ops:
            e = self.eobj[op.eng]
            ke, kd = self.known[op.eng]
            need = op.deps
            if op.is_dma and slot_last[op.slot] is not None:
                need = need + [slot_last[op.slot]]
            for d in need:
                if d.is_dma:
                    if kd[d.slot] >= d.slotn:
                        continue
                    e.wait_ge(dsem[d.slot], 16 * d.slotn)
                else:
                    if ke[d.eng] >= d.sig:
                        continue
                    e.wait_ge(esem[d.eng], d.sig)
                ce, cd = d.clock
                for x in ENGS:
                    if ce[x] > ke[x]:
                        ke[x] = ce[x]
                for i in range(ns):
                    if cd[i] > kd[i]:
                        kd[i] = cd[i]
            inst = op.fn()
            if self.names is not None:
                self.names[inst.ins.name] = op.tag
            if op.is_dma:
                inst.then_inc(dsem[op.slot], 16)
                slot_last[op.slot] = op
                cd2 = list(kd)
                cd2[op.slot] = max(cd2[op.slot], op.slotn)
                op.clock = (dict(ke), cd2)
            elif op.signaled:
                inst.then_inc(esem[op.eng], 1)
                ce2 = dict(ke)
                ce2[op.eng] = max(ce2[op.eng], op.sig)
                op.clock = (ce2, list(kd))
            op.fn = None
            op.deps = None
        self.n_emitted += len(ops)

    def finish(self):
        self.barrier()
        for i in range(self.nslots):
            if self.slot_cnt[i]:
                self.nc.sync.wait_ge(self.dsem[i], 16 * self.slot_cnt[i])
        self.stack.close()


class RR:
    def __init__(self, items):
        self.items = items
        self.i = 0

    def get(self):
        x = self.items[self.i % len(self.items)]
        self.i += 1
        return x


def _rope_tables(rot_dim):
    rows = L // 64
    row = np.repeat(np.arange(rows), 64).astype(np.float32)
    col = np.tile(np.arange(64), rows).astype(np.float32)
    quarter = rot_dim // 4
    inv = (np.float32(10000.0) ** (-np.arange(quarter, dtype=np.float32) / quarter)).astype(np.float32)
    ang = np.concatenate([row[:, None] * inv, col[:, None] * inv], axis=-1).astype(np.float32)
    cos = np.cos(ang).astype(np.float32).T
    sin = np.sin(ang).astype(np.float32).T
    cos2 = np.concatenate([cos, cos], 0)
    sin2 = np.concatenate([-sin, sin], 0)
    return cos2, sin2


_CONST = {}


def _consts():
    if _CONST:
        return _CONST
    cm, sm = _rope_tables(32)
    ropeM = np.zeros((128, 2, L), np.float32)
    ropeM[64:96, 0] = cm
    ropeM[64:96, 1] = sm
    cd, sd = _rope_tables(64)
    ropeD = np.zeros((128, 2, L), np.float32)
    ropeD[0:64, 0] = cd
    ropeD[64:128, 0] = cd
    ropeD[0:64, 1] = sd
    ropeD[64:128, 1] = sd
    k = np.arange(128)[:, None]
    q = np.arange(128)[None, :]
    m_prev = (q <= k).astype(np.float32)
    m_next = (k <= q).astype(np.float32)
    wmask = np.stack([np.tile(m_prev, (1, 4)), np.tile(m_next, (1, 4))], 1).astype(ml_dtypes.bfloat16)
    t = np.arange(L, dtype=np.int64)
    f = np.arange(L, dtype=np.int64)
    m = (t[:, None] * f[None, :]) % (2 * L)
    th = 2.0 * np.pi * m.astype(np.float64) / (2 * L)
    Cm = np.cos(th)
    Sm = np.sin(th)
    alt = np.where(t % 2 == 0, 1.0, -1.0)
    FM = np.concatenate([Cm, -Sm], axis=1)
    FM[:, L] = alt
    N = 2 * L
    IMr = 2.0 * Cm.T / N
    IMr[0, :] = 1.0 / N
    IMi = -2.0 * Sm.T / N
    IMi[0, :] = alt / N
    IM = np.concatenate([IMr, IMi], axis=0)
    tf = np.arange(L, dtype=np.float32)
    tn = tf / np.float32(L - 1)
    bands = np.linspace(1e-4, 15, 16, dtype=np.float32)
    ang = (np.float32(2.0 * math.pi) * bands[None, :] * tf[:, None] / np.float32(L)).astype(np.float32)
    feats = np.concatenate([tn[:, None], np.cos(ang), -np.sin(ang)], axis=-1).astype(np.float32)
    min_decay = math.log(1e-2) / 1.5
    max_decay = math.log(1e-2) / 0.3
    deltas = np.abs(np.linspace(min_decay, max_decay, 512, dtype=np.float32))
    decay = np.exp(-tn[:, None] * deltas[None, :]).astype(np.float32)
    ident = np.eye(128, dtype=np.float32)
    _CONST.update(dict(ropeM=ropeM, ropeD=ropeD, wmask=wmask, FM=FM.astype(ml_dtypes.bfloat16),
                       IM=IM.astype(ml_dtypes.bfloat16), featsT=np.ascontiguousarray(feats.T), decay=decay,
                       ident=ident))
    return _CONST


def _swap_halves(w, block):
    k, n = w.shape
    return np.ascontiguousarray(w.reshape(k, n // block, 2, block // 2)[:, :, ::-1, :].reshape(k, n))


def _colsT(v, ntile):
    return np.ascontiguousarray(np.asarray(v, np.float32).reshape(ntile, 128).T)


def _host_layout(inp, NS, core):
    f = lambda a: np.ascontiguousarray(np.asarray(a, np.float32))
    cst = _consts()
    b0 = core * NS
    m = {}
    m["x"] = f(inp["x"][b0:b0 + NS])
    m["ctx"] = f(inp["ctx"][b0:b0 + NS])
    rows = np.concatenate([f(inp["c"][b0:b0 + NS]), f(inp["c_ctx"])[None, :]], 0)
    m["cT"] = np.ascontiguousarray(rows.reshape(NS + 1, 8, 128).transpose(2, 1, 0))
    m["ada_w"] = f(inp["ada_w"])
    m["ada_bT"] = np.ascontiguousarray(f(inp["ada_b"]).reshape(2, 48, 128).transpose(2, 0, 1))
    m["norm_gT"] = np.ascontiguousarray(f(inp["norm_g"]).reshape(2, 4, 8, 128).transpose(3, 0, 1, 2))
    m["w_out"] = f(inp["mix_w_out"])
    m["w_up"] = f(inp["ffn_w_up"])
    m["w_down"] = f(inp["ffn_w_down"])
    m["fcw"] = np.ascontiguousarray(f(inp["ffn_conv_w"]).reshape(2, 3, 44, 128).transpose(3, 0, 2, 1))
    m["fcb"] = np.ascontiguousarray(f(inp["ffn_conv_b"]).reshape(2, 44, 128).transpose(2, 0, 1))
    we = f(inp["even_w_in"][0])
    cq, ckv, kr, dq, dk, dv = np.split(we, np.cumsum([384, 256, 32, 512, 512])[:], axis=1)
    m["w0in"] = np.ascontiguousarray(np.concatenate(
        [cq, ckv, dq, _swap_halves(dq, 64), dk, _swap_halves(dk, 64), dv, kr, _swap_halves(kr, 32)], 1))
    uq = f(inp["mla_w_uq"][0]).reshape(384, 8, 96)
    uq_rot = _swap_halves(np.ascontiguousarray(uq[:, :, 64:]).reshape(384, 256), 32)
    m["w_uq"] = np.ascontiguousarray(np.concatenate([uq.reshape(384, 768), uq_rot], 1))
    ukv = f(inp["mla_w_ukv"][0]).reshape(256, 8, 128)
    m["w_ukv"] = np.ascontiguousarray(np.concatenate([ukv[:, :, :64].reshape(256, 512), ukv[:, :, 64:].reshape(256, 512)], 1))
    m["qngT"] = _colsT(inp["mla_q_norm_g"][0], 3)
    m["kvngT"] = _colsT(inp["mla_kv_norm_g"][0], 2)
    m["subgT"] = _colsT(inp["diff_subln_g"][0], 1)
    m["dlam"] = f(inp["diff_lambda"][0]).reshape(1, 256)
    wo = f(inp["odd_w_in"][0])
    q, k_, v_, u = np.split(wo, np.cumsum([512, 128, 128]), axis=1)
    m["w1in"] = np.ascontiguousarray(np.concatenate([q, _swap_halves(q, 64), k_, _swap_halves(k_, 64), v_, u], 1))
    m["sink"] = f(inp["win_sink"][0]).reshape(1, 8)
    m["hcw"] = np.ascontiguousarray(f(inp["hy_conv_w"][0]).reshape(3, 12, 128).transpose(2, 1, 0))
    m["hcb"] = _colsT(inp["hy_conv_b"][0], 12)
    m["hbias"] = np.ascontiguousarray(f(inp["hy_bias"][0]).reshape(2, 4, 128).transpose(2, 0, 1))
    m["hw1"] = f(inp["hy_f_w1"][0])
    m["hb1"] = f(inp["hy_f_b1"][0]).reshape(64, 1)
    m["hw2"] = f(inp["hy_f_w2"][0])
    m["hb2"] = f(inp["hy_f_b2"][0]).reshape(64, 1)
    m["hw3"] = f(inp["hy_f_w3"][0])
    m["hb3"] = f(inp["hy_f_b3"][0]).reshape(1, 2048)
    m["hfreq"] = f(inp["hy_f_freq"][0]).reshape(64, 1)
    for kk in ("ropeM", "ropeD", "wmask", "FM", "IM", "featsT", "decay", "ident"):
        m[kk] = cst[kk]
    return m


def build_program(NS, dbg=None, stop=None, names=None):
    R = NS + 1
    nc = bass.Bass("TRN2", target_bir_lowering=False)
    S = Sched(nc)
    S.names = names
    uid = [0]

    def din(name, shape, dt=F32):
        return nc.dram_tensor(name, list(shape), dt, kind="ExternalInput").ap()

    def dscr(name, shape, dt):
        return nc.dram_tensor(name, list(shape), dt, kind="Internal").ap()

    I = {}
    I["x"] = din("x", [NS, L, D])
    I["ctx"] = din("ctx", [NS, C, D])
    I["cT"] = din("cT", [128, 8, R])
    I["ada_w"] = din("ada_w", [2, D, 6 * D])
    I["ada_bT"] = din("ada_bT", [128, 2, 48])
    I["norm_gT"] = din("norm_gT", [128, 2, 4, 8])
    I["w_out"] = din("w_out", [2, D, D])
    I["w_up"] = din("w_up", [2, D, 2 * DFF])
    I["w_down"] = din("w_down", [2, DFF, D])
    I["fcw"] = din("fcw", [128, 2, 44, 3])
    I["fcb"] = din("fcb", [128, 2, 44])
    I["w0in"] = din("w0in", [D, 3264])
    I["w_uq"] = din("w_uq", [384, 1024])
    I["w_ukv"] = din("w_ukv", [256, 1024])
    I["qngT"] = din("qngT", [128, 3])
    I["kvngT"] = din("kvngT", [128, 2])
    I["subgT"] = din("subgT", [128, 1])
    I["dlam"] = din("dlam", [1, 256])
    I["w1in"] = din("w1in", [D, 2944])
    I["sink"] = din("sink", [1, 8])
    I["hcw"] = din("hcw", [128, 12, 3])
    I["hcb"] = din("hcb", [128, 12])
    I["hbias"] = din("hbias", [128, 2, 4])
    I["hw1"] = din("hw1", [33, 64])
    I["hb1"] = din("hb1", [64, 1])
    I["hw2"] = din("hw2", [64, 64])
    I["hb2"] = din("hb2", [64, 1])
    I["hw3"] = din("hw3", [64, 2048])
    I["hb3"] = din("hb3", [1, 2048])
    I["hfreq"] = din("hfreq", [64, 1])
    I["ropeM"] = din("ropeM", [128, 2, L])
    I["ropeD"] = din("ropeD", [128, 2, L])
    I["wmask"] = din("wmask", [128, 2, 512], BF)
    I["FM"] = din("FM", [L, 2 * L], BF)
    I["IM"] = din("IM", [2 * L, L], BF)
    I["featsT"] = din("featsT", [33, L])
    I["decay"] = din("decay", [L, 512])
    I["ident"] = din("ident", [128, 128])
    OUT = nc.dram_tensor("y", [NS, L, D], F32, kind="ExternalOutput").ap()
    DBG = {}
    if dbg:
        for name, shape in dbg.items():
            DBG[name] = nc.dram_tensor("dbg_" + name, list(shape), F32, kind="ExternalOutput").ap()

    XD = dscr("XD", [NS, NT, D], F32)
    GSC = dscr("GSC", [2, R, 2, D], F32)
    SPEC = dscr("SPEC", [2, 3, L, 512], F32)
    X12 = dscr("X12", [2, 512, L], F32)
    WB = {}
    for nm, shp in (("w0in", [D, 3264]), ("w_uq", [384, 1024]), ("w_ukv", [256, 1024]), ("w1in", [D, 2944]),
                    ("w_out", [2 * D, D]), ("w_up", [2 * D, 2 * DFF]), ("w_down", [2 * DFF, D])):
        WB[nm] = dscr("wb_" + nm, shp, BF)

    es = contextlib.ExitStack()

    @contextlib.contextmanager
    def phase():
        st_ = contextlib.ExitStack()
        try:
            yield st_
            S.barrier()
        finally:
            st_.close()


    live = [0, 0]
    SB_LIMIT = 229344 - 16481 - 4096

    def sb(shape, dt, name=None, stack=None):
        uid[0] += 1
        nb = 1
        for d_ in shape[1:]:
            nb *= int(d_)
        nb *= 2 if dt == BF else 4
        nb = (nb + 31) // 32 * 32
        live[0] += nb
        live[1] = max(live[1], live[0])
        assert live[0] <= SB_LIMIT, ("SBUF over budget", name, live[0])
        stk = stack or es

        def _rel():
            live[0] -= nb
        stk.callback(_rel)
        return stk.enter_context(nc.sbuf_tensor("%s_%d" % (name or "t", uid[0]), list(shape), dt))

    with es:
        PS = [es.enter_context(nc.psum_tensor("ps%d" % i, [128, 512], F32)) for i in range(8)]
        ident = sb([128, 128], F32, "ident")
        ones_bf = sb([128, 128], BF, "ones")
        modA = sb([128, 2, 4, R, 8], F32, "modA")
        fcw = sb([128, 2, 44, 3], F32, "fcw")
        fcb = sb([128, 2, 44], F32, "fcb")
        qng = sb([128, 3], F32, "qng")
        kvng = sb([128, 2], F32, "kvng")
        subg = sb([128, 1], F32, "subg")
        nlam = sb([128, 1], F32, "nlam")
        esink = sb([128, 8], F32, "esink")
        hcw = sb([128, 12, 3], F32, "hcw")
        hcb = sb([128, 12], F32, "hcb")
        hbias = sb([128, 2, 4], F32, "hbias")
        small = sb([128, 64], F32, "small")
        smallrr = [0]

        def scol():
            smallrr[0] = (smallrr[0] + 1) % 64
            return small[:, smallrr[0]:smallrr[0] + 1]

        zeros512 = sb([128, 512], F32, "zeros")
        S.memset(zeros512[:], 0.0)
        S.dma(ident[:], I["ident"])
        S.memset(ones_bf[:], 1.0)
        S.dma(fcw[:], I["fcw"])
        S.dma(fcb[:], I["fcb"])
        S.dma(qng[:], I["qngT"])
        S.dma(kvng[:], I["kvngT"])
        S.dma(subg[:], I["subgT"])
        S.dma(hcw[:], I["hcw"])
        S.dma(hcb[:], I["hcb"])
        S.dma(hbias[:], I["hbias"])

        def rstd_from(ss, n_feat, np_=128):
            a = scol()
            S.act(a[0:np_], ss, AF.Sqrt, bias=EPS, scale=1.0 / n_feat)
            b = scol()
            S.recip(b[0:np_], a[0:np_])
            return b

        def prologue_cast():
            S.tag = "cast"
            with phase() as ps_:
                CH = 4096
                fbuf = RR([sb([128, CH], F32, "cf", ps_) for _ in range(3)])
                bbuf = RR([sb([128, CH], BF, "cb", ps_) for _ in range(3)])
                engs = RR(["dve", "act", "pool"])
                srcs = [("w0in", I["w0in"]), ("w_uq", I["w_uq"]), ("w_ukv", I["w_ukv"]), ("w1in", I["w1in"]),
                        ("w_out", I["w_out"].rearrange("l k n -> (l k) n")),
                        ("w_up", I["w_up"].rearrange("l k n -> (l k) n")),
                        ("w_down", I["w_down"].rearrange("l k n -> (l k) n"))]
                for nm, src in srcs:
                    rows, cols = src.shape
                    tot = rows * cols
                    per = tot // 128
                    sflat = src.rearrange("(p a) n -> p (a n)", p=128)
                    dflat = WB[nm].rearrange("(p a) n -> p (a n)", p=128)
                    o = 0
                    while o < per:
                        n = min(CH, per - o)
                        fb = fbuf.get()
                        bb = bbuf.get()
                        S.dma(fb[:, 0:n], sflat[:, o:o + n])
                        S.copy(bb[:, 0:n], fb[:, 0:n], eng=engs.get())
                        S.dma(dflat[:, o:o + n], bb[:, 0:n])
                        o += n
            S.barrier()

        def prologue_mod():
            S.tag = "mod"
            with phase() as ps_:
                scT = sb([128, 8, R], F32, "scT", ps_)
                gT = sb([128, 2, 4, 8], F32, "gT", ps_)
                abT = sb([128, 2, 48], F32, "abT", ps_)
                modT = sb([128, 48, R], F32, "modT", ps_)
                PG = sb([128, 2, 2, R, 8], F32, "PG", ps_)
                rowt = sb([8, 128], F32, "rowt", ps_)
                S.dma(scT[:], I["cT"])
                S.act(scT[:], scT[:], AF.Silu)
                S.dma(gT[:], I["norm_gT"])
                S.dma(abT[:], I["ada_bT"])
                awb = RR([sb([128, 8, 512], F32, "aw", ps_) for _ in range(2)])
                for l in range(2):
                    for cc in range(12):
                        aw = awb.get()
                        S.dma(aw[:], I["ada_w"][l, :, cc * 512:(cc + 1) * 512].rearrange("(k p) n -> p k n", p=128))
                        for ct in range(4):
                            j = cc * 4 + ct
                            ps = PS[j % 2]
                            S.mmacc(ps[:, 0:R], [(aw[:, k, ct * 128:(ct + 1) * 128], scT[:, k, :]) for k in range(8)])
                            S.ts(modT[:, j, :], ps[:, 0:R], abT[:, l, j:j + 1], None, ALU.add)
                    for r in range(R):
                        def m_(i):
                            return modT[:, i * 8:(i + 1) * 8, r]
                        S.stt(modA[:, l, 0, r, :], m_(1), 1.0, gT[:, l, 0, :], ALU.add, ALU.mult)
                        S.copy(modA[:, l, 1, r, :], m_(0))
                        S.stt(modA[:, l, 2, r, :], m_(4), 1.0, gT[:, l, 2, :], ALU.add, ALU.mult)
                        S.copy(modA[:, l, 3, r, :], m_(3))
                        S.tt(PG[:, l, 0, r, :], m_(2), gT[:, l, 1, :], ALU.mult)
                        S.tt(PG[:, l, 1, r, :], m_(5), gT[:, l, 3, :], ALU.mult)
                        for w_ in range(2):
                            ps = PS[2 + (w_ % 2)]
                            S.transpose(ps[0:8, 0:128], PG[:, l, w_, r, :], ident[:])
                            S.copy(rowt[:], ps[0:8, 0:128])
                            S.dma(GSC[l, r, w_, :].rearrange("(k p) -> k p", p=128), rowt[:])
                dl = sb([128, 256], F32, "dl", ps_)
                S.dma(dl[:], I["dlam"].partition_broadcast(128))
                pr = sb([128, 128], F32, "pr", ps_)
                S.tt(pr[:, 0:64], dl[:, 0:64], dl[:, 64:128], ALU.mult)
                S.tt(pr[:, 64:128], dl[:, 128:192], dl[:, 192:256], ALU.mult)
                s1 = scol(); s2 = scol()
                S.act(pr[:, 0:64], pr[:, 0:64], AF.Identity, accum_out=s1)
                S.act(pr[:, 64:128], pr[:, 64:128], AF.Identity, accum_out=s2)
                e1 = scol(); e2 = scol()
                S.act(e1, s1, AF.Exp)
                S.act(e2, s2, AF.Exp)
                S.tt(e2, e2, e1, ALU.subtract)
                S.ts(nlam[:], e2, -0.2, None, ALU.add)
                S.ts(subg[:], subg[:], 0.8, None, ALU.mult)
                sk = sb([128, 8], F32, "sk", ps_)
                S.dma(sk[:], I["sink"].partition_broadcast(128))
                S.act(esink[:], sk[:], AF.Exp)
            S.barrier()

        def sin_safe(out, pre, np_, n, tmp1, tmp2):
            S.act(tmp1, pre, AF.Sin, scale=0.25)
            S.act(tmp2, pre, AF.Sin, scale=0.25, bias=halfpi[0:np_, :])
            S.tt(tmp2, tmp1, tmp2, ALU.mult)
            S.tt(tmp1, tmp1, tmp1, ALU.mult)
            S.ts(tmp1, tmp1, -8.0, 4.0, ALU.mult, ALU.add)
            S.tt(out, tmp1, tmp2, ALU.mult)

        halfpi = sb([128, 1], F32, "halfpi")
        S.memset(halfpi[:], math.pi / 2)

        def prologue_hyena():
            S.tag = "hyp"
            with phase() as ps_:
                ft = sb([33, L], F32, "ft", ps_)
                w1 = sb([33, 64], F32, "w1", ps_)
                w2 = sb([64, 64], F32, "w2", ps_)
                w3 = sb([65, 2048], F32, "w3", ps_)
                b1 = sb([64, 1], F32, "b1", ps_)
                b2 = sb([64, 1], F32, "b2", ps_)
                fr = sb([64, 1], F32, "fr", ps_)
                h1 = sb([64, L], F32, "h1", ps_)
                h2 = sb([65, L], F32, "h2", ps_)
                t1 = sb([64, 512], F32, "t1", ps_)
                t2 = sb([64, 512], F32, "t2", ps_)
                t3 = sb([64, 512], F32, "t3", ps_)
                S.dma(ft[:], I["featsT"])
                S.dma(w1[:], I["hw1"])
                S.dma(w2[:], I["hw2"])
                S.dma(w3[0:64, :], I["hw3"])
                S.dma(w3[64:65, :], I["hb3"])
                S.dma(b1[:], I["hb1"])
                S.dma(b2[:], I["hb2"])
                S.dma(fr[:], I["hfreq"])
                S.tt(b1[:], b1[:], fr[:], ALU.mult)
                S.tt(b2[:], b2[:], fr[:], ALU.mult)
                S.memset(h2[64:65, :], 1.0)
                for c in range(4):
                    cs = slice(c * 512, (c + 1) * 512)
                    ps = PS[c % 2]
                    S.mm(ps[0:64, :], w1[:, :], ft[:, cs])
                    S.ts(t3[:], ps[0:64, :], fr[:, 0:1], b1[:, 0:1], ALU.mult, ALU.add)
                    sin_safe(h1[:, cs], t3[:], 64, 512, t1[:], t2[:])
                for c in range(4):
                    cs = slice(c * 512, (c + 1) * 512)
                    ps = PS[2 + c % 2]
                    S.mm(ps[0:64, :], w2[:, :], h1[:, cs])
                    S.ts(t3[:], ps[0:64, :], fr[:, 0:1], b2[:, 0:1], ALU.mult, ALU.add)
                    sin_safe(h2[0:64, cs], t3[:], 64, 512, t1[:], t2[:])
                dec = sb([128, 16, 512], F32, "dec", ps_)
                S.dma(dec[:], I["decay"].rearrange("(a p) n -> p a n", p=128))
                Pm = [sb([128, 16, 512], BF, "Pm", ps_) for _ in range(2)]
                Mm = [sb([128, 16, 512], BF, "Mm", ps_) for _ in range(2)]
                hf = sb([128, 512], F32, "hf", ps_)
                hb = sb([128, 512], F32, "hb", ps_)
                for n in range(2):
                    for a in range(16):
                        psf = PS[4]
                        psb = PS[5]
                        S.mm(psf[:, :], h2[:, a * 128:(a + 1) * 128], w3[:, n * 512:(n + 1) * 512])
                        S.mm(psb[:, :], h2[:, a * 128:(a + 1) * 128], w3[:, 1024 + n * 512:1024 + (n + 1) * 512])
                        S.tt(hf[:], psf[:, :], dec[:, a, :], ALU.mult)
                        S.tt(hb[:], psb[:, :], dec[:, a, :], ALU.mult)
                        if a == 0:
                            S.memset(hb[0:1, :], 0.0)
                        S.tt(Pm[n][:, a, :], hf[:], hb[:], ALU.add)
                        S.tt(Mm[n][:, a, :], hf[:], hb[:], ALU.subtract)
                fmb = RR([sb([128, 16, 128], BF, "fm", ps_) for _ in range(3)])
                ob = RR([sb([128, 512], F32, "so", ps_) for _ in range(3)])
                FMv = I["FM"].rearrange("(a p) f -> p a f", p=128)
                for n in range(2):
                    for ftile in range(32):
                        fm = fmb.get()
                        S.dma(fm[:], FMv[:, :, ftile * 128:(ftile + 1) * 128])
                        src = Pm[n] if ftile < 16 else Mm[n]
                        ps = PS[6 + ftile % 2]
                        S.mmacc(ps[:, :], [(fm[:, a, :], src[:, a, :]) for a in range(16)])
                        o = ob.get()
                        S.copy(o[:], ps[:, :], eng="act")
                        if ftile < 16:
                            S.dma(SPEC[n, 0, ftile * 128:(ftile + 1) * 128, :], o[:])
                            if ftile > 0:
                                S.dma(SPEC[n, 2, ftile * 128:(ftile + 1) * 128, :], o[:])
                            else:
                                S.dma(SPEC[n, 2, 1:128, :], o[1:128, :])
                        else:
                            if ftile == 16:
                                ps2 = PS[4]
                                S.mmacc(ps2[:, :], [(fm[:, a, :], Pm[n][:, a, :]) for a in range(16)])
                                o2 = ob.get()
                                S.copy(o2[0:1, :], ps2[0:1, :], eng="act")
                                S.dma(SPEC[n, 2, 0:1, :], o2[0:1, :])
                                S.memset(o[0:1, :], 0.0)
                            S.dma(SPEC[n, 1, (ftile - 16) * 128:(ftile - 15) * 128, :], o[:])
            S.barrier()

        def xtile_src(layer, s, tt):
            if layer == 0:
                if tt < 2:
                    return I["ctx"][s, tt * 128:(tt + 1) * 128, :]
                return I["x"][s, (tt - 2) * 128:(tt - 1) * 128, :]
            return XD[s, tt * 128:(tt + 1) * 128, :]

        def norm_transpose(src_fn, tts, acol, bcol_, HT, stack, tok_base=0):
            S.tag = "norm"
            xb = RR([sb([128, D], F32, "xb", stack) for _ in range(6)])
            xnb = RR([sb([128, D], F32, "xn", stack) for _ in range(8)])
            junk = sb([128, D], BF, "junk", stack)
            psr = RR([PS[0], PS[1], PS[2], PS[3]])
            for g0 in range(0, len(tts), 4):
                grp = tts[g0:g0 + 4]
                xns = []
                for tt in grp:
                    x = xb.get()
                    S.dma(x[:], src_fn(tt))
                    ss = scol()
                    S.act(junk[:], x[:], AF.Square, accum_out=ss)
                    r = rstd_from(ss, D)
                    xn = xnb.get()
                    S.ts(xn[:], x[:], r, None, ALU.mult)
                    xns.append(xn)
                for k in range(8):
                    ps = psr.get()
                    for i, xn in enumerate(xns):
                        S.transpose(ps[:, i * 128:(i + 1) * 128], xn[:, k * 128:(k + 1) * 128], ident[:])
                    i = 0
                    while i < len(grp):
                        j = i
                        while j + 1 < len(grp) and (grp[j + 1] < 2) == (grp[i] < 2):
                            j += 1
                        t0 = grp[i] * 128 - tok_base
                        if k % 2 == 0:
                            S.act(HT[:, k, t0:t0 + (j - i + 1) * 128], ps[:, i * 128:(j + 1) * 128], AF.Identity,
                                  scale=acol(grp[i])[:, k:k + 1], bias=bcol_(grp[i])[:, k:k + 1])
                        else:
                            S.ts(HT[:, k, t0:t0 + (j - i + 1) * 128], ps[:, i * 128:(j + 1) * 128],
                                 acol(grp[i])[:, k:k + 1], bcol_(grp[i])[:, k:k + 1], ALU.mult, ALU.add)
                        i = j + 1

        def load_w(dst, wb_ap, r0, nk, c0, ncols, q="sp"):
            S.dma(dst, wb_ap[r0:r0 + nk * 128, c0:c0 + ncols].rearrange("(k p) n -> p k n", p=128), q=q)

        CHUNKS0 = [(0, 256)] + [(256 + 512 * i, 512) for i in range(4)]

        def rope_apply(dst, ps, psr, tab, prange, tok0, n, tmpa, tmpb):
            p0, p1 = prange
            S.tt(tmpa[p0:p1, 0:n], ps[p0:p1, 0:n], tab[p0:p1, 0, tok0:tok0 + n], ALU.mult)
            S.tt(tmpb[p0:p1, 0:n], psr[p0:p1, 0:n], tab[p0:p1, 1, tok0:tok0 + n], ALU.mult)
            S.tt(dst, tmpa[p0:p1, 0:n], tmpb[p0:p1, 0:n], ALU.add, eng="pool")

        def residual_update(s, layer, which, tt, psy, xsrc, gtile, dst, stack_bufs):
            xb, tb, junk = stack_bufs
            s1 = scol(); s2 = scol()
            S.act(junk[:, 0:512], psy[0][:, :], AF.Square, accum_out=s1)
            S.act(junk[:, 512:1024], psy[1][:, :], AF.Square, accum_out=s2)
            S.tt(s1, s1, s2, ALU.add)
            r = rstd_from(s1, D)
            x = xsrc
            t = tb.get()
            for h in range(2):
                hs = slice(h * 512, (h + 1) * 512)
                S.stt(t[:, hs], psy[h][:, :], r, gtile[:, hs], ALU.mult, ALU.mult)
                S.tt(t[:, hs], t[:, hs], x[:, hs], ALU.add, eng="pool")
            S.dma(dst, t[:])

        def mixer0(s, HT, OT, stack):
            r_l, r_c = s, NS
            with phase() as st:
                CQN = sb([128, 3, NT], BF, "CQN", st)
                CKVN = sb([128, 2, NT], BF, "CKVN", st)
                KRT = sb([128, NT], BF, "KRT", st)
                ropeM = sb([128, 2, L], F32, "ropeM", st)
                S.dma(ropeM[64:96], I["ropeM"][64:96])
                tmpa = sb([128, 512], F32, "tmpa", st)
                tmpb = sb([128, 512], F32, "tmpb", st)
                S.tag = "lat"
                with phase() as st2:
                    wl = sb([128, 8, 640], BF, "wl", st2)
                    wkr = sb([128, 8, 64], BF, "wkr", st2)
                    load_w(wl[:], WB["w0in"], 0, 8, 0, 640)
                    load_w(wkr[:], WB["w0in"], 0, 8, 3200, 64)
                    cf = [sb([128, 512], F32, "cf", st2) for _ in range(5)]
                    sq = [sb([128, 512], BF, "sq", st2) for _ in range(5)]
                    rs = sb([128, 512], F32, "rs", st2)
                    for (c0, n) in CHUNKS0:
                        for (base, nt_, gcol, dst, nf) in ((0, 3, qng, CQN, 384), (3, 2, kvng, CKVN, 256)):
                            for ct in range(nt_):
                                ps = PS[ct]
                                S.mmacc(ps[:, 0:n], [(wl[:, k, (base + ct) * 128:(base + ct + 1) * 128], HT[:, k, c0:c0 + n]) for k in range(8)])
                                S.copy(cf[base + ct][:, 0:n], ps[:, 0:n], eng="act")
                                S.tt(sq[base + ct][:, 0:n], cf[base + ct][:, 0:n], cf[base + ct][:, 0:n], ALU.mult, eng="pool")
                            pss = PS[3]
                            S.mmacc(pss[:, 0:n], [(ones_bf[:, :], sq[base + ct][:, 0:n]) for ct in range(nt_)])
                            S.act(rs[:, 0:n], pss[:, 0:n], AF.Sqrt, bias=EPS, scale=1.0 / nf)
                            S.recip(rs[:, 0:n], rs[:, 0:n])
                            for ct in range(nt_):
                                S.stt(dst[:, ct, c0:c0 + n], cf[base + ct][:, 0:n], gcol[:, ct:ct + 1], rs[:, 0:n], ALU.mult, ALU.mult)
                        pk = PS[4]; pkr = PS[5]
                        S.mmacc(pk[64:96, 0:n], [(wkr[:, k, 0:32], HT[:, k, c0:c0 + n]) for k in range(8)])
                        if c0 == 0:
                            S.copy(KRT[64:96, 0:n], pk[64:96, 0:n], eng="act")
                        else:
                            S.mmacc(pkr[64:96, 0:n], [(wkr[:, k, 32:64], HT[:, k, c0:c0 + n]) for k in range(8)])
                            rope_apply(KRT[64:96, c0:c0 + n], pk, pkr, ropeM, (64, 96), c0 - C, n, tmpa, tmpb)
                S.tag = "mla"
                with phase() as st2:
                    wuq = sb([128, 3, 1024], BF, "wuq", st2)
                    wukv = sb([128, 2, 1024], BF, "wukv", st2)
                    load_w(wuq[:], WB["w_uq"], 0, 3, 0, 1024)
                    load_w(wukv[:], WB["w_ukv"], 0, 2, 0, 1024)
                    QHb = RR([sb([128, NT], BF, "QH", st2) for _ in range(2)])
                    KHb = RR([sb([128, NT], BF, "KH", st2) for _ in range(2)])
                    VHb = [sb([128, 18, 128], BF, "VH", st2) for _ in range(2)]
                    for v in VHb:
                        S.memset(v[:, :, 64:128], 1.0)
                    VHr = RR(VHb)
                    PTb = RR([sb([128, 512], BF, "PT", st2) for _ in range(6)])
                    dtmp = RR([sb([64, 512], F32, "dtmp", st2) for _ in range(2)])
                    scale = 96 ** -0.5
                    hb_ = {}

                    def mla_proj(h):
                        QH = QHb.get(); KH = KHb.get(); VH = VHr.get()
                        hb_[h] = (QH, KH, VH)
                        for (c0, n) in CHUNKS0:
                            pq = PS[0]; pqr = PS[1]; pk = PS[2]
                            S.mmacc(pq[0:96, 0:n], [(wuq[:, k, h * 96:(h + 1) * 96], CQN[:, k, c0:c0 + n]) for k in range(3)])
                            S.mmacc(pk[0:64, 0:n], [(wukv[:, k, h * 64:(h + 1) * 64], CKVN[:, k, c0:c0 + n]) for k in range(2)])
                            S.copy(KH[0:64, c0:c0 + n], pk[0:64, 0:n], eng="act")
                            if c0 == 0:
                                S.copy(QH[0:96, 0:n], pq[0:96, 0:n], eng="act")
                            else:
                                S.mmacc(pqr[64:96, 0:n], [(wuq[:, k, 768 + h * 32:768 + (h + 1) * 32], CQN[:, k, c0:c0 + n]) for k in range(3)])
                                S.copy(QH[0:64, c0:c0 + n], pq[0:64, 0:n], eng="act")
                                rope_apply(QH[64:96, c0:c0 + n], pq, pqr, ropeM, (64, 96), c0 - C, n, tmpa, tmpb)
                        S.copy(KH[64:96, :], KRT[64:96, :], eng="pool")
                        for t8 in range(0, 18, 8):
                            nn = min(8, 18 - t8)
                            psv = PS[1]
                            for i in range(nn):
                                tt = t8 + i
                                S.mmacc(psv[:, i * 64:(i + 1) * 64], [(CKVN[:, k, tt * 128:(tt + 1) * 128], wukv[:, k, 512 + h * 64:512 + (h + 1) * 64]) for k in range(2)])
                            S.copy(VH[:, t8:t8 + nn, 0:64], psv[:, 0:nn * 64].rearrange("p (a b) -> p a b", b=64))

                    def mla_attn(h):
                        QH, KH, VH = hb_.pop(h)
                        units = []
                        for ci, (c0, n) in enumerate(CHUNKS0):
                            nk = 2 if c0 == 0 else 18
                            for kt in range(nk):
                                units.append((ci, c0, n, kt, nk))
                        psS = RR([PS[3], PS[4], PS[5]])
                        psO = RR([PS[6], PS[7]])
                        cur_o = {}
                        pend = []

                        def do_pv(u, pt):
                            ci, c0, n, kt, nk = u
                            if kt == 0:
                                cur_o[ci] = psO.get()
                            po = cur_o[ci]
                            S.mm(po[:, 0:n], VH[:, kt, :], pt[:, 0:n], start=(kt == 0), stop=(kt == nk - 1))
                            if kt == nk - 1:
                                dt = dtmp.get()
                                S.copy(dt[0:64, 0:n], po[64:128, 0:n], eng="act")
                                S.recip(dt[0:64, 0:n], dt[0:64, 0:n])
                                S.tt(OT[(h % 2) * 64:(h % 2) * 64 + 64, h // 2, c0:c0 + n], po[0:64, 0:n], dt[0:64, 0:n], ALU.mult)

                        for u in units:
                            ci, c0, n, kt, nk = u
                            pss = psS.get()
                            S.mm(pss[:, 0:n], KH[0:96, kt * 128:(kt + 1) * 128], QH[0:96, c0:c0 + n])
                            pt = PTb.get()
                            S.act(pt[:, 0:n], pss[:, 0:n], AF.Exp, scale=scale)
                            pend.append((u, pt))
                            if len(pend) > 2:
                                do_pv(*pend.pop(0))
                        while pend:
                            do_pv(*pend.pop(0))
                    mla_proj(0)
                    for h in range(8):
                        if h + 1 < 8:
                            mla_proj(h + 1)
                        mla_attn(h)
            S.tag = "diff"
            with phase() as st2:
                tmpa = sb([128, 512], F32, "tmpa", st2)
                tmpb = sb([128, 512], F32, "tmpb", st2)
                ropeD = sb([128, 2, L], F32, "ropeD", st2)
                S.dma(ropeD[:], I["ropeD"])
                wdb = RR([sb([128, 8, 5, 128], BF, "wd", st2) for _ in range(2)])
                DQb = RR([sb([128, NT], BF, "DQ", st2) for _ in range(2)])
                DKb = RR([sb([128, NT], BF, "DK", st2) for _ in range(2)])
                DKzb = RR([[sb([128, NT], BF, "DKz", st2) for _ in range(2)] for _ in range(2)])
                for pair_ in DKzb.items:
                    S.memset(pair_[0][64:128, :], 0.0, eng="pool")
                    S.memset(pair_[1][0:64, :], 0.0, eng="pool")
                DVb = RR([sb([128, 18, 128], BF, "DV", st2) for _ in range(2)])
                PTb = RR([sb([128, 512], BF, "PT", st2) for _ in range(6)])
                esets = [(sb([128, 512], F32, "a1", st2), sb([128, 512], F32, "a2", st2), sb([128, 512], F32, "r1", st2),
                          sb([128, 512], F32, "r2", st2), sb([128, 512], BF, "sqd", st2)) for _ in range(2)]
                scale = 64 ** -0.5
                dh_ = {}

                def diff_proj(h):
                    wd = wdb.get()
                    for i, cb in enumerate((640, 1152, 1664, 2176, 2688)):
                        load_w(wd[:, :, i, :], WB["w0in"], 0, 8, cb + h * 128, 128)
                    DQ = DQb.get(); DK = DKb.get(); DV = DVb.get(); DKz = DKzb.get()
                    for (c0, n) in CHUNKS0:
                        for (dst, wi) in ((DQ, 0), (DK, 2)):
                            pa = PS[0]; pb = PS[1]
                            S.mmacc(pa[:, 0:n], [(wd[:, k, wi, :], HT[:, k, c0:c0 + n]) for k in range(8)])
                            if c0 == 0:
                                S.copy(dst[:, 0:n], pa[:, 0:n], eng="act")
                            else:
                                S.mmacc(pb[:, 0:n], [(wd[:, k, wi + 1, :], HT[:, k, c0:c0 + n]) for k in range(8)])
                                rope_apply(dst[:, c0:c0 + n], pa, pb, ropeD, (0, 128), c0 - C, n, tmpa, tmpb)
                    for t4 in range(0, 18, 4):
                        ps = PS[1]
                        nn = min(4, 18 - t4)
                        for i in range(nn):
                            tt = t4 + i
                            S.mmacc(ps[:, i * 128:(i + 1) * 128], [(HT[:, k, tt * 128:(tt + 1) * 128], wd[:, k, 4, :]) for k in range(8)])
                        S.copy(DV[:, t4:t4 + nn, :], ps[:, 0:nn * 128].rearrange("p (a b) -> p a b", b=128), eng="act")
                    S.copy(DKz[0][0:64, :], DK[0:64, :], eng="pool")
                    S.copy(DKz[1][64:128, :], DK[64:128, :], eng="pool")
                    dh_[h] = (DQ, DV, DKz)

                def diff_attn(h):
                    DQ, DV, DKz = dh_.pop(h)
                    deferred = []
                    for ci, (c0, n) in enumerate(CHUNKS0):
                        nk = 2 if c0 == 0 else 18
                        psSj = [PS[3], PS[4]]
                        pos = [(PS[5], PS[6]), (PS[7], PS[2])]
                        a1, a2, r1, r2, sqd = esets[ci % 2]
                        acc = [a1, a2]
                        pend = []

                        def do_pv(kt, j, pt, n=n, nk=nk):
                            po, pd = pos[j]
                            S.mm(po[:, 0:n], DV[:, kt, :], pt[:, 0:n], start=(kt == 0), stop=(kt == nk - 1))
                            S.mm(pd[:, 0:n], ones_bf[:, :], pt[:, 0:n], start=(kt == 0), stop=(kt == nk - 1))

                        for kt in range(nk):
                            for j in range(2):
                                pss = psSj[j]
                                S.mm(pss[:, 0:n], DKz[j][:, kt * 128:(kt + 1) * 128], DQ[:, c0:c0 + n])
                                pt = PTb.get()
                                S.act(pt[:, 0:n], pss[:, 0:n], AF.Exp, scale=scale)
                                pend.append((kt, j, pt))
                            while len(pend) > 2:
                                do_pv(*pend.pop(0))
                            if kt == min(3, nk - 1) and deferred:
                                deferred.pop(0)()
                        while pend:
                            do_pv(*pend.pop(0))
                        for j in range(2):
                            po, pd = pos[j]
                            rr_ = r1 if j == 0 else r2
                            S.recip(rr_[:, 0:n], pd[:, 0:n])
                            S.tt(acc[j][:, 0:n], po[:, 0:n], rr_[:, 0:n], ALU.mult)
                        S.stt(a1[:, 0:n], a2[:, 0:n], nlam[:, 0:1], a1[:, 0:n], ALU.mult, ALU.add)
                        S.tt(sqd[:, 0:n], a1[:, 0:n], a1[:, 0:n], ALU.mult, eng="pool")

                        def fin(a1=a1, r1=r1, sqd=sqd, c0=c0, n=n):
                            pss = PS[0]
                            S.mm(pss[:, 0:n], ones_bf[:, :], sqd[:, 0:n])
                            S.act(r1[:, 0:n], pss[:, 0:n], AF.Sqrt, bias=EPS, scale=1.0 / 128)
                            S.recip(r1[:, 0:n], r1[:, 0:n])
                            S.stt(OT[:, 4 + h, c0:c0 + n], a1[:, 0:n], subg[:, 0:1], r1[:, 0:n], ALU.mult, ALU.mult)
                        deferred.append(fin)
                    while deferred:
                        deferred.pop(0)()

                diff_proj(0)
                for h in range(4):
                    if h + 1 < 4:
                        diff_proj(h + 1)
                    diff_attn(h)

        def wout_residual(s, layer, OT, tts, tok_base):
            S.tag = "wout"
            with phase() as st:
                wo = sb([128, 8, D], BF, "wo", st)
                load_w(wo[:], WB["w_out"], layer * D, 8, 0, D)
                G = {}
                for which, r in (("l", s), ("c", NS)):
                    if which == "c" and layer == 1:
                        continue
                    G[which] = sb([128, D], F32, "G", st)
                    S.dma(G[which][:], GSC[layer, r, 0:1, :].partition_broadcast(128))
                bufs = (RR([sb([128, D], F32, "xr", st) for _ in range(4)]), RR([sb([128, D], F32, "tr", st) for _ in range(3)]),
                        sb([128, D], BF, "junk", st))
                psr = RR([(PS[0], PS[1]), (PS[2], PS[3]), (PS[4], PS[5])])
                xq = []
                PF = 2

                def pref(i_):
                    if i_ < len(tts):
                        x_ = bufs[0].get()
                        S.dma(x_[:], xtile_src(layer, s, tts[i_]))
                        xq.append(x_)
                for i_ in range(PF):
                    pref(i_)
                for i_, tt in enumerate(tts):
                    pref(i_ + PF)
                    psy = psr.get()
                    t0 = tt * 128 - tok_base
                    for h in range(2):
                        S.mmacc(psy[h][:, :], [(OT[:, k, t0:t0 + 128], wo[:, k, h * 512:(h + 1) * 512]) for k in range(8)])
                    residual_update(s, layer, 0, tt, psy, xq.pop(0), G["c" if tt < 2 else "l"],
                                    XD[s, tt * 128:(tt + 1) * 128, :], bufs)

        def ffn(s, layer, last, chk=lambda n: None):
            tts = list(range(18)) if layer == 0 else list(range(2, 18))
            tok_base = 0 if layer == 0 else C
            ntok = NT - tok_base
            with phase() as st:
                HT = sb([128, 8, ntok], BF, "HT2", st)
                with phase() as st2:
                    norm_transpose(lambda tt: XD[s, tt * 128:(tt + 1) * 128, :], tts,
                                   lambda tt: modA[:, layer, 2, NS if tt < 2 else s, :],
                                   lambda tt: modA[:, layer, 3, NS if tt < 2 else s, :], HT, st2, tok_base)
                chk("ffn_norm")
                S.tag = "ffn"
                wdn = sb([128, 22, D], BF, "wdn", st)
                load_w(wdn[:], WB["w_down"], layer * DFF, 22, 0, D)
                G = {}
                for which, r in (("l", s), ("c", NS)):
                    if which == "c" and layer == 1:
                        continue
                    G[which] = sb([128, D], F32, "G3", st)
                    S.dma(G[which][:], GSC[layer, r, 1:2, :].partition_broadcast(128))
                GT = sb([128, 22, 1024], BF, "GT", st)
                HALO = sb([128, 44, 2], F32, "HALO", st)
                wub = RR([sb([128, 8, 2, 128], BF, "wu", st) for _ in range(3)])
                Yb = RR([sb([128, 1026], F32, "Y", st) for _ in range(4)])
                Ub = RR([sb([128, 1024], F32, "U", st) for _ in range(4)])
                bufs = (RR([sb([128, D], F32, "xr", st) for _ in range(2)]), RR([sb([128, D], F32, "tr", st) for _ in range(2)]),
                        sb([128, D], BF, "junk", st))
                scs = []
                if layer == 0:
                    scs.append((0, 256, 0))
                scs += [(256, 1024, 1), (1280, 1024, 2)]
                psU = RR([PS[0], PS[1], PS[2], PS[3]])
                psH = RR([PS[4], PS[5]])
                psD = RR([(PS[6], PS[7]), (PS[4], PS[5])])
                chk("ffn_load")
                for (t0, n, kind) in scs:
                    a0 = t0 - tok_base
                    for j in range(22):
                        wu = wub.get()
                        load_w(wu[:, :, 0, :], WB["w_up"], layer * D, 8, j * 128, 128)
                        load_w(wu[:, :, 1, :], WB["w_up"], layer * D, 8, DFF + j * 128, 128)
                        Us = []
                        for ag in range(2):
                            Y = Yb.get(); U = Ub.get()
                            jj = ag * 22 + j
                            for c in range(0, n, 512):
                                nn = min(512, n - c)
                                ps = psU.get()
                                S.mmacc(ps[:, 0:nn], [(wu[:, k, ag, :], HT[:, k, a0 + c:a0 + c + nn]) for k in range(8)])
                                S.copy(Y[:, 1 + c:1 + c + nn], ps[:, 0:nn], eng="act")
                                S.act(U[:, c:c + nn], ps[:, 0:nn], AF.Identity, scale=fcw[:, layer, jj, 1:2], bias=fcb[:, layer, jj:jj + 1])
                            if kind == 1:
                                ph = psH.get()
                                S.mmacc(ph[:, 0:2], [(wu[:, k, ag, :], HT[:, k, a0 + n - 1:a0 + n + 1]) for k in range(8)])
                                S.copy(Y[:, n + 1:n + 2], ph[:, 1:2], eng="act")
                                S.copy(HALO[:, jj, 0:1], ph[:, 0:1], eng="act")
                                S.memset(Y[:, 0:1], 0.0, eng="pool")
                            elif kind == 2:
                                S.copy(Y[:, 0:1], HALO[:, jj, 0:1], eng="pool")
                                S.memset(Y[:, n + 1:n + 2], 0.0, eng="pool")
                            else:
                                S.memset(Y[:, 0:1], 0.0, eng="pool")
                                S.memset(Y[:, n + 1:n + 2], 0.0, eng="pool")
                            S.stt(U[:, 0:n], Y[:, 0:n], fcw[:, layer, jj, 0:1], U[:, 0:n], ALU.mult, ALU.add)
                            S.stt(U[:, 0:n], Y[:, 2:n + 2], fcw[:, layer, jj, 2:3], U[:, 0:n], ALU.mult, ALU.add)
                            Us.append(U)
                        S.act(Us[1][:, 0:n], Us[1][:, 0:n], AF.Silu)
                        S.tt(GT[:, j, 0:n], Us[0][:, 0:n], Us[1][:, 0:n], ALU.mult, eng="pool")
                    chk("ffn_up")
                    xq = []

                    def pref(ti_):
                        if ti_ < n // 128:
                            x_ = bufs[0].get()
                            tt_ = t0 // 128 + ti_
                            S.dma(x_[:], XD[s, tt_ * 128:(tt_ + 1) * 128, :])
                            xq.append(x_)
                    pref(0)
                    for ti in range(n // 128):
                        pref(ti + 1)
                        tt = t0 // 128 + ti
                        psy = psD.get()
                        for h in range(2):
                            S.mmacc(psy[h][:, :], [(GT[:, j, ti * 128:(ti + 1) * 128], wdn[:, j, h * 512:(h + 1) * 512]) for j in range(22)])
                        if last:
                            dst = OUT[s, (tt - 2) * 128:(tt - 1) * 128, :]
                        else:
                            dst = XD[s, tt * 128:(tt + 1) * 128, :]
                        residual_update(s, layer, 1, tt, psy, xq.pop(0), G["c" if tt < 2 else "l"], dst, bufs)
                        chk("ffn_res")

        def mixer1(s, HT, OT, Z):
            S.tag = "m1hyproj"
            with phase() as st2:
                wub = RR([sb([128, 8, 128], BF, "wu1", st2) for _ in range(3)])
                Yb = RR([sb([128, L + 2], F32, "Yh", st2) for _ in range(2)])
                Ub = RR([sb([128, L], F32, "Uh", st2) for _ in range(2)])
                for j in range(12):
                    wu = wub.get()
                    load_w(wu[:], WB["w1in"], 0, 8, 1408 + j * 128, 128)
                    Y = Yb.get()
                    S.memset(Y[:, 0:1], 0.0)
                    S.memset(Y[:, L + 1:L + 2], 0.0)
                    dst = Z[:, j, :] if j < 4 else Ub.get()[:, :]
                    for c in range(4):
                        ps = PS[5 + c % 2]
                        tk = C + c * 512
                        S.mmacc(ps[:, :], [(wu[:, k, :], HT[:, k, tk:tk + 512]) for k in range(8)])
                        S.copy(Y[:, 1 + c * 512:1 + (c + 1) * 512], ps[:, :], eng="act")
                        S.ts(dst[:, c * 512:(c + 1) * 512], ps[:, :], hcw[:, j, 1:2], hcb[:, j:j + 1], ALU.mult, ALU.add)
                    S.stt(dst, Y[:, 0:L], hcw[:, j, 0:1], dst, ALU.mult, ALU.add)
                    S.stt(dst, Y[:, 2:L + 2], hcw[:, j, 2:3], dst, ALU.mult, ALU.add)
                    if j >= 4:
                        jj = j - 4
                        S.dma(X12[jj // 4, (jj % 4) * 128:(jj % 4 + 1) * 128, :], dst)
            S.tag = "m1winproj"
            with phase() as st1:
                tmpa = sb([128, 512], F32, "tmpa", st1)
                tmpb = sb([128, 512], F32, "tmpb", st1)
                QW = sb([64, 8, L], BF, "QW", st1)
                KW = sb([64, 2, NT], BF, "KW", st1)
                VW = sb([128, 18, 2, 128], BF, "VW", st1)
                S.memset(VW[:, :, :, 64:128], 1.0)
                with phase() as st2:
                    wqb = RR([sb([128, 8, 2, 64], BF, "wq", st2) for _ in range(3)])
                    wk = sb([128, 8, 384], BF, "wk", st2)
                    rtb = RR([sb([128, 2, 512], F32, "rt", st2) for _ in range(1)])
                    load_w(wk[:], WB["w1in"], 0, 8, 1024, 384)
                    for c in range(4):
                        rt = rtb.get()
                        S.dma(rt[0:64], I["ropeD"][0:64, :, c * 512:(c + 1) * 512])
                        tk = C + c * 512
                        for hd in range(8):
                            pa = PS[0]; pb = PS[1]
                            wq = wqb.get()
                            load_w(wq[:, :, 0, :], WB["w1in"], 0, 8, hd * 64, 64)
                            load_w(wq[:, :, 1, :], WB["w1in"], 0, 8, 512 + hd * 64, 64)
                            S.mmacc(pa[0:64, :], [(wq[:, k, 0, :], HT[:, k, tk:tk + 512]) for k in range(8)])
                            S.mmacc(pb[0:64, :], [(wq[:, k, 1, :], HT[:, k, tk:tk + 512]) for k in range(8)])
                            rope_apply(QW[0:64, hd, c * 512:(c + 1) * 512], pa, pb, rt, (0, 64), 0, 512, tmpa, tmpb)
                        for kh in range(2):
                            pa = PS[2]; pb = PS[3]
                            S.mmacc(pa[0:64, :], [(wk[:, k, kh * 64:(kh + 1) * 64], HT[:, k, tk:tk + 512]) for k in range(8)])
                            S.mmacc(pb[0:64, :], [(wk[:, k, 128 + kh * 64:128 + (kh + 1) * 64], HT[:, k, tk:tk + 512]) for k in range(8)])
                            rope_apply(KW[0:64, kh, tk:tk + 512], pa, pb, rt, (0, 64), 0, 512, tmpa, tmpb)
                    for kh in range(2):
                        pa = PS[2]
                        S.mmacc(pa[0:64, 0:C], [(wk[:, k, kh * 64:(kh + 1) * 64], HT[:, k, 0:C]) for k in range(8)])
                        S.copy(KW[0:64, kh, 0:C], pa[0:64, 0:C], eng="act")
                    for t4 in range(0, 18, 4):
                        ps = PS[4]
                        nn = min(4, 18 - t4)
                        for i in range(nn):
                            tt = t4 + i
                            S.mmacc(ps[:, i * 128:(i + 1) * 128], [(HT[:, k, tt * 128:(tt + 1) * 128], wk[:, k, 256:384]) for k in range(8)])
                        for kh in range(2):
                            S.copy(VW[:, t4:t4 + nn, kh, 0:64],
                                   ps[:, 0:nn * 128].rearrange("p (a b) -> p a b", b=128)[:, :, kh * 64:(kh + 1) * 64], eng="act")
                S.tag = "m1win"
                with phase() as st2:
                    wm = sb([128, 2, 512], BF, "wm", st2)
                    S.dma(wm[:], I["wmask"])
                    esx = sb([128, 2, 512], F32, "esx", st2)
                    for kh_ in range(2):
                        for g_ in range(4):
                            S.ts(esx[:, kh_, g_ * 128:(g_ + 1) * 128], zeros512[:, 0:128], esink[:, kh_ * 4 + g_:kh_ * 4 + g_ + 1], None, ALU.add)
                    PTb = RR([sb([128, 512], BF, "PTw", st2) for _ in range(6)])
                    dtmp = RR([sb([64, 512], F32, "dtw", st2) for _ in range(2)])
                    psS = RR([PS[0], PS[1], PS[2], PS[3]])
                    psO = RR([PS[4], PS[5]])
                    scale = 64 ** -0.5
                    for kh in range(2):
                        for nb in range(16):
                            kts = [(0, None), (1, None)]
                            if nb > 0:
                                kts.append((2 + nb - 1, 0))
                            kts.append((2 + nb, None))
                            if nb < 15:
                                kts.append((2 + nb + 1, 1))
                            po = psO.get()
                            pts = []
                            for (kt, mk) in kts:
                                pss = psS.get()
                                S.mm(pss[:, :], KW[0:64, kh, kt * 128:(kt + 1) * 128], QW[0:64, kh * 4:(kh + 1) * 4, nb * 128:(nb + 1) * 128])
                                pt = PTb.get()
                                S.act(pt[:, :], pss[:, :], AF.Exp, scale=scale)
                                if mk is not None:
                                    S.tt(pt[:, :], pt[:, :], wm[:, mk, :], ALU.mult, eng="pool")
                                pts.append((kt, pt))
                            for i, (kt, pt) in enumerate(pts):
                                S.mm(po[:, :], VW[:, kt, kh, :], pt[:, :], start=(i == 0), stop=(i == len(pts) - 1))
                            dt = dtmp.get()
                            S.tt(dt[0:64, :], po[64:128, :], esx[64:128, kh, :], ALU.add)
                            S.recip(dt[0:64, :], dt[0:64, :])
                            pov = po[0:64, :].rearrange("p (g q) -> p g q", g=4)
                            dtv = dt[0:64, :].rearrange("p (g q) -> p g q", g=4)
                            for g2 in range(2):
                                S.tt(OT[g2 * 64:g2 * 64 + 64, kh * 2:kh * 2 + 2, nb * 128:(nb + 1) * 128], pov[:, g2::2, :], dtv[:, g2::2, :], ALU.mult)

        def hyena(s, Z, OT, st):
            S.tag = "hyena"
            ZT = sb([128, 16, 512], BF, "ZT", st)
            Yf = sb([128, 32, 512], BF, "Yf", st)
            fmb = RR([sb([128, 16, 128], BF, "fmh", st) for _ in range(4)])
            imb = RR([sb([128, 8, 512], BF, "imh", st) for _ in range(2)])
            spb = RR([sb([128, 3, 512], F32, "sp", st) for _ in range(2)])
            xb = RR([sb([128, 512], F32, "x12", st) for _ in range(2)])
            tA = sb([128, 512], F32, "tA", st)
            tB = sb([128, 512], F32, "tB", st)
            tC = sb([128, 512], F32, "tC", st)
            zb = sb([128, 128], BF, "zb", st)
            FMv = I["FM"].rearrange("(a p) f -> p a f", p=128)
            IMv = I["IM"].rearrange("(a p) t -> p a t", p=128)
            for n in range(2):
                for a in range(16):
                    ps = PS[a % 2]
                    for ct in range(4):
                        S.transpose(ps[:, ct * 128:(ct + 1) * 128], Z[:, ct, a * 128:(a + 1) * 128], ident[:])
                    S.copy(ZT[:, a, :], ps[:, :], eng="act")
                for i in range(16):
                    fr_ = fmb.get(); fi_ = fmb.get()
                    S.dma(fr_[:], FMv[:, :, i * 128:(i + 1) * 128])
                    S.dma(fi_[:], FMv[:, :, L + i * 128:L + (i + 1) * 128])
                    sp = spb.get()
                    S.dma(sp[:], SPEC[n, :, i * 128:(i + 1) * 128, :].rearrange("w p c -> p w c"))
                    pr = PS[2 + 2 * (i % 2)]
                    pi = PS[3 + 2 * (i % 2)]
                    S.mmacc(pr[:, :], [(fr_[:, a, :], ZT[:, a, :]) for a in range(16)])
                    S.mmacc(pi[:, :], [(fi_[:, a, :], ZT[:, a, :]) for a in range(16)])
                    S.tt(tA[:], pr[:, :], sp[:, 0, :], ALU.mult)
                    S.tt(tB[:], pi[:, :], sp[:, 1, :], ALU.mult)
                    S.tt(Yf[:, i, :], tA[:], tB[:], ALU.subtract, eng="pool")
                    S.tt(tC[:], pr[:, :], sp[:, 1, :], ALU.mult)
                    S.tt(tB[:], pi[:, :], sp[:, 2, :], ALU.mult)
                    S.tt(Yf[:, 16 + i, :], tC[:], tB[:], ALU.add, eng="pool")
                for c in range(4):
                    pss = [PS[4 + ct] for ct in range(4)]
                    for f8 in range(4):
                        im = imb.get()
                        S.dma(im[:], IMv[:, f8 * 8:(f8 + 1) * 8, c * 512:(c + 1) * 512])
                        for ct in range(4):
                            for a in range(8):
                                fa = f8 * 8 + a
                                S.mm(pss[ct][:, :], Yf[:, fa, ct * 128:(ct + 1) * 128], im[:, a, :], start=(fa == 0), stop=(fa == 31))
                    for ct in range(4):
                        x = xb.get()
                        S.dma(x[:], X12[n, ct * 128:(ct + 1) * 128, c * 512:(c + 1) * 512])
                        zs = Z[:, ct, c * 512:(c + 1) * 512]
                        S.stt(tA[:], zs, hbias[:, n, ct:ct + 1], pss[ct][:, :], ALU.mult, ALU.add)
                        if n == 0:
                            S.tt(zs, tA[:], x[:], ALU.mult)
                        else:
                            S.tt(OT[:, 4 + ct, c * 512:(c + 1) * 512], tA[:], x[:], ALU.mult)

        def main_body(chk):
            for s in range(NS):
                with phase() as st:
                    HT = sb([128, 8, NT], BF, "HT", st)
                    OT = sb([128, 8, NT], BF, "OT", st)
                    with phase() as st2:
                        norm_transpose(lambda tt: xtile_src(0, s, tt), list(range(18)),
                                       lambda tt: modA[:, 0, 0, NS if tt < 2 else s, :],
                                       lambda tt: modA[:, 0, 1, NS if tt < 2 else s, :], HT, st2)
                    chk("norm0")
                    mixer0(s, HT, OT, st)
                    chk("mixer0")
                    if dbg and "ot0" in DBG and s == 0:
                        with phase() as std:
                            otf = sb([128, 8, NT], F32, "otf", std)
                            S.copy(otf[:], OT[:])
                            S.dma(DBG["ot0"].rearrange("(k p) t -> p k t", p=128), otf[:])
                    wout_residual(s, 0, OT, list(range(18)), 0)
                    chk("wout0")
                ffn(s, 0, False, chk)
                chk("ffn0")
                S.barrier()
                if dbg and "xd0" in DBG and s == 0:
                    for tt_ in range(18):
                        S.dma(DBG["xd0"][tt_ * 128:(tt_ + 1) * 128, :], XD[0, tt_ * 128:(tt_ + 1) * 128, :])
                    S.barrier()
                with phase() as st:
                    OT = sb([128, 8, L], BF, "OT1", st)
                    Z = sb([128, 4, L], F32, "Z", st)
                    with phase() as stH:
                        HT = sb([128, 8, NT], BF, "HT1", stH)
                        with phase() as st2:
                            norm_transpose(lambda tt: xtile_src(1, s, tt), list(range(18)),
                                           lambda tt: modA[:, 1, 0, NS if tt < 2 else s, :],
                                           lambda tt: modA[:, 1, 1, NS if tt < 2 else s, :], HT, st2)
                        chk("norm1")
                        mixer1(s, HT, OT, Z)
                        chk("mixer1")
                    with phase() as stY:
                        hyena(s, Z, OT, stY)
                        chk("hyena")
                    if dbg and "ot1" in DBG and s == 0:
                        otf = sb([128, 8, L], F32, "otf", st)
                        S.copy(otf[:], OT[:])
                        S.dma(DBG["ot1"].rearrange("(k p) t -> p k t", p=128), otf[:])
                        S.barrier()
                    wout_residual(s, 1, OT, list(range(2, 18)), C)
                    S.barrier()
                ffn(s, 1, True)
                S.barrier()

        class _Stop(Exception):
            pass

        def chk(name):
            if stop == name:
                raise _Stop()

        try:
            prologue_cast()
            chk("cast")
            prologue_mod()
            chk("mod")
            prologue_hyena()
            chk("hyp")
            main_body(chk)
        except _Stop:
            pass
        S.finish()
    return nc


def kernel(**inputs):
    NCORE = 8
    NS = 4
    nc = build_program(NS)
    in_maps = [_host_layout(inputs, NS, c) for c in range(NCORE)]
    res = run_bass_kernel_spmd(nc, in_maps, core_ids=list(range(NCORE)))
    out = np.concatenate([np.asarray(r["y"], np.float32) for r in res.results], axis=0)
    return out
```

```python
import contextlib
import math
import numpy as np
import ml_dtypes
import concourse.bass as bass
import concourse.mybir as mybir
from concourse.bass_utils import run_bass_kernel_spmd

F32 = mybir.dt.float32
BF = mybir.dt.bfloat16
AF = mybir.ActivationFunctionType
ALU = mybir.AluOpType
ENGS = ("pe", "act", "dve", "pool", "sp")

D = 1024
L = 2048
C = 256
NT = L + C
DFF = 2816
EPS = 1e-6


def _box(ap):
    t = ap.tensor
    dims = list(ap.ap)
    off = int(ap.offset)
    if t.__class__.__name__.startswith("DRam"):
        hi = off + sum((int(c) - 1) * abs(int(s)) for s, c in dims) + 1
        return (t.name, 0, 1, off, hi)
    row = 1
    for s in list(t.shape)[1:]:
        row *= int(s)
    p0 = off // row
    f0 = off % row
    pstep, pcnt = int(dims[0][0]), int(dims[0][1])
    npart = pcnt if (pstep == row or pcnt == 1) else 128 - p0
    ext = sum((int(c) - 1) * abs(int(s)) for s, c in dims[1:]) + 1
    if t.__class__.__name__.startswith("PSum"):
        return (t.name, 0, 128, 0, row)
    return (t.name, p0, p0 + npart, f0, f0 + ext)


def _overlap(a, b):
    return a[1] < b[2] and b[1] < a[2] and a[3] < b[4] and b[3] < a[4]


def _covers(a, b):
    return a[1] <= b[1] and a[2] >= b[2] and a[3] <= b[3] and a[4] >= b[4]


class Op:
    __slots__ = ("eng", "fn", "deps", "sig", "signaled", "is_dma", "slot", "slotn", "clock", "mm", "tag")

    def __init__(self, eng, fn, is_dma=False, mm=False):
        self.eng = eng
        self.fn = fn
        self.deps = []
        self.sig = 0
        self.signaled = False
        self.is_dma = is_dma
        self.slot = -1
        self.slotn = 0
        self.clock = None
        self.mm = mm


class Sched:
    def __init__(self, nc, n_dma_slots=32):
        self.nc = nc
        self.ops = []
        self.recs = {}
        self.nslots = n_dma_slots
        self.eobj = {"pe": nc.tensor, "act": nc.scalar, "dve": nc.vector, "pool": nc.gpsimd, "sp": nc.sync}
        self.last = {e: None for e in ENGS}
        self.dmas_since = []
        self.bar = {e: None for e in ENGS}
        self.stack = contextlib.ExitStack()
        self.esem = {e: self.stack.enter_context(nc.semaphore("s_" + e)) for e in ENGS}
        self.dsem = [self.stack.enter_context(nc.semaphore("d_%d" % i)) for i in range(n_dma_slots)]
        self.cnt = {e: 0 for e in ENGS}
        self.rr = 0
        self.rrq = {}
        self.slot_cnt = [0] * n_dma_slots
        self.slot_last = [None] * n_dma_slots
        self.known = {e: ({x: 0 for x in ENGS}, [0] * n_dma_slots) for e in ENGS}
        self.n_emitted = 0
        self.tag = ""
        self.names = None
        self.default_q = "sp"

    def add(self, eng, fn, reads=(), writes=(), is_dma=False, mm=False):
        op = Op(eng, fn, is_dma, mm)
        op.tag = self.tag
        deps = {}
        rb = [_box(a) for a in reads]
        wb = [_box(a) for a in writes]
        for b in rb:
            lst = self.recs.get(b[0])
            if lst:
                psum = b[0].startswith("ps")
                for r in lst:
                    if _overlap(r[0], b) and (r[2] or (psum and r[1].eng != eng)):
                        deps[id(r[1])] = r[1]
        for b in wb:
            lst = self.recs.get(b[0])
            if lst:
                for r in lst:
                    if _overlap(r[0], b):
                        if mm and r[2] and r[1].mm:
                            continue
                        deps[id(r[1])] = r[1]
        if self.bar[eng] is not None:
            for d in self.bar[eng]:
                deps[id(d)] = d
            self.bar[eng] = None
        op.deps = list(deps.values())
        for b in wb:
            lst = self.recs.setdefault(b[0], [])
            lst[:] = [r for r in lst if not _covers(b, r[0])]
            lst.append([b, op, True])
        for b in rb:
            lst = self.recs.setdefault(b[0], [])
            if not is_dma:
                lst[:] = [r for r in lst if not ((not r[2]) and r[0] == b and r[1].eng == eng and not r[1].is_dma)]
            lst.append([b, op, False])
        self.ops.append(op)
        if is_dma:
            self.dmas_since.append(op)
        else:
            self.last[eng] = op
        return op

    def barrier(self, engs=None, keep=()):
        sel = ENGS if engs is None else engs
        for e in sel:
            o = self.last[e]
            if o is not None:
                o.signaled = True
        self.flush()
        b = [self.last[e] for e in sel if self.last[e] is not None] + [d for d in self.dmas_since if d.eng in sel]
        self.dmas_since = [d for d in self.dmas_since if d.eng not in sel]
        for e in sel:
            self.bar[e] = list(b) + (self.bar[e] or [])
        if engs is None:
            self.recs = {}
        else:
            self.recs = {k: v for k, v in self.recs.items() if k.startswith(tuple(keep))}

    def mm(self, out, lhsT, rhs, start=True, stop=True):
        nc = self.nc
        return self.add("pe", lambda: nc.tensor.matmul(out, lhsT, rhs, start=start, stop=stop),
                        reads=[lhsT, rhs], writes=[out], mm=True)

    def mmacc(self, out, pairs):
        n = len(pairs)
        for i, (l, r) in enumerate(pairs):
            self.mm(out, l, r, start=(i == 0), stop=(i == n - 1))

    def transpose(self, out, in_, ident):
        nc = self.nc
        return self.add("pe", lambda: nc.tensor.transpose(out, in_, ident), reads=[in_, ident], writes=[out], mm=True)

    def act(self, out, in_, func, bias=None, scale=None, accum_out=None):
        nc = self.nc
        kw = {}
        rd = [in_]
        wr = [out]
        if bias is not None:
            kw["bias"] = bias
            if not isinstance(bias, (int, float)):
                rd.append(bias)
        if scale is not None:
            kw["scale"] = scale
            if not isinstance(scale, (int, float)):
                rd.append(scale)
        if accum_out is not None:
            kw["accum_out"] = accum_out
            wr.append(accum_out)
        return self.add("act", lambda: nc.scalar.activation(out, in_, func, **kw), reads=rd, writes=wr)

    def tt(self, out, in0, in1, op, eng="dve"):
        e = self.eobj[eng]
        return self.add(eng, lambda: e.tensor_tensor(out, in0, in1, op), reads=[in0, in1], writes=[out])

    def ts(self, out, in0, s1, s2, op0, op1=None, eng="dve"):
        e = self.eobj[eng]
        rd = [in0] + [s for s in (s1, s2) if s is not None and not isinstance(s, (int, float))]
        if op1 is None:
            return self.add(eng, lambda: e.tensor_scalar(out, in0, s1, None, op0), reads=rd, writes=[out])
        return self.add(eng, lambda: e.tensor_scalar(out, in0, s1, s2, op0, op1), reads=rd, writes=[out])

    def stt(self, out, in0, scalar, in1, op0, op1, eng="dve"):
        e = self.eobj[eng]
        rd = [in0, in1] + ([] if isinstance(scalar, (int, float)) else [scalar])
        return self.add(eng, lambda: e.scalar_tensor_tensor(out, in0, scalar, in1, op0, op1), reads=rd, writes=[out])

    def copy(self, out, in_, eng="dve"):
        e = self.eobj[eng]
        if eng == "act":
            return self.add(eng, lambda: e.copy(out, in_), reads=[in_], writes=[out])
        return self.add(eng, lambda: e.tensor_copy(out, in_), reads=[in_], writes=[out])

    def memset(self, ap, val, eng="dve"):
        e = self.eobj[eng]
        return self.add(eng, lambda: e.memset(ap, val), reads=[], writes=[ap])

    def recip(self, out, in_):
        nc = self.nc
        return self.add("dve", lambda: nc.vector.reciprocal(out, in_), reads=[in_], writes=[out])

    def dma(self, out, in_, q=None, **kw):
        q = q or self.default_q
        e = self.eobj[q]
        return self.add(q, lambda: e.dma_start(out, in_, **kw), reads=[in_], writes=[out], is_dma=True)

    def flush(self):
        ops = self.ops
        self.ops = []
        ns = self.nslots
        for op in ops:
            for d in op.deps:
                d.signaled = True
        for op in ops:
            if op.is_dma:
                lo, hi = (0, 20) if op.eng == "sp" else (20, ns)
                r_ = self.rrq.get(op.eng, lo)
                op.slot = r_
                self.rrq[op.eng] = lo + (r_ + 1 - lo) % (hi - lo)
                self.slot_cnt[op.slot] += 1
                op.slotn = self.slot_cnt[op.slot]
            elif op.signaled:
                self.cnt[op.eng] += 1
                op.sig = self.cnt[op.eng]
        esem, dsem, slot_last = self.esem, self.dsem, self.slot_last
        for op in ops:
            e = self.eobj[op.eng]
            ke, kd = self.known[op.eng]
            need = op.deps
            if op.is_dma and slot_last[op.slot] is not None:
                need = need + [slot_last[op.slot]]
            for d in need:
                if d.is_dma:
                    if kd[d.slot] >= d.slotn:
                        continue
                    e.wait_ge(dsem[d.slot], 16 * d.slotn)
                else:
                    if ke[d.eng] >= d.sig:
                        continue
                    e.wait_ge(esem[d.eng], d.sig)
                ce, cd = d.clock
                for x in ENGS:
                    if ce[x] > ke[x]:
                        ke[x] = ce[x]
                for i in range(ns):
                    if cd[i] > kd[i]:
                        kd[i] = cd[i]
            inst = op.fn()
            if self.names is not None:
                self.names[inst.ins.name] = op.tag
            if op.is_dma:
                inst.then_inc(dsem[op.slot], 16)
                slot_last[op.slot] = op
                cd2 = list(kd)
                cd2[op.slot] = max(cd2[op.slot], op.slotn)
                op.clock = (dict(ke), cd2)
            elif op.signaled:
                inst.then_inc(esem[op.eng], 1)
                ce2 = dict(ke)
                ce2[op.eng] = max(ce2[op.eng], op.sig)
                op.clock = (ce2, list(kd))
            op.fn = None
            op.deps = None
        self.n_emitted += len(ops)

    def finish(self):
        self.barrier()
        for i in range(self.nslots):
            if self.slot_cnt[i]:
                self.nc.sync.wait_ge(self.dsem[i], 16 * self.slot_cnt[i])
        self.stack.close()


class RR:
    def __init__(self, items):
        self.items = items
        self.i = 0

    def get(self):
        x = self.items[self.i % len(self.items)]
        self.i += 1
        return x


def _rope_tables(rot_dim):
    rows = L // 64
    row = np.repeat(np.arange(rows), 64).astype(np.float32)
    col = np.tile(np.arange(64), rows).astype(np.float32)
    quarter = rot_dim // 4
    inv = (np.float32(10000.0) ** (-np.arange(quarter, dtype=np.float32) / quarter)).astype(np.float32)
    ang = np.concatenate([row[:, None] * inv, col[:, None] * inv], axis=-1).astype(np.float32)
    cos = np.cos(ang).astype(np.float32).T
    sin = np.sin(ang).astype(np.float32).T
    cos2 = np.concatenate([cos, cos], 0)
    sin2 = np.concatenate([-sin, sin], 0)
    return cos2, sin2


_CONST = {}


def _consts():
    if _CONST:
        return _CONST
    cm, sm = _rope_tables(32)
    ropeM = np.zeros((128, 2, L), np.float32)
    ropeM[64:96, 0] = cm
    ropeM[64:96, 1] = sm
    cd, sd = _rope_tables(64)
    ropeD = np.zeros((128, 2, L), np.float32)
    ropeD[0:64, 0] = cd
    ropeD[64:128, 0] = cd
    ropeD[0:64, 1] = sd
    ropeD[64:128, 1] = sd
    k = np.arange(128)[:, None]
    q = np.arange(128)[None, :]
    m_prev = (q <= k).astype(np.float32)
    m_next = (k <= q).astype(np.float32)
    wmask = np.stack([np.tile(m_prev, (1, 4)), np.tile(m_next, (1, 4))], 1).astype(ml_dtypes.bfloat16)
    t = np.arange(L, dtype=np.int64)
    f = np.arange(L, dtype=np.int64)
    m = (t[:, None] * f[None, :]) % (2 * L)
    th = 2.0 * np.pi * m.astype(np.float64) / (2 * L)
    Cm = np.cos(th)
    Sm = np.sin(th)
    alt = np.where(t % 2 == 0, 1.0, -1.0)
    FM = np.concatenate([Cm, -Sm], axis=1)
    FM[:, L] = alt
    N = 2 * L
    IMr = 2.0 * Cm.T / N
    IMr[0, :] = 1.0 / N
    IMi = -2.0 * Sm.T / N
    IMi[0, :] = alt / N
    IM = np.concatenate([IMr, IMi], axis=0)
    tf = np.arange(L, dtype=np.float32)
    tn = tf / np.float32(L - 1)
    bands = np.linspace(1e-4, 15, 16, dtype=np.float32)
    ang = (np.float32(2.0 * math.pi) * bands[None, :] * tf[:, None] / np.float32(L)).astype(np.float32)
    feats = np.concatenate([tn[:, None], np.cos(ang), -np.sin(ang)], axis=-1).astype(np.float32)
    min_decay = math.log(1e-2) / 1.5
    max_decay = math.log(1e-2) / 0.3
    deltas = np.abs(np.linspace(min_decay, max_decay, 512, dtype=np.float32))
    decay = np.exp(-tn[:, None] * deltas[None, :]).astype(np.float32)
    ident = np.eye(128, dtype=np.float32)
    _CONST.update(dict(ropeM=ropeM, ropeD=ropeD, wmask=wmask, FM=FM.astype(ml_dtypes.bfloat16),
                       IM=IM.astype(ml_dtypes.bfloat16), featsT=np.ascontiguousarray(feats.T), decay=decay,
                       ident=ident))
    return _CONST


def _swap_halves(w, block):
    k, n = w.shape
    return np.ascontiguousarray(w.reshape(k, n // block, 2, block // 2)[:, :, ::-1, :].reshape(k, n))


def _colsT(v, ntile):
    return np.ascontiguousarray(np.asarray(v, np.float32).reshape(ntile, 128).T)


def _host_layout(inp, NS, core):
    f = lambda a: np.ascontiguousarray(np.asarray(a, np.float32))
    cst = _consts()
    b0 = core * NS
    m = {}
    m["x"] = f(inp["x"][b0:b0 + NS])
    m["ctx"] = f(inp["ctx"][b0:b0 + NS])
    rows = np.concatenate([f(inp["c"][b0:b0 + NS]), f(inp["c_ctx"])[None, :]], 0)
    m["cT"] = np.ascontiguousarray(rows.reshape(NS + 1, 8, 128).transpose(2, 1, 0))
    m["ada_w"] = f(inp["ada_w"])
    m["ada_bT"] = np.ascontiguousarray(f(inp["ada_b"]).reshape(2, 48, 128).transpose(2, 0, 1))
    m["norm_gT"] = np.ascontiguousarray(f(inp["norm_g"]).reshape(2, 4, 8, 128).transpose(3, 0, 1, 2))
    m["w_out"] = f(inp["mix_w_out"])
    m["w_up"] = f(inp["ffn_w_up"])
    m["w_down"] = f(inp["ffn_w_down"])
    m["fcw"] = np.ascontiguousarray(f(inp["ffn_conv_w"]).reshape(2, 3, 44, 128).transpose(3, 0, 2, 1))
    m["fcb"] = np.ascontiguousarray(f(inp["ffn_conv_b"]).reshape(2, 44, 128).transpose(2, 0, 1))
    we = f(inp["even_w_in"][0])
    cq, ckv, kr, dq, dk, dv = np.split(we, np.cumsum([384, 256, 32, 512, 512])[:], axis=1)
    m["w0in"] = np.ascontiguousarray(np.concatenate(
        [cq, ckv, dq, _swap_halves(dq, 64), dk, _swap_halves(dk, 64), dv, kr, _swap_halves(kr, 32)], 1))
    uq = f(inp["mla_w_uq"][0]).reshape(384, 8, 96)
    uq_rot = _swap_halves(np.ascontiguousarray(uq[:, :, 64:]).reshape(384, 256), 32)
    m["w_uq"] = np.ascontiguousarray(np.concatenate([uq.reshape(384, 768), uq_rot], 1))
    ukv = f(inp["mla_w_ukv"][0]).reshape(256, 8, 128)
    m["w_ukv"] = np.ascontiguousarray(np.concatenate([ukv[:, :, :64].reshape(256, 512), ukv[:, :, 64:].reshape(256, 512)], 1))
    m["qngT"] = _colsT(inp["mla_q_norm_g"][0], 3)
    m["kvngT"] = _colsT(inp["mla_kv_norm_g"][0], 2)
    m["subgT"] = _colsT(inp["diff_subln_g"][0], 1)
    m["dlam"] = f(inp["diff_lambda"][0]).reshape(1, 256)
    wo = f(inp["odd_w_in"][0])
    q, k_, v_, u = np.split(wo, np.cumsum([512, 128, 128]), axis=1)
    m["w1in"] = np.ascontiguousarray(np.concatenate([q, _swap_halves(q, 64), k_, _swap_halves(k_, 64), v_, u], 1))
    m["sink"] = f(inp["win_sink"][0]).reshape(1, 8)
    m["hcw"] = np.ascontiguousarray(f(inp["hy_conv_w"][0]).reshape(3, 12, 128).transpose(2, 1, 0))
    m["hcb"] = _colsT(inp["hy_conv_b"][0], 12)
    m["hbias"] = np.ascontiguousarray(f(inp["hy_bias"][0]).reshape(2, 4, 128).transpose(2, 0, 1))
    m["hw1"] = f(inp["hy_f_w1"][0])
    m["hb1"] = f(inp["hy_f_b1"][0]).reshape(64, 1)
    m["hw2"] = f(inp["hy_f_w2"][0])
    m["hb2"] = f(inp["hy_f_b2"][0]).reshape(64, 1)
    m["hw3"] = f(inp["hy_f_w3"][0])
    m["hb3"] = f(inp["hy_f_b3"][0]).reshape(1, 2048)
    m["hfreq"] = f(inp["hy_f_freq"][0]).reshape(64, 1)
    for kk in ("ropeM", "ropeD", "wmask", "FM", "IM", "featsT", "decay", "ident"):
        m[kk] = cst[kk]
    return m


def build_program(NS, dbg=None, stop=None, names=None):
    R = NS + 1
    nc = bass.Bass("TRN2", target_bir_lowering=False)
    S = Sched(nc)
    S.names = names
    uid = [0]

    def din(name, shape, dt=F32):
        return nc.dram_tensor(name, list(shape), dt, kind="ExternalInput").ap()

    def dscr(name, shape, dt):
        return nc.dram_tensor(name, list(shape), dt, kind="Internal").ap()

    I = {}
    I["x"] = din("x", [NS, L, D])
    I["ctx"] = din("ctx", [NS, C, D])
    I["cT"] = din("cT", [128, 8, R])
    I["ada_w"] = din("ada_w", [2, D, 6 * D])
    I["ada_bT"] = din("ada_bT", [128, 2, 48])
    I["norm_gT"] = din("norm_gT", [128, 2, 4, 8])
    I["w_out"] = din("w_out", [2, D, D])
    I["w_up"] = din("w_up", [2, D, 2 * DFF])
    I["w_down"] = din("w_down", [2, DFF, D])
    I["fcw"] = din("fcw", [128, 2, 44, 3])
    I["fcb"] = din("fcb", [128, 2, 44])
    I["w0in"] = din("w0in", [D, 3264])
    I["w_uq"] = din("w_uq", [384, 1024])
    I["w_ukv"] = din("w_ukv", [256, 1024])
    I["qngT"] = din("qngT", [128, 3])
    I["kvngT"] = din("kvngT", [128, 2])
    I["subgT"] = din("subgT", [128, 1])
    I["dlam"] = din("dlam", [1, 256])
    I["w1in"] = din("w1in", [D, 2944])
    I["sink"] = din("sink", [1, 8])
    I["hcw"] = din("hcw", [128, 12, 3])
    I["hcb"] = din("hcb", [128, 12])
    I["hbias"] = din("hbias", [128, 2, 4])
    I["hw1"] = din("hw1", [33, 64])
    I["hb1"] = din("hb1", [64, 1])
    I["hw2"] = din("hw2", [64, 64])
    I["hb2"] = din("hb2", [64, 1])
    I["hw3"] = din("hw3", [64, 2048])
    I["hb3"] = din("hb3", [1, 2048])
    I["hfreq"] = din("hfreq", [64, 1])
    I["ropeM"] = din("ropeM", [128, 2, L])
    I["ropeD"] = din("ropeD", [128, 2, L])
    I["wmask"] = din("wmask", [128, 2, 512], BF)
    I["FM"] = din("FM", [L, 2 * L], BF)
    I["IM"] = din("IM", [2 * L, L], BF)
    I["featsT"] = din("featsT", [33, L])
    I["decay"] = din("decay", [L, 512])
    I["ident"] = din("ident", [128, 128])
    OUT = nc.dram_tensor("y", [NS, L, D], F32, kind="ExternalOutput").ap()
    DBG = {}
    if dbg:
        for name, shape in dbg.items():
            DBG[name] = nc.dram_tensor("dbg_" + name, list(shape), F32, kind="ExternalOutput").ap()

    XD = dscr("XD", [NS, NT, D], F32)
    GSC = dscr("GSC", [2, R, 2, D], F32)
    SPEC = dscr("SPEC", [2, 3, L, 512], F32)
    X12 = dscr("X12", [2, 512, L], F32)
    WB = {}
    for nm, shp in (("w0in", [D, 3264]), ("w_uq", [384, 1024]), ("w_ukv", [256, 1024]), ("w1in", [D, 2944]),
                    ("w_out", [2 * D, D]), ("w_up", [2 * D, 2 * DFF]), ("w_down", [2 * DFF, D])):
        WB[nm] = dscr("wb_" + nm, shp, BF)

    es = contextlib.ExitStack()

    pmode = {"engs": None, "keep": ()}

    @contextlib.contextmanager
    def phase():
        st_ = contextlib.ExitStack()
        try:
            yield st_
            S.barrier(pmode["engs"], pmode["keep"])
        finally:
            st_.close()


    live = [0, 0]
    SB_LIMIT = 229344 - 16481 - 4096

    def sb(shape, dt, name=None, stack=None):
        uid[0] += 1
        nb = 1
        for d_ in shape[1:]:
            nb *= int(d_)
        nb *= 2 if dt == BF else 4
        nb = (nb + 31) // 32 * 32
        live[0] += nb
        live[1] = max(live[1], live[0])
        assert live[0] <= SB_LIMIT, ("SBUF over budget", name, live[0])
        stk = stack or es

        def _rel():
            live[0] -= nb
        stk.callback(_rel)
        return stk.enter_context(nc.sbuf_tensor("%s_%d" % (name or "t", uid[0]), list(shape), dt))

    with es:
        PS = [es.enter_context(nc.psum_tensor("ps%d" % i, [128, 512], F32)) for i in range(8)]
        ident = sb([128, 128], F32, "ident")
        ones_bf = sb([128, 128], BF, "ones")
        modA = sb([128, 2, 4, R, 8], F32, "modA")
        fcw = sb([128, 2, 44, 3], F32, "fcw")
        fcb = sb([128, 2, 44], F32, "fcb")
        qng = sb([128, 3], F32, "qng")
        kvng = sb([128, 2], F32, "kvng")
        subg = sb([128, 1], F32, "subg")
        nlam = sb([128, 1], F32, "nlam")
        esink = sb([128, 8], F32, "esink")
        hcw = sb([128, 12, 3], F32, "hcw")
        hcb = sb([128, 12], F32, "hcb")
        hbias = sb([128, 2, 4], F32, "hbias")
        small = sb([128, 64], F32, "small")
        smallrr = [0]

        def scol():
            smallrr[0] = (smallrr[0] + 1) % 64
            return small[:, smallrr[0]:smallrr[0] + 1]

        zeros512 = sb([128, 512], F32, "zeros")
        S.memset(zeros512[:], 0.0)
        S.dma(ident[:], I["ident"])
        S.memset(ones_bf[:], 1.0)
        S.dma(fcw[:], I["fcw"])
        S.dma(fcb[:], I["fcb"])
        S.dma(qng[:], I["qngT"])
        S.dma(kvng[:], I["kvngT"])
        S.dma(subg[:], I["subgT"])
        S.dma(hcw[:], I["hcw"])
        S.dma(hcb[:], I["hcb"])
        S.dma(hbias[:], I["hbias"])

        def rstd_from(ss, n_feat, np_=128):
            a = scol()
            S.act(a[0:np_], ss, AF.Sqrt, bias=EPS, scale=1.0 / n_feat)
            b = scol()
            S.recip(b[0:np_], a[0:np_])
            return b

        def prologue_cast(ps_):
            S.tag = "cast"
            CH = 2048
            fbuf = RR([sb([128, CH], F32, "cf", ps_) for _ in range(4)])
            bbuf = RR([sb([128, CH], BF, "cb", ps_) for _ in range(4)])
            srcs = [("w0in", I["w0in"]), ("w_uq", I["w_uq"]), ("w_ukv", I["w_ukv"]), ("w1in", I["w1in"]),
                    ("w_out", I["w_out"].rearrange("l k n -> (l k) n")),
                    ("w_up", I["w_up"].rearrange("l k n -> (l k) n")),
                    ("w_down", I["w_down"].rearrange("l k n -> (l k) n"))]
            for nm, src in srcs:
                rows, cols = src.shape
                per = rows * cols // 128
                sflat = src.rearrange("(p a) n -> p (a n)", p=128)
                dflat = WB[nm].rearrange("(p a) n -> p (a n)", p=128)
                o = 0
                while o < per:
                    n = min(CH, per - o)
                    fb = fbuf.get()
                    bb = bbuf.get()
                    S.dma(fb[:, 0:n], sflat[:, o:o + n], q="sp")
                    S.copy(bb[:, 0:n], fb[:, 0:n], eng="pool")
                    S.dma(dflat[:, o:o + n], bb[:, 0:n], q="sp")
                    o += n

        def prologue_mod():
            S.tag = "mod"
            with phase() as ps_:
                scT = sb([128, 8, R], F32, "scT", ps_)
                gT = sb([128, 2, 4, 8], F32, "gT", ps_)
                abT = sb([128, 2, 48], F32, "abT", ps_)
                modT = sb([128, 48, R], F32, "modT", ps_)
                PG = sb([128, 2, 2, R, 8], F32, "PG", ps_)
                rowt = sb([8, 128], F32, "rowt", ps_)
                S.dma(scT[:], I["cT"])
                S.act(scT[:], scT[:], AF.Silu)
                S.dma(gT[:], I["norm_gT"])
                S.dma(abT[:], I["ada_bT"])
                awb = RR([sb([128, 8, 512], F32, "aw", ps_) for _ in range(2)])
                for l in range(2):
                    for cc in range(12):
                        aw = awb.get()
                        S.dma(aw[:], I["ada_w"][l, :, cc * 512:(cc + 1) * 512].rearrange("(k p) n -> p k n", p=128))
                        for ct in range(4):
                            j = cc * 4 + ct
                            ps = PS[j % 2]
                            S.mmacc(ps[:, 0:R], [(aw[:, k, ct * 128:(ct + 1) * 128], scT[:, k, :]) for k in range(8)])
                            S.ts(modT[:, j, :], ps[:, 0:R], abT[:, l, j:j + 1], None, ALU.add)
                    for r in range(R):
                        def m_(i):
                            return modT[:, i * 8:(i + 1) * 8, r]
                        S.stt(modA[:, l, 0, r, :], m_(1), 1.0, gT[:, l, 0, :], ALU.add, ALU.mult)
                        S.copy(modA[:, l, 1, r, :], m_(0))
                        S.stt(modA[:, l, 2, r, :], m_(4), 1.0, gT[:, l, 2, :], ALU.add, ALU.mult)
                        S.copy(modA[:, l, 3, r, :], m_(3))
                        S.tt(PG[:, l, 0, r, :], m_(2), gT[:, l, 1, :], ALU.mult)
                        S.tt(PG[:, l, 1, r, :], m_(5), gT[:, l, 3, :], ALU.mult)
                        for w_ in range(2):
                            ps = PS[2 + (w_ % 2)]
                            S.transpose(ps[0:8, 0:128], PG[:, l, w_, r, :], ident[:])
                            S.copy(rowt[:], ps[0:8, 0:128])
                            S.dma(GSC[l, r, w_, :].rearrange("(k p) -> k p", p=128), rowt[:])
                dl = sb([128, 256], F32, "dl", ps_)
                S.dma(dl[:], I["dlam"].partition_broadcast(128))
                pr = sb([128, 128], F32, "pr", ps_)
                S.tt(pr[:, 0:64], dl[:, 0:64], dl[:, 64:128], ALU.mult)
                S.tt(pr[:, 64:128], dl[:, 128:192], dl[:, 192:256], ALU.mult)
                s1 = scol(); s2 = scol()
                S.act(pr[:, 0:64], pr[:, 0:64], AF.Identity, accum_out=s1)
                S.act(pr[:, 64:128], pr[:, 64:128], AF.Identity, accum_out=s2)
                e1 = scol(); e2 = scol()
                S.act(e1, s1, AF.Exp)
                S.act(e2, s2, AF.Exp)
                S.tt(e2, e2, e1, ALU.subtract)
                S.ts(nlam[:], e2, -0.2, None, ALU.add)
                S.ts(subg[:], subg[:], 0.8, None, ALU.mult)
                sk = sb([128, 8], F32, "sk", ps_)
                S.dma(sk[:], I["sink"].partition_broadcast(128))
                S.act(esink[:], sk[:], AF.Exp)

        def sin_safe(out, pre, np_, n, tmp1, tmp2):
            S.act(tmp1, pre, AF.Sin, scale=0.25)
            S.act(tmp2, pre, AF.Sin, scale=0.25, bias=halfpi[0:np_, :])
            S.tt(tmp2, tmp1, tmp2, ALU.mult)
            S.tt(tmp1, tmp1, tmp1, ALU.mult)
            S.ts(tmp1, tmp1, -8.0, 4.0, ALU.mult, ALU.add)
            S.tt(out, tmp1, tmp2, ALU.mult)

        halfpi = sb([128, 1], F32, "halfpi")
        S.memset(halfpi[:], math.pi / 2)

        def prologue_hyena():
            S.tag = "hyp"
            with phase() as ps_:
                ft = sb([33, L], F32, "ft", ps_)
                w1 = sb([33, 64], F32, "w1", ps_)
                w2 = sb([64, 64], F32, "w2", ps_)
                w3 = sb([65, 2048], F32, "w3", ps_)
                b1 = sb([64, 1], F32, "b1", ps_)
                b2 = sb([64, 1], F32, "b2", ps_)
                fr = sb([64, 1], F32, "fr", ps_)
                h1 = sb([64, L], F32, "h1", ps_)
                h2 = sb([65, L], F32, "h2", ps_)
                t1 = sb([64, 512], F32, "t1", ps_)
                t2 = sb([64, 512], F32, "t2", ps_)
                t3 = sb([64, 512], F32, "t3", ps_)
                S.dma(ft[:], I["featsT"])
                S.dma(w1[:], I["hw1"])
                S.dma(w2[:], I["hw2"])
                S.dma(w3[0:64, :], I["hw3"])
                S.dma(w3[64:65, :], I["hb3"])
                S.dma(b1[:], I["hb1"])
                S.dma(b2[:], I["hb2"])
                S.dma(fr[:], I["hfreq"])
                S.tt(b1[:], b1[:], fr[:], ALU.mult)
                S.tt(b2[:], b2[:], fr[:], ALU.mult)
                S.memset(h2[64:65, :], 1.0)
                for c in range(4):
                    cs = slice(c * 512, (c + 1) * 512)
                    ps = PS[c % 2]
                    S.mm(ps[0:64, :], w1[:, :], ft[:, cs])
                    S.ts(t3[:], ps[0:64, :], fr[:, 0:1], b1[:, 0:1], ALU.mult, ALU.add)
                    sin_safe(h1[:, cs], t3[:], 64, 512, t1[:], t2[:])
                for c in range(4):
                    cs = slice(c * 512, (c + 1) * 512)
                    ps = PS[2 + c % 2]
                    S.mm(ps[0:64, :], w2[:, :], h1[:, cs])
                    S.ts(t3[:], ps[0:64, :], fr[:, 0:1], b2[:, 0:1], ALU.mult, ALU.add)
                    sin_safe(h2[0:64, cs], t3[:], 64, 512, t1[:], t2[:])
                decb = RR([sb([128, 512], F32, "dec", ps_) for _ in range(3)])
                Pm = [sb([128, 16, 512], BF, "Pm", ps_) for _ in range(2)]
                Mm = [sb([128, 16, 512], BF, "Mm", ps_) for _ in range(2)]
                hfb = RR([sb([128, 512], F32, "hf", ps_) for _ in range(2)])
                hbb = RR([sb([128, 512], F32, "hb", ps_) for _ in range(2)])
                for n in range(2):
                    for a in range(16):
                        psf = PS[4 + 2 * (a % 2)]
                        psb = PS[5 + 2 * (a % 2)]
                        hf = hfb.get(); hb = hbb.get()
                        dec_a = decb.get()
                        S.dma(dec_a[:], I["decay"][a * 128:(a + 1) * 128, :])
                        S.mm(psf[:, :], h2[:, a * 128:(a + 1) * 128], w3[:, n * 512:(n + 1) * 512])
                        S.mm(psb[:, :], h2[:, a * 128:(a + 1) * 128], w3[:, 1024 + n * 512:1024 + (n + 1) * 512])
                        S.tt(hf[:], psf[:, :], dec_a[:], ALU.mult)
                        S.tt(hb[:], psb[:, :], dec_a[:], ALU.mult)
                        if a == 0:
                            S.memset(hb[0:1, :], 0.0)
                        S.tt(Pm[n][:, a, :], hf[:], hb[:], ALU.add)
                        S.tt(Mm[n][:, a, :], hf[:], hb[:], ALU.subtract)
                fmb = RR([sb([128, 16, 128], BF, "fm", ps_) for _ in range(3)])
                ob = RR([sb([128, 512], F32, "so", ps_) for _ in range(3)])
                FMv = I["FM"].rearrange("(a p) f -> p a f", p=128)
                for n in range(2):
                    for ftile in range(32):
                        fm = fmb.get()
                        S.dma(fm[:], FMv[:, :, ftile * 128:(ftile + 1) * 128])
                        src = Pm[n] if ftile < 16 else Mm[n]
                        ps = PS[6 + ftile % 2]
                        S.mmacc(ps[:, :], [(fm[:, a, :], src[:, a, :]) for a in range(16)])
                        o = ob.get()
                        S.copy(o[:], ps[:, :], eng="act")
                        if ftile < 16:
                            S.dma(SPEC[n, 0, ftile * 128:(ftile + 1) * 128, :], o[:])
                            if ftile > 0:
                                S.dma(SPEC[n, 2, ftile * 128:(ftile + 1) * 128, :], o[:])
                            else:
                                S.dma(SPEC[n, 2, 1:128, :], o[1:128, :])
                        else:
                            if ftile == 16:
                                ps2 = PS[4]
                                S.mmacc(ps2[:, :], [(fm[:, a, :], Pm[n][:, a, :]) for a in range(16)])
                                o2 = ob.get()
                                S.copy(o2[0:1, :], ps2[0:1, :], eng="act")
                                S.dma(SPEC[n, 2, 0:1, :], o2[0:1, :])
                                S.memset(o[0:1, :], 0.0)
                            S.dma(SPEC[n, 1, (ftile - 16) * 128:(ftile - 15) * 128, :], o[:])

        def xtile_src(layer, s, tt):
            if layer == 0:
                if tt < 2:
                    return I["ctx"][s, tt * 128:(tt + 1) * 128, :]
                return I["x"][s, (tt - 2) * 128:(tt - 1) * 128, :]
            return XD[s, tt * 128:(tt + 1) * 128, :]

        def norm_transpose(src_fn, tts, acol, bcol_, HT, stack, tok_base=0):
            S.tag = "norm"
            xb = RR([sb([128, D], F32, "xb", stack) for _ in range(6)])
            xnb = RR([sb([128, D], F32, "xn", stack) for _ in range(8)])
            junk = sb([128, D], BF, "junk", stack)
            psr = RR([PS[0], PS[1], PS[2], PS[3]])
            for g0 in range(0, len(tts), 4):
                grp = tts[g0:g0 + 4]
                xns = []
                for tt in grp:
                    x = xb.get()
                    S.dma(x[:], src_fn(tt))
                    ss = scol()
                    S.act(junk[:], x[:], AF.Square, accum_out=ss)
                    r = rstd_from(ss, D)
                    xn = xnb.get()
                    S.ts(xn[:], x[:], r, None, ALU.mult)
                    xns.append(xn)
                for k in range(8):
                    ps = psr.get()
                    for i, xn in enumerate(xns):
                        S.transpose(ps[:, i * 128:(i + 1) * 128], xn[:, k * 128:(k + 1) * 128], ident[:])
                    i = 0
                    while i < len(grp):
                        j = i
                        while j + 1 < len(grp) and (grp[j + 1] < 2) == (grp[i] < 2):
                            j += 1
                        t0 = grp[i] * 128 - tok_base
                        if k % 2 == 0:
                            S.act(HT[:, k, t0:t0 + (j - i + 1) * 128], ps[:, i * 128:(j + 1) * 128], AF.Identity,
                                  scale=acol(grp[i])[:, k:k + 1], bias=bcol_(grp[i])[:, k:k + 1])
                        else:
                            S.ts(HT[:, k, t0:t0 + (j - i + 1) * 128], ps[:, i * 128:(j + 1) * 128],
                                 acol(grp[i])[:, k:k + 1], bcol_(grp[i])[:, k:k + 1], ALU.mult, ALU.add)
                        i = j + 1

        def load_w(dst, wb_ap, r0, nk, c0, ncols, q="sp"):
            S.dma(dst, wb_ap[r0:r0 + nk * 128, c0:c0 + ncols].rearrange("(k p) n -> p k n", p=128), q=q)

        CHUNKS0 = [(0, 256)] + [(256 + 512 * i, 512) for i in range(4)]

        def rope_apply(dst, ps, psr, tab, prange, tok0, n, tmpa, tmpb):
            p0, p1 = prange
            S.tt(tmpa[p0:p1, 0:n], ps[p0:p1, 0:n], tab[p0:p1, 0, tok0:tok0 + n], ALU.mult)
            S.tt(tmpb[p0:p1, 0:n], psr[p0:p1, 0:n], tab[p0:p1, 1, tok0:tok0 + n], ALU.mult)
            S.tt(dst, tmpa[p0:p1, 0:n], tmpb[p0:p1, 0:n], ALU.add, eng="pool")

        def residual_update(s, layer, which, tt, psy, xsrc, gtile, dst, stack_bufs):
            xb, tb, junk = stack_bufs
            s1 = scol(); s2 = scol()
            S.act(junk[:, 0:512], psy[0][:, :], AF.Square, accum_out=s1)
            S.act(junk[:, 512:1024], psy[1][:, :], AF.Square, accum_out=s2)
            S.tt(s1, s1, s2, ALU.add)
            r = rstd_from(s1, D)
            x = xsrc
            t = tb.get()
            for h in range(2):
                hs = slice(h * 512, (h + 1) * 512)
                S.stt(t[:, hs], psy[h][:, :], r, gtile[:, hs], ALU.mult, ALU.mult)
                S.tt(t[:, hs], t[:, hs], x[:, hs], ALU.add, eng="pool")
            S.dma(dst, t[:])

        def mixer0(s, HT, OT, stack):
            r_l, r_c = s, NS
            with phase() as st:
                CQN = sb([128, 3, NT], BF, "CQN", st)
                CKVN = sb([128, 2, NT], BF, "CKVN", st)
                KRT = sb([128, NT], BF, "KRT", st)
                ropeM = sb([128, 2, L], F32, "ropeM", st)
                S.dma(ropeM[64:96], I["ropeM"][64:96])
                tmpa = sb([128, 512], F32, "tmpa", st)
                tmpb = sb([128, 512], F32, "tmpb", st)
                S.tag = "lat"
                with phase() as st2:
                    wl = sb([128, 8, 640], BF, "wl", st2)
                    wkr = sb([128, 8, 64], BF, "wkr", st2)
                    load_w(wl[:], WB["w0in"], 0, 8, 0, 640)
                    load_w(wkr[:], WB["w0in"], 0, 8, 3200, 64)
                    cf = [sb([128, 512], F32, "cf", st2) for _ in range(5)]
                    sq = [sb([128, 512], BF, "sq", st2) for _ in range(5)]
                    rs = sb([128, 512], F32, "rs", st2)
                    for (c0, n) in CHUNKS0:
                        for (base, nt_, gcol, dst, nf) in ((0, 3, qng, CQN, 384), (3, 2, kvng, CKVN, 256)):
                            for ct in range(nt_):
                                ps = PS[ct]
                                S.mmacc(ps[:, 0:n], [(wl[:, k, (base + ct) * 128:(base + ct + 1) * 128], HT[:, k, c0:c0 + n]) for k in range(8)])
                                S.copy(cf[base + ct][:, 0:n], ps[:, 0:n], eng="act")
                                S.tt(sq[base + ct][:, 0:n], cf[base + ct][:, 0:n], cf[base + ct][:, 0:n], ALU.mult, eng="pool")
                            pss = PS[3]
                            S.mmacc(pss[:, 0:n], [(ones_bf[:, :], sq[base + ct][:, 0:n]) for ct in range(nt_)])
                            S.act(rs[:, 0:n], pss[:, 0:n], AF.Sqrt, bias=EPS, scale=1.0 / nf)
                            S.recip(rs[:, 0:n], rs[:, 0:n])
                            for ct in range(nt_):
                                S.stt(dst[:, ct, c0:c0 + n], cf[base + ct][:, 0:n], gcol[:, ct:ct + 1], rs[:, 0:n], ALU.mult, ALU.mult)
                        pk = PS[4]; pkr = PS[5]
                        S.mmacc(pk[64:96, 0:n], [(wkr[:, k, 0:32], HT[:, k, c0:c0 + n]) for k in range(8)])
                        if c0 == 0:
                            S.copy(KRT[64:96, 0:n], pk[64:96, 0:n], eng="act")
                        else:
                            S.mmacc(pkr[64:96, 0:n], [(wkr[:, k, 32:64], HT[:, k, c0:c0 + n]) for k in range(8)])
                            rope_apply(KRT[64:96, c0:c0 + n], pk, pkr, ropeM, (64, 96), c0 - C, n, tmpa, tmpb)
                S.tag = "mla"
                with phase() as st2:
                    wuq = sb([128, 3, 1024], BF, "wuq", st2)
                    wukv = sb([128, 2, 1024], BF, "wukv", st2)
                    load_w(wuq[:], WB["w_uq"], 0, 3, 0, 1024)
                    load_w(wukv[:], WB["w_ukv"], 0, 2, 0, 1024)
                    QHb = RR([sb([128, NT], BF, "QH", st2) for _ in range(2)])
                    KHb = RR([sb([128, NT], BF, "KH", st2) for _ in range(2)])
                    VHb = [sb([128, 18, 128], BF, "VH", st2) for _ in range(2)]
                    for v in VHb:
                        S.memset(v[:, :, 64:128], 1.0)
                    VHr = RR(VHb)
                    PTb = RR([sb([128, 512], BF, "PT", st2) for _ in range(6)])
                    dtmp = RR([sb([64, 512], F32, "dtmp", st2) for _ in range(2)])
                    scale = 96 ** -0.5
                    hb_ = {}

                    def mla_proj(h):
                        QH = QHb.get(); KH = KHb.get(); VH = VHr.get()
                        hb_[h] = (QH, KH, VH)
                        for (c0, n) in CHUNKS0:
                            pq = PS[0]; pqr = PS[1]; pk = PS[2]
                            S.mmacc(pq[0:96, 0:n], [(wuq[:, k, h * 96:(h + 1) * 96], CQN[:, k, c0:c0 + n]) for k in range(3)])
                            S.mmacc(pk[0:64, 0:n], [(wukv[:, k, h * 64:(h + 1) * 64], CKVN[:, k, c0:c0 + n]) for k in range(2)])
                            S.copy(KH[0:64, c0:c0 + n], pk[0:64, 0:n], eng="act")
                            if c0 == 0:
                                S.copy(QH[0:96, 0:n], pq[0:96, 0:n], eng="act")
                            else:
                                S.mmacc(pqr[64:96, 0:n], [(wuq[:, k, 768 + h * 32:768 + (h + 1) * 32], CQN[:, k, c0:c0 + n]) for k in range(3)])
                                S.copy(QH[0:64, c0:c0 + n], pq[0:64, 0:n], eng="act")
                                rope_apply(QH[64:96, c0:c0 + n], pq, pqr, ropeM, (64, 96), c0 - C, n, tmpa, tmpb)
                        S.copy(KH[64:96, :], KRT[64:96, :], eng="pool")
                        for t8 in range(0, 18, 8):
                            nn = min(8, 18 - t8)
                            psv = PS[1]
                            for i in range(nn):
                                tt = t8 + i
                                S.mmacc(psv[:, i * 64:(i + 1) * 64], [(CKVN[:, k, tt * 128:(tt + 1) * 128], wukv[:, k, 512 + h * 64:512 + (h + 1) * 64]) for k in range(2)])
                            S.copy(VH[:, t8:t8 + nn, 0:64], psv[:, 0:nn * 64].rearrange("p (a b) -> p a b", b=64))

                    def mla_attn(h):
                        QH, KH, VH = hb_.pop(h)
                        units = []
                        for ci, (c0, n) in enumerate(CHUNKS0):
                            nk = 2 if c0 == 0 else 18
                            for kt in range(nk):
                                units.append((ci, c0, n, kt, nk))
                        psS = RR([PS[3], PS[4], PS[5]])
                        psO = RR([PS[6], PS[7]])
                        cur_o = {}
                        pend = []

                        def do_pv(u, pt):
                            ci, c0, n, kt, nk = u
                            if kt == 0:
                                cur_o[ci] = psO.get()
                            po = cur_o[ci]
                            S.mm(po[:, 0:n], VH[:, kt, :], pt[:, 0:n], start=(kt == 0), stop=(kt == nk - 1))
                            if kt == nk - 1:
                                dt = dtmp.get()
                                S.copy(dt[0:64, 0:n], po[64:128, 0:n], eng="act")
                                S.recip(dt[0:64, 0:n], dt[0:64, 0:n])
                                S.tt(OT[(h % 2) * 64:(h % 2) * 64 + 64, h // 2, c0:c0 + n], po[0:64, 0:n], dt[0:64, 0:n], ALU.mult)

                        for u in units:
                            ci, c0, n, kt, nk = u
                            pss = psS.get()
                            S.mm(pss[:, 0:n], KH[0:96, kt * 128:(kt + 1) * 128], QH[0:96, c0:c0 + n])
                            pt = PTb.get()
                            S.act(pt[:, 0:n], pss[:, 0:n], AF.Exp, scale=scale)
                            pend.append((u, pt))
                            if len(pend) > 2:
                                do_pv(*pend.pop(0))
                        while pend:
                            do_pv(*pend.pop(0))
                    mla_proj(0)
                    for h in range(8):
                        if h + 1 < 8:
                            mla_proj(h + 1)
                        mla_attn(h)
            S.tag = "diff"
            with phase() as st2:
                tmpa = sb([128, 512], F32, "tmpa", st2)
                tmpb = sb([128, 512], F32, "tmpb", st2)
                ropeD = sb([128, 2, L], F32, "ropeD", st2)
                S.dma(ropeD[:], I["ropeD"])
                wdb = RR([sb([128, 8, 5, 128], BF, "wd", st2) for _ in range(2)])
                DQb = RR([sb([128, NT], BF, "DQ", st2) for _ in range(2)])
                DKb = RR([sb([128, NT], BF, "DK", st2) for _ in range(2)])
                DKzb = RR([[sb([128, NT], BF, "DKz", st2) for _ in range(2)] for _ in range(2)])
                for pair_ in DKzb.items:
                    S.memset(pair_[0][64:128, :], 0.0, eng="pool")
                    S.memset(pair_[1][0:64, :], 0.0, eng="pool")
                DVb = RR([sb([128, 18, 128], BF, "DV", st2) for _ in range(2)])
                PTb = RR([sb([128, 512], BF, "PT", st2) for _ in range(6)])
                esets = [(sb([128, 512], F32, "a1", st2), sb([128, 512], F32, "a2", st2), sb([128, 512], F32, "r1", st2),
                          sb([128, 512], F32, "r2", st2), sb([128, 512], BF, "sqd", st2)) for _ in range(2)]
                scale = 64 ** -0.5
                dh_ = {}

                def diff_proj(h):
                    wd = wdb.get()
                    for i, cb in enumerate((640, 1152, 1664, 2176, 2688)):
                        load_w(wd[:, :, i, :], WB["w0in"], 0, 8, cb + h * 128, 128)
                    DQ = DQb.get(); DK = DKb.get(); DV = DVb.get(); DKz = DKzb.get()
                    for (c0, n) in CHUNKS0:
                        for (dst, wi) in ((DQ, 0), (DK, 2)):
                            pa = PS[0]; pb = PS[1]
                            S.mmacc(pa[:, 0:n], [(wd[:, k, wi, :], HT[:, k, c0:c0 + n]) for k in range(8)])
                            if c0 == 0:
                                S.copy(dst[:, 0:n], pa[:, 0:n], eng="act")
                            else:
                                S.mmacc(pb[:, 0:n], [(wd[:, k, wi + 1, :], HT[:, k, c0:c0 + n]) for k in range(8)])
                                rope_apply(dst[:, c0:c0 + n], pa, pb, ropeD, (0, 128), c0 - C, n, tmpa, tmpb)
                    for t4 in range(0, 18, 4):
                        ps = PS[1]
                        nn = min(4, 18 - t4)
                        for i in range(nn):
                            tt = t4 + i
                            S.mmacc(ps[:, i * 128:(i + 1) * 128], [(HT[:, k, tt * 128:(tt + 1) * 128], wd[:, k, 4, :]) for k in range(8)])
                        S.copy(DV[:, t4:t4 + nn, :], ps[:, 0:nn * 128].rearrange("p (a b) -> p a b", b=128), eng="act")
                    S.copy(DKz[0][0:64, :], DK[0:64, :], eng="pool")
                    S.copy(DKz[1][64:128, :], DK[64:128, :], eng="pool")
                    dh_[h] = (DQ, DV, DKz)

                def diff_attn(h):
                    DQ, DV, DKz = dh_.pop(h)
                    deferred = []
                    for ci, (c0, n) in enumerate(CHUNKS0):
                        nk = 2 if c0 == 0 else 18
                        psSj = [PS[3], PS[4]]
                        pos = [(PS[5], PS[6]), (PS[7], PS[2])]
                        a1, a2, r1, r2, sqd = esets[ci % 2]
                        acc = [a1, a2]
                        pend = []

                        def do_pv(kt, j, pt, n=n, nk=nk):
                            po, pd = pos[j]
                            S.mm(po[:, 0:n], DV[:, kt, :], pt[:, 0:n], start=(kt == 0), stop=(kt == nk - 1))
                            S.mm(pd[:, 0:n], ones_bf[:, :], pt[:, 0:n], start=(kt == 0), stop=(kt == nk - 1))

                        for kt in range(nk):
                            for j in range(2):
                                pss = psSj[j]
                                S.mm(pss[:, 0:n], DKz[j][:, kt * 128:(kt + 1) * 128], DQ[:, c0:c0 + n])
                                pt = PTb.get()
                                S.act(pt[:, 0:n], pss[:, 0:n], AF.Exp, scale=scale)
                                pend.append((kt, j, pt))
                            while len(pend) > 2:
                                do_pv(*pend.pop(0))
                            if kt == min(3, nk - 1) and deferred:
                                deferred.pop(0)()
                        while pend:
                            do_pv(*pend.pop(0))
                        for j in range(2):
                            po, pd = pos[j]
                            rr_ = r1 if j == 0 else r2
                            S.recip(rr_[:, 0:n], pd[:, 0:n])
                            S.tt(acc[j][:, 0:n], po[:, 0:n], rr_[:, 0:n], ALU.mult)
                        S.stt(a1[:, 0:n], a2[:, 0:n], nlam[:, 0:1], a1[:, 0:n], ALU.mult, ALU.add)
                        S.tt(sqd[:, 0:n], a1[:, 0:n], a1[:, 0:n], ALU.mult, eng="pool")

                        def fin(a1=a1, r1=r1, sqd=sqd, c0=c0, n=n):
                            pss = PS[0]
                            S.mm(pss[:, 0:n], ones_bf[:, :], sqd[:, 0:n])
                            S.act(r1[:, 0:n], pss[:, 0:n], AF.Sqrt, bias=EPS, scale=1.0 / 128)
                            S.recip(r1[:, 0:n], r1[:, 0:n])
                            S.stt(OT[:, 4 + h, c0:c0 + n], a1[:, 0:n], subg[:, 0:1], r1[:, 0:n], ALU.mult, ALU.mult)
                        deferred.append(fin)
                    while deferred:
                        deferred.pop(0)()

                diff_proj(0)
                for h in range(4):
                    if h + 1 < 4:
                        diff_proj(h + 1)
                    diff_attn(h)

        def wout_residual(s, layer, OT, tts, tok_base):
            S.tag = "wout"
            with phase() as st:
                wo = sb([128, 8, D], BF, "wo", st)
                load_w(wo[:], WB["w_out"], layer * D, 8, 0, D)
                G = {}
                for which, r in (("l", s), ("c", NS)):
                    if which == "c" and layer == 1:
                        continue
                    G[which] = sb([128, D], F32, "G", st)
                    S.dma(G[which][:], GSC[layer, r, 0:1, :].partition_broadcast(128))
                bufs = (RR([sb([128, D], F32, "xr", st) for _ in range(4)]), RR([sb([128, D], F32, "tr", st) for _ in range(3)]),
                        sb([128, D], BF, "junk", st))
                psr = RR([(PS[0], PS[1]), (PS[2], PS[3]), (PS[4], PS[5])])
                xq = []
                PF = 2

                def pref(i_):
                    if i_ < len(tts):
                        x_ = bufs[0].get()
                        S.dma(x_[:], xtile_src(layer, s, tts[i_]))
                        xq.append(x_)
                for i_ in range(PF):
                    pref(i_)
                for i_, tt in enumerate(tts):
                    pref(i_ + PF)
                    psy = psr.get()
                    t0 = tt * 128 - tok_base
                    for h in range(2):
                        S.mmacc(psy[h][:, :], [(OT[:, k, t0:t0 + 128], wo[:, k, h * 512:(h + 1) * 512]) for k in range(8)])
                    residual_update(s, layer, 0, tt, psy, xq.pop(0), G["c" if tt < 2 else "l"],
                                    XD[s, tt * 128:(tt + 1) * 128, :], bufs)

        def ffn(s, layer, last, chk=lambda n: None):
            tts = list(range(18)) if layer == 0 else list(range(2, 18))
            tok_base = 0 if layer == 0 else C
            ntok = NT - tok_base
            with phase() as st:
                HT = sb([128, 8, ntok], BF, "HT2", st)
                with phase() as st2:
                    norm_transpose(lambda tt: XD[s, tt * 128:(tt + 1) * 128, :], tts,
                                   lambda tt: modA[:, layer, 2, NS if tt < 2 else s, :],
                                   lambda tt: modA[:, layer, 3, NS if tt < 2 else s, :], HT, st2, tok_base)
                chk("ffn_norm")
                S.tag = "ffn"
                wdn = sb([128, 22, D], BF, "wdn", st)
                load_w(wdn[:], WB["w_down"], layer * DFF, 22, 0, D)
                G = {}
                for which, r in (("l", s), ("c", NS)):
                    if which == "c" and layer == 1:
                        continue
                    G[which] = sb([128, D], F32, "G3", st)
                    S.dma(G[which][:], GSC[layer, r, 1:2, :].partition_broadcast(128))
                GT = sb([128, 22, 1024], BF, "GT", st)
                HALO = sb([128, 44, 2], F32, "HALO", st)
                wub = RR([sb([128, 8, 2, 128], BF, "wu", st) for _ in range(3)])
                Yb = RR([sb([128, 1026], F32, "Y", st) for _ in range(4)])
                Ub = RR([sb([128, 1024], F32, "U", st) for _ in range(4)])
                bufs = (RR([sb([128, D], F32, "xr", st) for _ in range(2)]), RR([sb([128, D], F32, "tr", st) for _ in range(2)]),
                        sb([128, D], BF, "junk", st))
                scs = []
                if layer == 0:
                    scs.append((0, 256, 0))
                scs += [(256, 1024, 1), (1280, 1024, 2)]
                psU = RR([PS[0], PS[1], PS[2], PS[3]])
                psH = RR([PS[4], PS[5]])
                psD = RR([(PS[6], PS[7]), (PS[4], PS[5])])
                chk("ffn_load")
                for (t0, n, kind) in scs:
                    a0 = t0 - tok_base
                    wq_ = []

                    def wpref(j_):
                        if j_ < 22:
                            w_ = wub.get()
                            load_w(w_[:, :, 0, :], WB["w_up"], layer * D, 8, j_ * 128, 128)
                            load_w(w_[:, :, 1, :], WB["w_up"], layer * D, 8, DFF + j_ * 128, 128)
                            wq_.append(w_)
                    wpref(0)
                    wpref(1)
                    for j in range(22):
                        wpref(j + 2)
                        wu = wq_.pop(0)
                        Us = []
                        for ag in range(2):
                            Y = Yb.get(); U = Ub.get()
                            jj = ag * 22 + j
                            for c in range(0, n, 512):
                                nn = min(512, n - c)
                                ps = psU.get()
                                S.mmacc(ps[:, 0:nn], [(wu[:, k, ag, :], HT[:, k, a0 + c:a0 + c + nn]) for k in range(8)])
                                S.copy(Y[:, 1 + c:1 + c + nn], ps[:, 0:nn], eng="act")
                                S.act(U[:, c:c + nn], ps[:, 0:nn], AF.Identity, scale=fcw[:, layer, jj, 1:2], bias=fcb[:, layer, jj:jj + 1])
                            if kind == 1:
                                ph = psH.get()
                                S.mmacc(ph[:, 0:2], [(wu[:, k, ag, :], HT[:, k, a0 + n - 1:a0 + n + 1]) for k in range(8)])
                                S.copy(Y[:, n + 1:n + 2], ph[:, 1:2], eng="act")
                                S.copy(HALO[:, jj, 0:1], ph[:, 0:1], eng="act")
                                S.memset(Y[:, 0:1], 0.0, eng="pool")
                            elif kind == 2:
                                S.copy(Y[:, 0:1], HALO[:, jj, 0:1], eng="pool")
                                S.memset(Y[:, n + 1:n + 2], 0.0, eng="pool")
                            else:
                                S.memset(Y[:, 0:1], 0.0, eng="pool")
                                S.memset(Y[:, n + 1:n + 2], 0.0, eng="pool")
                            S.stt(U[:, 0:n], Y[:, 0:n], fcw[:, layer, jj, 0:1], U[:, 0:n], ALU.mult, ALU.add)
                            S.stt(U[:, 0:n], Y[:, 2:n + 2], fcw[:, layer, jj, 2:3], U[:, 0:n], ALU.mult, ALU.add)
                            Us.append(U)
                        S.act(Us[1][:, 0:n], Us[1][:, 0:n], AF.Silu)
                        S.tt(GT[:, j, 0:n], Us[0][:, 0:n], Us[1][:, 0:n], ALU.mult, eng="pool")
                    chk("ffn_up")
                    xq = []

                    def pref(ti_):
                        if ti_ < n // 128:
                            x_ = bufs[0].get()
                            tt_ = t0 // 128 + ti_
                            S.dma(x_[:], XD[s, tt_ * 128:(tt_ + 1) * 128, :])
                            xq.append(x_)
                    pref(0)
                    for ti in range(n // 128):
                        pref(ti + 1)
                        tt = t0 // 128 + ti
                        psy = psD.get()
                        for h in range(2):
                            S.mmacc(psy[h][:, :], [(GT[:, j, ti * 128:(ti + 1) * 128], wdn[:, j, h * 512:(h + 1) * 512]) for j in range(22)])
                        if last:
                            dst = OUT[s, (tt - 2) * 128:(tt - 1) * 128, :]
                        else:
                            dst = XD[s, tt * 128:(tt + 1) * 128, :]
                        residual_update(s, layer, 1, tt, psy, xq.pop(0), G["c" if tt < 2 else "l"], dst, bufs)
                        chk("ffn_res")

        def mixer1(s, HT, OT, Z):
            S.tag = "m1hyproj"
            with phase() as st2:
                wub = RR([sb([128, 8, 128], BF, "wu1", st2) for _ in range(4)])
                Yb = RR([sb([128, L + 2], F32, "Yh", st2) for _ in range(2)])
                Ub = RR([sb([128, L], F32, "Uh", st2) for _ in range(3)])
                wq_ = []

                def wpref(j_):
                    if j_ < 12:
                        w_ = wub.get()
                        load_w(w_[:], WB["w1in"], 0, 8, 1408 + j_ * 128, 128)
                        wq_.append(w_)
                wpref(0)
                wpref(1)
                for j in range(12):
                    wpref(j + 2)
                    wu = wq_.pop(0)
                    Y = Yb.get()
                    S.memset(Y[:, 0:1], 0.0)
                    S.memset(Y[:, L + 1:L + 2], 0.0)
                    dst = Z[:, j, :] if j < 4 else Ub.get()[:, :]
                    for c in range(4):
                        ps = PS[5 + c % 2]
                        tk = C + c * 512
                        S.mmacc(ps[:, :], [(wu[:, k, :], HT[:, k, tk:tk + 512]) for k in range(8)])
                        S.copy(Y[:, 1 + c * 512:1 + (c + 1) * 512], ps[:, :], eng="act")
                        S.ts(dst[:, c * 512:(c + 1) * 512], ps[:, :], hcw[:, j, 1:2], hcb[:, j:j + 1], ALU.mult, ALU.add)
                    S.stt(dst, Y[:, 0:L], hcw[:, j, 0:1], dst, ALU.mult, ALU.add)
                    S.stt(dst, Y[:, 2:L + 2], hcw[:, j, 2:3], dst, ALU.mult, ALU.add)
                    if j >= 4:
                        jj = j - 4
                        S.dma(X12[jj // 4, (jj % 4) * 128:(jj % 4 + 1) * 128, :], dst)
            S.tag = "m1winproj"
            with phase() as st1:
                tmpa = sb([128, 512], F32, "tmpa", st1)
                tmpb = sb([128, 512], F32, "tmpb", st1)
                QW = sb([64, 8, L], BF, "QW", st1)
                KW = sb([64, 2, NT], BF, "KW", st1)
                VW = sb([128, 18, 2, 128], BF, "VW", st1)
                S.memset(VW[:, :, :, 64:128], 1.0)
                with phase() as st2:
                    wqb = RR([sb([128, 8, 2, 64], BF, "wq", st2) for _ in range(3)])
                    wk = sb([128, 8, 384], BF, "wk", st2)
                    rtb = RR([sb([128, 2, 512], F32, "rt", st2) for _ in range(1)])
                    load_w(wk[:], WB["w1in"], 0, 8, 1024, 384)
                    for c in range(4):
                        rt = rtb.get()
                        S.dma(rt[0:64], I["ropeD"][0:64, :, c * 512:(c + 1) * 512])
                        tk = C + c * 512
                        for hd in range(8):
                            pa = PS[0]; pb = PS[1]
                            wq = wqb.get()
                            load_w(wq[:, :, 0, :], WB["w1in"], 0, 8, hd * 64, 64)
                            load_w(wq[:, :, 1, :], WB["w1in"], 0, 8, 512 + hd * 64, 64)
                            S.mmacc(pa[0:64, :], [(wq[:, k, 0, :], HT[:, k, tk:tk + 512]) for k in range(8)])
                            S.mmacc(pb[0:64, :], [(wq[:, k, 1, :], HT[:, k, tk:tk + 512]) for k in range(8)])
                            rope_apply(QW[0:64, hd, c * 512:(c + 1) * 512], pa, pb, rt, (0, 64), 0, 512, tmpa, tmpb)
                        for kh in range(2):
                            pa = PS[2]; pb = PS[3]
                            S.mmacc(pa[0:64, :], [(wk[:, k, kh * 64:(kh + 1) * 64], HT[:, k, tk:tk + 512]) for k in range(8)])
                            S.mmacc(pb[0:64, :], [(wk[:, k, 128 + kh * 64:128 + (kh + 1) * 64], HT[:, k, tk:tk + 512]) for k in range(8)])
                            rope_apply(KW[0:64, kh, tk:tk + 512], pa, pb, rt, (0, 64), 0, 512, tmpa, tmpb)
                    for kh in range(2):
                        pa = PS[2]
                        S.mmacc(pa[0:64, 0:C], [(wk[:, k, kh * 64:(kh + 1) * 64], HT[:, k, 0:C]) for k in range(8)])
                        S.copy(KW[0:64, kh, 0:C], pa[0:64, 0:C], eng="act")
                    for t4 in range(0, 18, 4):
                        ps = PS[4]
                        nn = min(4, 18 - t4)
                        for i in range(nn):
                            tt = t4 + i
                            S.mmacc(ps[:, i * 128:(i + 1) * 128], [(HT[:, k, tt * 128:(tt + 1) * 128], wk[:, k, 256:384]) for k in range(8)])
                        for kh in range(2):
                            S.copy(VW[:, t4:t4 + nn, kh, 0:64],
                                   ps[:, 0:nn * 128].rearrange("p (a b) -> p a b", b=128)[:, :, kh * 64:(kh + 1) * 64], eng="act")
                S.tag = "m1win"
                with phase() as st2:
                    wm = sb([128, 2, 512], BF, "wm", st2)
                    S.dma(wm[:], I["wmask"])
                    esx = sb([128, 2, 512], F32, "esx", st2)
                    for kh_ in range(2):
                        for g_ in range(4):
                            S.ts(esx[:, kh_, g_ * 128:(g_ + 1) * 128], zeros512[:, 0:128], esink[:, kh_ * 4 + g_:kh_ * 4 + g_ + 1], None, ALU.add)
                    PTb = RR([sb([128, 512], BF, "PTw", st2) for _ in range(6)])
                    dtmp = RR([sb([64, 512], F32, "dtw", st2) for _ in range(2)])
                    psS = RR([PS[0], PS[1], PS[2], PS[3]])
                    psO = RR([PS[4], PS[5]])
                    scale = 64 ** -0.5
                    for kh in range(2):
                        for nb in range(16):
                            kts = [(0, None), (1, None)]
                            if nb > 0:
                                kts.append((2 + nb - 1, 0))
                            kts.append((2 + nb, None))
                            if nb < 15:
                                kts.append((2 + nb + 1, 1))
                            po = psO.get()
                            pts = []
                            for (kt, mk) in kts:
                                pss = psS.get()
                                S.mm(pss[:, :], KW[0:64, kh, kt * 128:(kt + 1) * 128], QW[0:64, kh * 4:(kh + 1) * 4, nb * 128:(nb + 1) * 128])
                                pt = PTb.get()
                                S.act(pt[:, :], pss[:, :], AF.Exp, scale=scale)
                                if mk is not None:
                                    S.tt(pt[:, :], pt[:, :], wm[:, mk, :], ALU.mult, eng="pool")
                                pts.append((kt, pt))
                            for i, (kt, pt) in enumerate(pts):
                                S.mm(po[:, :], VW[:, kt, kh, :], pt[:, :], start=(i == 0), stop=(i == len(pts) - 1))
                            dt = dtmp.get()
                            S.tt(dt[0:64, :], po[64:128, :], esx[64:128, kh, :], ALU.add)
                            S.recip(dt[0:64, :], dt[0:64, :])
                            pov = po[0:64, :].rearrange("p (g q) -> p g q", g=4)
                            dtv = dt[0:64, :].rearrange("p (g q) -> p g q", g=4)
                            for g2 in range(2):
                                S.tt(OT[g2 * 64:g2 * 64 + 64, kh * 2:kh * 2 + 2, nb * 128:(nb + 1) * 128], pov[:, g2::2, :], dtv[:, g2::2, :], ALU.mult)

        def hyena(s, Z, OT, st):
            S.tag = "hyena"
            ZT = sb([128, 16, 512], BF, "ZT", st)
            Yf = sb([128, 32, 512], BF, "Yf", st)
            fmb = RR([sb([128, 16, 128], BF, "fmh", st) for _ in range(4)])
            imb = RR([sb([128, 8, 512], BF, "imh", st) for _ in range(2)])
            spb = RR([sb([128, 3, 512], F32, "sp", st) for _ in range(2)])
            xb = RR([sb([128, 512], F32, "x12", st) for _ in range(2)])
            tA = sb([128, 512], F32, "tA", st)
            tB = sb([128, 512], F32, "tB", st)
            tC = sb([128, 512], F32, "tC", st)
            zb = sb([128, 128], BF, "zb", st)
            FMv = I["FM"].rearrange("(a p) f -> p a f", p=128)
            IMv = I["IM"].rearrange("(a p) t -> p a t", p=128)
            for n in range(2):
                for a in range(16):
                    ps = PS[a % 2]
                    for ct in range(4):
                        S.transpose(ps[:, ct * 128:(ct + 1) * 128], Z[:, ct, a * 128:(a + 1) * 128], ident[:])
                    S.copy(ZT[:, a, :], ps[:, :], eng="act")
                for i in range(16):
                    fr_ = fmb.get(); fi_ = fmb.get()
                    S.dma(fr_[:], FMv[:, :, i * 128:(i + 1) * 128])
                    S.dma(fi_[:], FMv[:, :, L + i * 128:L + (i + 1) * 128])
                    sp = spb.get()
                    S.dma(sp[:], SPEC[n, :, i * 128:(i + 1) * 128, :].rearrange("w p c -> p w c"))
                    pr = PS[2 + 2 * (i % 2)]
                    pi = PS[3 + 2 * (i % 2)]
                    S.mmacc(pr[:, :], [(fr_[:, a, :], ZT[:, a, :]) for a in range(16)])
                    S.mmacc(pi[:, :], [(fi_[:, a, :], ZT[:, a, :]) for a in range(16)])
                    S.tt(tA[:], pr[:, :], sp[:, 0, :], ALU.mult)
                    S.tt(tB[:], pi[:, :], sp[:, 1, :], ALU.mult)
                    S.tt(Yf[:, i, :], tA[:], tB[:], ALU.subtract, eng="pool")
                    S.tt(tC[:], pr[:, :], sp[:, 1, :], ALU.mult)
                    S.tt(tB[:], pi[:, :], sp[:, 2, :], ALU.mult)
                    S.tt(Yf[:, 16 + i, :], tC[:], tB[:], ALU.add, eng="pool")
                for c in range(4):
                    pss = [PS[4 + ct] for ct in range(4)]
                    for f8 in range(4):
                        im = imb.get()
                        S.dma(im[:], IMv[:, f8 * 8:(f8 + 1) * 8, c * 512:(c + 1) * 512])
                        for ct in range(4):
                            for a in range(8):
                                fa = f8 * 8 + a
                                S.mm(pss[ct][:, :], Yf[:, fa, ct * 128:(ct + 1) * 128], im[:, a, :], start=(fa == 0), stop=(fa == 31))
                    for ct in range(4):
                        x = xb.get()
                        S.dma(x[:], X12[n, ct * 128:(ct + 1) * 128, c * 512:(c + 1) * 512])
                        zs = Z[:, ct, c * 512:(c + 1) * 512]
                        S.stt(tA[:], zs, hbias[:, n, ct:ct + 1], pss[ct][:, :], ALU.mult, ALU.add)
                        if n == 0:
                            S.tt(zs, tA[:], x[:], ALU.mult)
                        else:
                            S.tt(OT[:, 4 + ct, c * 512:(c + 1) * 512], tA[:], x[:], ALU.mult)

        def main_body(chk):
            for s in range(NS):
                with phase() as st:
                    HT = sb([128, 8, NT], BF, "HT", st)
                    OT = sb([128, 8, NT], BF, "OT", st)
                    with phase() as st2:
                        norm_transpose(lambda tt: xtile_src(0, s, tt), list(range(18)),
                                       lambda tt: modA[:, 0, 0, NS if tt < 2 else s, :],
                                       lambda tt: modA[:, 0, 1, NS if tt < 2 else s, :], HT, st2)
                    chk("norm0")
                    mixer0(s, HT, OT, st)
                    chk("mixer0")
                    if dbg and "ot0" in DBG and s == 0:
                        with phase() as std:
                            otf = sb([128, 8, NT], F32, "otf", std)
                            S.copy(otf[:], OT[:])
                            S.dma(DBG["ot0"].rearrange("(k p) t -> p k t", p=128), otf[:])
                    wout_residual(s, 0, OT, list(range(18)), 0)
                    chk("wout0")
                ffn(s, 0, False, chk)
                chk("ffn0")
                S.barrier()
                if dbg and "xd0" in DBG and s == 0:
                    for tt_ in range(18):
                        S.dma(DBG["xd0"][tt_ * 128:(tt_ + 1) * 128, :], XD[0, tt_ * 128:(tt_ + 1) * 128, :])
                    S.barrier()
                with phase() as st:
                    OT = sb([128, 8, L], BF, "OT1", st)
                    Z = sb([128, 4, L], F32, "Z", st)
                    with phase() as stH:
                        HT = sb([128, 8, NT], BF, "HT1", stH)
                        with phase() as st2:
                            norm_transpose(lambda tt: xtile_src(1, s, tt), list(range(18)),
                                           lambda tt: modA[:, 1, 0, NS if tt < 2 else s, :],
                                           lambda tt: modA[:, 1, 1, NS if tt < 2 else s, :], HT, st2)
                        chk("norm1")
                        mixer1(s, HT, OT, Z)
                        chk("mixer1")
                    with phase() as stY:
                        hyena(s, Z, OT, stY)
                        chk("hyena")
                    if dbg and "ot1" in DBG and s == 0:
                        otf = sb([128, 8, L], F32, "otf", st)
                        S.copy(otf[:], OT[:])
                        S.dma(DBG["ot1"].rearrange("(k p) t -> p k t", p=128), otf[:])
                        S.barrier()
                    wout_residual(s, 1, OT, list(range(2, 18)), C)
                    S.barrier()
                ffn(s, 1, True)
                S.barrier()

        class _Stop(Exception):
            pass

        def chk(name):
            if stop == name:
                raise _Stop()

        try:
            with phase() as pc_:
                prologue_cast(pc_)
                S.default_q = "act"
                pmode["engs"] = ("pe", "act", "dve")
                pmode["keep"] = ("cf_", "cb_", "wb_", "w0in", "w_uq", "w_ukv", "w1in", "w_out", "w_up", "w_down")
                prologue_mod()
                chk("mod")
                prologue_hyena()
                chk("hyp")
                pmode["engs"] = None
                pmode["keep"] = ()
                S.default_q = "sp"
            main_body(chk)
        except _Stop:
            pass
        S.finish()
    return nc


def kernel(**inputs):
    NCORE = 8
    NS = 4
    nc = build_program(NS)
    in_maps = [_host_layout(inputs, NS, c) for c in range(NCORE)]
    res = run_bass_kernel_spmd(nc, in_maps, core_ids=list(range(NCORE)))
    out = np.concatenate([np.asarray(r["y"], np.float32) for r in res.results], axis=0)
    return out
```

```python
import contextlib
import math
import numpy as np
import ml_dtypes
import concourse.bass as bass
import concourse.mybir as mybir
from concourse.bass_utils import run_bass_kernel_spmd

F32 = mybir.dt.float32
BF = mybir.dt.bfloat16
AF = mybir.ActivationFunctionType
ALU = mybir.AluOpType
ENGS = ("pe", "act", "dve", "pool", "sp")

D = 1024
L = 2048
C = 256
NT = L + C
DFF = 2816
EPS = 1e-6


def _box(ap):
    t = ap.tensor
    dims = list(ap.ap)
    off = int(ap.offset)
    if t.__class__.__name__.startswith("DRam"):
        hi = off + sum((int(c) - 1) * abs(int(s)) for s, c in dims) + 1
        return (t.name, 0, 1, off, hi)
    row = 1
    for s in list(t.shape)[1:]:
        row *= int(s)
    p0 = off // row
    f0 = off % row
    pstep, pcnt = int(dims[0][0]), int(dims[0][1])
    npart = pcnt if (pstep == row or pcnt == 1) else 128 - p0
    ext = sum((int(c) - 1) * abs(int(s)) for s, c in dims[1:]) + 1
    if t.__class__.__name__.startswith("PSum"):
        return (t.name, 0, 128, 0, row)
    return (t.name, p0, p0 + npart, f0, f0 + ext)


def _overlap(a, b):
    return a[1] < b[2] and b[1] < a[2] and a[3] < b[4] and b[3] < a[4]


def _covers(a, b):
    return a[1] <= b[1] and a[2] >= b[2] and a[3] <= b[3] and a[4] >= b[4]


class Op:
    __slots__ = ("eng", "fn", "deps", "sig", "signaled", "is_dma", "slot", "slotn", "clock", "mm", "tag")

    def __init__(self, eng, fn, is_dma=False, mm=False):
        self.eng = eng
        self.fn = fn
        self.deps = []
        self.sig = 0
        self.signaled = False
        self.is_dma = is_dma
        self.slot = -1
        self.slotn = 0
        self.clock = None
        self.mm = mm


class Sched:
    def __init__(self, nc, n_dma_slots=32):
        self.nc = nc
        self.ops = []
        self.recs = {}
        self.nslots = n_dma_slots
        self.eobj = {"pe": nc.tensor, "act": nc.scalar, "dve": nc.vector, "pool": nc.gpsimd, "sp": nc.sync}
        self.last = {e: None for e in ENGS}
        self.dmas_since = []
        self.bar = {e: None for e in ENGS}
        self.stack = contextlib.ExitStack()
        self.esem = {e: self.stack.enter_context(nc.semaphore("s_" + e)) for e in ENGS}
        self.dsem = [self.stack.enter_context(nc.semaphore("d_%d" % i)) for i in range(n_dma_slots)]
        self.cnt = {e: 0 for e in ENGS}
        self.rr = 0
        self.rrq = {}
        self.slot_cnt = [0] * n_dma_slots
        self.slot_last = [None] * n_dma_slots
        self.known = {e: ({x: 0 for x in ENGS}, [0] * n_dma_slots) for e in ENGS}
        self.n_emitted = 0
        self.tag = ""
        self.names = None
        self.default_q = "sp"

    def add(self, eng, fn, reads=(), writes=(), is_dma=False, mm=False):
        op = Op(eng, fn, is_dma, mm)
        op.tag = self.tag
        deps = {}
        rb = [_box(a) for a in reads]
        wb = [_box(a) for a in writes]
        for b in rb:
            lst = self.recs.get(b[0])
            if lst:
                psum = b[0].startswith("ps")
                for r in lst:
                    if _overlap(r[0], b) and (r[2] or (psum and r[1].eng != eng)):
                        deps[id(r[1])] = r[1]
        for b in wb:
            lst = self.recs.get(b[0])
            if lst:
                for r in lst:
                    if _overlap(r[0], b):
                        if mm and r[2] and r[1].mm:
                            continue
                        deps[id(r[1])] = r[1]
        if self.bar[eng] is not None:
            for d in self.bar[eng]:
                deps[id(d)] = d
            self.bar[eng] = None
        op.deps = list(deps.values())
        for b in wb:
            lst = self.recs.setdefault(b[0], [])
            lst[:] = [r for r in lst if not _covers(b, r[0])]
            lst.append([b, op, True])
        for b in rb:
            lst = self.recs.setdefault(b[0], [])
            if not is_dma:
                lst[:] = [r for r in lst if not ((not r[2]) and r[0] == b and r[1].eng == eng and not r[1].is_dma)]
            lst.append([b, op, False])
        self.ops.append(op)
        if is_dma:
            self.dmas_since.append(op)
        else:
            self.last[eng] = op
        return op

    def barrier(self, engs=None, keep=()):
        sel = ENGS if engs is None else engs
        for e in sel:
            o = self.last[e]
            if o is not None:
                o.signaled = True
        self.flush()
        b = [self.last[e] for e in sel if self.last[e] is not None] + [d for d in self.dmas_since if d.eng in sel]
        self.dmas_since = [d for d in self.dmas_since if d.eng not in sel]
        for e in sel:
            self.bar[e] = list(b) + (self.bar[e] or [])
        if engs is None:
            self.recs = {}
        else:
            self.recs = {k: v for k, v in self.recs.items() if k.startswith(tuple(keep))}

    def mm(self, out, lhsT, rhs, start=True, stop=True):
        nc = self.nc
        return self.add("pe", lambda: nc.tensor.matmul(out, lhsT, rhs, start=start, stop=stop),
                        reads=[lhsT, rhs], writes=[out], mm=True)

    def mmacc(self, out, pairs):
        n = len(pairs)
        for i, (l, r) in enumerate(pairs):
            self.mm(out, l, r, start=(i == 0), stop=(i == n - 1))

    def transpose(self, out, in_, ident):
        nc = self.nc
        return self.add("pe", lambda: nc.tensor.transpose(out, in_, ident), reads=[in_, ident], writes=[out], mm=True)

    def act(self, out, in_, func, bias=None, scale=None, accum_out=None):
        nc = self.nc
        kw = {}
        rd = [in_]
        wr = [out]
        if bias is not None:
            kw["bias"] = bias
            if not isinstance(bias, (int, float)):
                rd.append(bias)
        if scale is not None:
            kw["scale"] = scale
            if not isinstance(scale, (int, float)):
                rd.append(scale)
        if accum_out is not None:
            kw["accum_out"] = accum_out
            wr.append(accum_out)
        return self.add("act", lambda: nc.scalar.activation(out, in_, func, **kw), reads=rd, writes=wr)

    def tt(self, out, in0, in1, op, eng="dve"):
        e = self.eobj[eng]
        return self.add(eng, lambda: e.tensor_tensor(out, in0, in1, op), reads=[in0, in1], writes=[out])

    def ts(self, out, in0, s1, s2, op0, op1=None, eng="dve"):
        e = self.eobj[eng]
        rd = [in0] + [s for s in (s1, s2) if s is not None and not isinstance(s, (int, float))]
        if op1 is None:
            return self.add(eng, lambda: e.tensor_scalar(out, in0, s1, None, op0), reads=rd, writes=[out])
        return self.add(eng, lambda: e.tensor_scalar(out, in0, s1, s2, op0, op1), reads=rd, writes=[out])

    def stt(self, out, in0, scalar, in1, op0, op1, eng="dve"):
        e = self.eobj[eng]
        rd = [in0, in1] + ([] if isinstance(scalar, (int, float)) else [scalar])
        return self.add(eng, lambda: e.scalar_tensor_tensor(out, in0, scalar, in1, op0, op1), reads=rd, writes=[out])

    def copy(self, out, in_, eng="dve"):
        e = self.eobj[eng]
        if eng == "act":
            return self.add(eng, lambda: e.copy(out, in_), reads=[in_], writes=[out])
        return self.add(eng, lambda: e.tensor_copy(out, in_), reads=[in_], writes=[out])

    def memset(self, ap, val, eng="dve"):
        e = self.eobj[eng]
        return self.add(eng, lambda: e.memset(ap, val), reads=[], writes=[ap])

    def recip(self, out, in_):
        nc = self.nc
        return self.add("dve", lambda: nc.vector.reciprocal(out, in_), reads=[in_], writes=[out])

    def dma(self, out, in_, q=None, **kw):
        q = q or self.default_q
        e = self.eobj[q]
        return self.add(q, lambda: e.dma_start(out, in_, **kw), reads=[in_], writes=[out], is_dma=True)

    def flush(self):
        ops = self.ops
        self.ops = []
        ns = self.nslots
        for op in ops:
            for d in op.deps:
                d.signaled = True
        for op in ops:
            if op.is_dma:
                lo, hi = (0, 20) if op.eng == "sp" else (20, ns)
                r_ = self.rrq.get(op.eng, lo)
                op.slot = r_
                self.rrq[op.eng] = lo + (r_ + 1 - lo) % (hi - lo)
                self.slot_cnt[op.slot] += 1
                op.slotn = self.slot_cnt[op.slot]
            elif op.signaled:
                self.cnt[op.eng] += 1
                op.sig = self.cnt[op.eng]
        esem, dsem, slot_last = self.esem, self.dsem, self.slot_last
        for op in ops:
            e = self.eobj[op.eng]
            ke, kd = self.known[op.eng]
            need = op.deps
            if op.is_dma and slot_last[op.slot] is not None:
                need = need + [slot_last[op.slot]]
            for d in need:
                if d.is_dma:
                    if kd[d.slot] >= d.slotn:
                        continue
                    e.wait_ge(dsem[d.slot], 16 * d.slotn)
                else:
                    if ke[d.eng] >= d.sig:
                        continue
                    e.wait_ge(esem[d.eng], d.sig)
                ce, cd = d.clock
                for x in ENGS:
                    if ce[x] > ke[x]:
                        ke[x] = ce[x]
                for i in range(ns):
                    if cd[i] > kd[i]:
                        kd[i] = cd[i]
            inst = op.fn()
            if self.names is not None:
                self.names[inst.ins.name] = op.tag
            if op.is_dma:
                inst.then_inc(dsem[op.slot], 16)
                slot_last[op.slot] = op
                cd2 = list(kd)
                cd2[op.slot] = max(cd2[op.slot], op.slotn)
                op.clock = (dict(ke), cd2)
            elif op.signaled:
                inst.then_inc(esem[op.eng], 1)
                ce2 = dict(ke)
                ce2[op.eng] = max(ce2[op.eng], op.sig)
                op.clock = (ce2, list(kd))
            op.fn = None
            op.deps = None
        self.n_emitted += len(ops)

    def finish(self):
        self.barrier()
        for i in range(self.nslots):
            if self.slot_cnt[i]:
                self.nc.sync.wait_ge(self.dsem[i], 16 * self.slot_cnt[i])
        self.stack.close()


class RR:
    def __init__(self, items):
        self.items = items
        self.i = 0

    def get(self):
        x = self.items[self.i % len(self.items)]
        self.i += 1
        return x


def _rope_tables(rot_dim):
    rows = L // 64
    row = np.repeat(np.arange(rows), 64).astype(np.float32)
    col = np.tile(np.arange(64), rows).astype(np.float32)
    quarter = rot_dim // 4
    inv = (np.float32(10000.0) ** (-np.arange(quarter, dtype=np.float32) / quarter)).astype(np.float32)
    ang = np.concatenate([row[:, None] * inv, col[:, None] * inv], axis=-1).astype(np.float32)
    cos = np.cos(ang).astype(np.float32).T
    sin = np.sin(ang).astype(np.float32).T
    cos2 = np.concatenate([cos, cos], 0)
    sin2 = np.concatenate([-sin, sin], 0)
    return cos2, sin2


_CONST = {}


def _consts():
    if _CONST:
        return _CONST
    cm, sm = _rope_tables(32)
    ropeM = np.zeros((128, 2, L), np.float32)
    ropeM[64:96, 0] = cm
    ropeM[64:96, 1] = sm
    cd, sd = _rope_tables(64)
    ropeD = np.zeros((128, 2, L), np.float32)
    ropeD[0:64, 0] = cd
    ropeD[64:128, 0] = cd
    ropeD[0:64, 1] = sd
    ropeD[64:128, 1] = sd
    k = np.arange(128)[:, None]
    q = np.arange(128)[None, :]
    m_prev = (q <= k).astype(np.float32)
    m_next = (k <= q).astype(np.float32)
    wmask = np.stack([np.tile(m_prev, (1, 4)), np.tile(m_next, (1, 4))], 1).astype(ml_dtypes.bfloat16)
    t = np.arange(L, dtype=np.int64)
    f = np.arange(L, dtype=np.int64)
    m = (t[:, None] * f[None, :]) % (2 * L)
    th = 2.0 * np.pi * m.astype(np.float64) / (2 * L)
    Cm = np.cos(th)
    Sm = np.sin(th)
    alt = np.where(t % 2 == 0, 1.0, -1.0)
    FM = np.concatenate([Cm, -Sm], axis=1)
    FM[:, L] = alt
    N = 2 * L
    IMr = 2.0 * Cm.T / N
    IMr[0, :] = 1.0 / N
    IMi = -2.0 * Sm.T / N
    IMi[0, :] = alt / N
    IM = np.concatenate([IMr, IMi], axis=0)
    tf = np.arange(L, dtype=np.float32)
    tn = tf / np.float32(L - 1)
    bands = np.linspace(1e-4, 15, 16, dtype=np.float32)
    ang = (np.float32(2.0 * math.pi) * bands[None, :] * tf[:, None] / np.float32(L)).astype(np.float32)
    feats = np.concatenate([tn[:, None], np.cos(ang), -np.sin(ang)], axis=-1).astype(np.float32)
    min_decay = math.log(1e-2) / 1.5
    max_decay = math.log(1e-2) / 0.3
    deltas = np.abs(np.linspace(min_decay, max_decay, 512, dtype=np.float32))
    decay = np.exp(-tn[:, None] * deltas[None, :]).astype(np.float32)
    ident = np.eye(128, dtype=np.float32)
    FMt = np.ascontiguousarray(FM.reshape(16, 128, 32, 128).transpose(2, 1, 0, 3)).reshape(32, 128, 16 * 128)
    IMt = np.ascontiguousarray(IM.reshape(4, 8, 128, 4, 512).transpose(0, 3, 2, 1, 4)).reshape(4, 4, 128, 8 * 512)
    _CONST.update(dict(ropeM=ropeM, ropeD=ropeD, wmask=wmask, FM=FMt.astype(ml_dtypes.bfloat16),
                       IM=IMt.astype(ml_dtypes.bfloat16), featsT=np.ascontiguousarray(feats.T), decay=decay,
                       ident=ident))
    return _CONST


def _swap_halves(w, block):
    k, n = w.shape
    return np.ascontiguousarray(w.reshape(k, n // block, 2, block // 2)[:, :, ::-1, :].reshape(k, n))


def _colsT(v, ntile):
    return np.ascontiguousarray(np.asarray(v, np.float32).reshape(ntile, 128).T)


def _host_layout(inp, NS, core):
    f = lambda a: np.ascontiguousarray(np.asarray(a, np.float32))
    cst = _consts()
    b0 = core * NS
    m = {}
    m["x"] = f(inp["x"][b0:b0 + NS])
    m["ctx"] = f(inp["ctx"][b0:b0 + NS])
    rows = np.concatenate([f(inp["c"][b0:b0 + NS]), f(inp["c_ctx"])[None, :]], 0)
    m["cT"] = np.ascontiguousarray(rows.reshape(NS + 1, 8, 128).transpose(2, 1, 0))
    m["ada_w"] = f(inp["ada_w"])
    m["ada_bT"] = np.ascontiguousarray(f(inp["ada_b"]).reshape(2, 48, 128).transpose(2, 0, 1))
    m["norm_gT"] = np.ascontiguousarray(f(inp["norm_g"]).reshape(2, 4, 8, 128).transpose(3, 0, 1, 2))
    m["w_out"] = f(inp["mix_w_out"])
    m["w_up"] = f(inp["ffn_w_up"])
    m["w_down"] = f(inp["ffn_w_down"])
    m["fcw"] = np.ascontiguousarray(f(inp["ffn_conv_w"]).reshape(2, 3, 44, 128).transpose(3, 0, 2, 1))
    m["fcb"] = np.ascontiguousarray(f(inp["ffn_conv_b"]).reshape(2, 44, 128).transpose(2, 0, 1))
    we = f(inp["even_w_in"][0])
    cq, ckv, kr, dq, dk, dv = np.split(we, np.cumsum([384, 256, 32, 512, 512])[:], axis=1)
    m["w0in"] = np.ascontiguousarray(np.concatenate(
        [cq, ckv, dq, _swap_halves(dq, 64), dk, _swap_halves(dk, 64), dv, kr, _swap_halves(kr, 32)], 1))
    uq = f(inp["mla_w_uq"][0]).reshape(384, 8, 96)
    uq_rot = _swap_halves(np.ascontiguousarray(uq[:, :, 64:]).reshape(384, 256), 32)
    m["w_uq"] = np.ascontiguousarray(np.concatenate([uq.reshape(384, 768), uq_rot], 1))
    ukv = f(inp["mla_w_ukv"][0]).reshape(256, 8, 128)
    m["w_ukv"] = np.ascontiguousarray(np.concatenate([ukv[:, :, :64].reshape(256, 512), ukv[:, :, 64:].reshape(256, 512)], 1))
    m["qngT"] = _colsT(inp["mla_q_norm_g"][0], 3)
    m["kvngT"] = _colsT(inp["mla_kv_norm_g"][0], 2)
    m["subgT"] = _colsT(inp["diff_subln_g"][0], 1)
    m["dlam"] = f(inp["diff_lambda"][0]).reshape(1, 256)
    wo = f(inp["odd_w_in"][0])
    q, k_, v_, u = np.split(wo, np.cumsum([512, 128, 128]), axis=1)
    m["w1in"] = np.ascontiguousarray(np.concatenate([q, _swap_halves(q, 64), k_, _swap_halves(k_, 64), v_, u], 1))
    m["sink"] = f(inp["win_sink"][0]).reshape(1, 8)
    m["hcw"] = np.ascontiguousarray(f(inp["hy_conv_w"][0]).reshape(3, 12, 128).transpose(2, 1, 0))
    m["hcb"] = _colsT(inp["hy_conv_b"][0], 12)
    m["hbias"] = np.ascontiguousarray(f(inp["hy_bias"][0]).reshape(2, 4, 128).transpose(2, 0, 1))
    m["hw1"] = f(inp["hy_f_w1"][0])
    m["hb1"] = f(inp["hy_f_b1"][0]).reshape(64, 1)
    m["hw2"] = f(inp["hy_f_w2"][0])
    m["hb2"] = f(inp["hy_f_b2"][0]).reshape(64, 1)
    m["hw3"] = f(inp["hy_f_w3"][0])
    m["hb3"] = f(inp["hy_f_b3"][0]).reshape(1, 2048)
    m["hfreq"] = f(inp["hy_f_freq"][0]).reshape(64, 1)
    for kk in ("ropeM", "ropeD", "wmask", "FM", "IM", "featsT", "decay", "ident"):
        m[kk] = cst[kk]
    return m


def build_program(NS, dbg=None, stop=None, names=None):
    R = NS + 1
    nc = bass.Bass("TRN2", target_bir_lowering=False)
    S = Sched(nc)
    S.names = names
    uid = [0]

    def din(name, shape, dt=F32):
        return nc.dram_tensor(name, list(shape), dt, kind="ExternalInput").ap()

    def dscr(name, shape, dt):
        return nc.dram_tensor(name, list(shape), dt, kind="Internal").ap()

    I = {}
    I["x"] = din("x", [NS, L, D])
    I["ctx"] = din("ctx", [NS, C, D])
    I["cT"] = din("cT", [128, 8, R])
    I["ada_w"] = din("ada_w", [2, D, 6 * D])
    I["ada_bT"] = din("ada_bT", [128, 2, 48])
    I["norm_gT"] = din("norm_gT", [128, 2, 4, 8])
    I["w_out"] = din("w_out", [2, D, D])
    I["w_up"] = din("w_up", [2, D, 2 * DFF])
    I["w_down"] = din("w_down", [2, DFF, D])
    I["fcw"] = din("fcw", [128, 2, 44, 3])
    I["fcb"] = din("fcb", [128, 2, 44])
    I["w0in"] = din("w0in", [D, 3264])
    I["w_uq"] = din("w_uq", [384, 1024])
    I["w_ukv"] = din("w_ukv", [256, 1024])
    I["qngT"] = din("qngT", [128, 3])
    I["kvngT"] = din("kvngT", [128, 2])
    I["subgT"] = din("subgT", [128, 1])
    I["dlam"] = din("dlam", [1, 256])
    I["w1in"] = din("w1in", [D, 2944])
    I["sink"] = din("sink", [1, 8])
    I["hcw"] = din("hcw", [128, 12, 3])
    I["hcb"] = din("hcb", [128, 12])
    I["hbias"] = din("hbias", [128, 2, 4])
    I["hw1"] = din("hw1", [33, 64])
    I["hb1"] = din("hb1", [64, 1])
    I["hw2"] = din("hw2", [64, 64])
    I["hb2"] = din("hb2", [64, 1])
    I["hw3"] = din("hw3", [64, 2048])
    I["hb3"] = din("hb3", [1, 2048])
    I["hfreq"] = din("hfreq", [64, 1])
    I["ropeM"] = din("ropeM", [128, 2, L])
    I["ropeD"] = din("ropeD", [128, 2, L])
    I["wmask"] = din("wmask", [128, 2, 512], BF)
    I["FM"] = din("FM", [32, 128, 16 * 128], BF)
    I["IM"] = din("IM", [4, 4, 128, 8 * 512], BF)
    I["featsT"] = din("featsT", [33, L])
    I["decay"] = din("decay", [L, 512])
    I["ident"] = din("ident", [128, 128])
    OUT = nc.dram_tensor("y", [NS, L, D], F32, kind="ExternalOutput").ap()
    DBG = {}
    if dbg:
        for name, shape in dbg.items():
            DBG[name] = nc.dram_tensor("dbg_" + name, list(shape), F32, kind="ExternalOutput").ap()

    XD = dscr("XD", [NS, NT, D], F32)
    GSC = dscr("GSC", [2, R, 2, D], F32)
    SPEC = dscr("SPEC", [2, 3, L, 512], F32)
    X12 = dscr("X12", [2, 512, L], F32)
    WB = {}
    for nm, shp in (("w0in", [D, 3264]), ("w_uq", [384, 1024]), ("w_ukv", [256, 1024]), ("w1in", [D, 2944]),
                    ("w_out", [2 * D, D]), ("w_up", [2 * D, 2 * DFF]), ("w_down", [2 * DFF, D])):
        WB[nm] = dscr("wb_" + nm, shp, BF)

    es = contextlib.ExitStack()

    pmode = {"engs": None, "keep": ()}

    @contextlib.contextmanager
    def phase():
        st_ = contextlib.ExitStack()
        try:
            yield st_
            S.barrier(pmode["engs"], pmode["keep"])
        finally:
            st_.close()


    live = [0, 0]
    SB_LIMIT = 229344 - 16481 - 4096

    def sb(shape, dt, name=None, stack=None):
        uid[0] += 1
        nb = 1
        for d_ in shape[1:]:
            nb *= int(d_)
        nb *= 2 if dt == BF else 4
        nb = (nb + 31) // 32 * 32
        live[0] += nb
        live[1] = max(live[1], live[0])
        assert live[0] <= SB_LIMIT, ("SBUF over budget", name, live[0])
        stk = stack or es

        def _rel():
            live[0] -= nb
        stk.callback(_rel)
        return stk.enter_context(nc.sbuf_tensor("%s_%d" % (name or "t", uid[0]), list(shape), dt))

    with es:
        PS = [es.enter_context(nc.psum_tensor("ps%d" % i, [128, 512], F32)) for i in range(8)]
        ident = sb([128, 128], F32, "ident")
        ones_bf = sb([128, 128], BF, "ones")
        modA = sb([128, 2, 4, R, 8], F32, "modA")
        fcw = sb([128, 2, 44, 3], F32, "fcw")
        fcb = sb([128, 2, 44], F32, "fcb")
        qng = sb([128, 3], F32, "qng")
        kvng = sb([128, 2], F32, "kvng")
        subg = sb([128, 1], F32, "subg")
        nlam = sb([128, 1], F32, "nlam")
        esink = sb([128, 8], F32, "esink")
        hcw = sb([128, 12, 3], F32, "hcw")
        hcb = sb([128, 12], F32, "hcb")
        hbias = sb([128, 2, 4], F32, "hbias")
        small = sb([128, 64], F32, "small")
        smallrr = [0]

        def scol():
            smallrr[0] = (smallrr[0] + 1) % 64
            return small[:, smallrr[0]:smallrr[0] + 1]

        zeros512 = sb([128, 512], F32, "zeros")
        S.memset(zeros512[:], 0.0)
        S.dma(ident[:], I["ident"])
        S.memset(ones_bf[:], 1.0)
        S.dma(fcw[:], I["fcw"])
        S.dma(fcb[:], I["fcb"])
        S.dma(qng[:], I["qngT"])
        S.dma(kvng[:], I["kvngT"])
        S.dma(subg[:], I["subgT"])
        S.dma(hcw[:], I["hcw"])
        S.dma(hcb[:], I["hcb"])
        S.dma(hbias[:], I["hbias"])

        def rstd_from(ss, n_feat, np_=128):
            a = scol()
            S.act(a[0:np_], ss, AF.Sqrt, bias=EPS, scale=1.0 / n_feat)
            b = scol()
            S.recip(b[0:np_], a[0:np_])
            return b

        def prologue_cast(ps_):
            S.tag = "cast"
            CH = 2048
            fbuf = RR([sb([128, CH], F32, "cf", ps_) for _ in range(4)])
            bbuf = RR([sb([128, CH], BF, "cb", ps_) for _ in range(4)])
            srcs = [("w0in", I["w0in"]), ("w_uq", I["w_uq"]), ("w_ukv", I["w_ukv"]), ("w1in", I["w1in"]),
                    ("w_out", I["w_out"].rearrange("l k n -> (l k) n")),
                    ("w_up", I["w_up"].rearrange("l k n -> (l k) n")),
                    ("w_down", I["w_down"].rearrange("l k n -> (l k) n"))]
            for nm, src in srcs:
                rows, cols = src.shape
                per = rows * cols // 128
                sflat = src.rearrange("(p a) n -> p (a n)", p=128)
                dflat = WB[nm].rearrange("(p a) n -> p (a n)", p=128)
                o = 0
                while o < per:
                    n = min(CH, per - o)
                    fb = fbuf.get()
                    bb = bbuf.get()
                    S.dma(fb[:, 0:n], sflat[:, o:o + n], q="sp")
                    S.copy(bb[:, 0:n], fb[:, 0:n], eng="pool")
                    S.dma(dflat[:, o:o + n], bb[:, 0:n], q="sp")
                    o += n

        def prologue_mod():
            S.tag = "mod"
            with phase() as ps_:
                scT = sb([128, 8, R], F32, "scT", ps_)
                gT = sb([128, 2, 4, 8], F32, "gT", ps_)
                abT = sb([128, 2, 48], F32, "abT", ps_)
                modT = sb([128, 48, R], F32, "modT", ps_)
                PG = sb([128, 2, 2, R, 8], F32, "PG", ps_)
                rowt = sb([8, 128], F32, "rowt", ps_)
                S.dma(scT[:], I["cT"])
                S.act(scT[:], scT[:], AF.Silu)
                S.dma(gT[:], I["norm_gT"])
                S.dma(abT[:], I["ada_bT"])
                awb = RR([sb([128, 8, 512], F32, "aw", ps_) for _ in range(2)])
                for l in range(2):
                    for cc in range(12):
                        aw = awb.get()
                        S.dma(aw[:], I["ada_w"][l, :, cc * 512:(cc + 1) * 512].rearrange("(k p) n -> p k n", p=128))
                        for ct in range(4):
                            j = cc * 4 + ct
                            ps = PS[j % 2]
                            S.mmacc(ps[:, 0:R], [(aw[:, k, ct * 128:(ct + 1) * 128], scT[:, k, :]) for k in range(8)])
                            S.ts(modT[:, j, :], ps[:, 0:R], abT[:, l, j:j + 1], None, ALU.add)
                    for r in range(R):
                        def m_(i):
                            return modT[:, i * 8:(i + 1) * 8, r]
                        S.stt(modA[:, l, 0, r, :], m_(1), 1.0, gT[:, l, 0, :], ALU.add, ALU.mult)
                        S.copy(modA[:, l, 1, r, :], m_(0))
                        S.stt(modA[:, l, 2, r, :], m_(4), 1.0, gT[:, l, 2, :], ALU.add, ALU.mult)
                        S.copy(modA[:, l, 3, r, :], m_(3))
                        S.tt(PG[:, l, 0, r, :], m_(2), gT[:, l, 1, :], ALU.mult)
                        S.tt(PG[:, l, 1, r, :], m_(5), gT[:, l, 3, :], ALU.mult)
                        for w_ in range(2):
                            ps = PS[2 + (w_ % 2)]
                            S.transpose(ps[0:8, 0:128], PG[:, l, w_, r, :], ident[:])
                            S.copy(rowt[:], ps[0:8, 0:128])
                            S.dma(GSC[l, r, w_, :].rearrange("(k p) -> k p", p=128), rowt[:])
                dl = sb([128, 256], F32, "dl", ps_)
                S.dma(dl[:], I["dlam"].partition_broadcast(128))
                pr = sb([128, 128], F32, "pr", ps_)
                S.tt(pr[:, 0:64], dl[:, 0:64], dl[:, 64:128], ALU.mult)
                S.tt(pr[:, 64:128], dl[:, 128:192], dl[:, 192:256], ALU.mult)
                s1 = scol(); s2 = scol()
                S.act(pr[:, 0:64], pr[:, 0:64], AF.Identity, accum_out=s1)
                S.act(pr[:, 64:128], pr[:, 64:128], AF.Identity, accum_out=s2)
                e1 = scol(); e2 = scol()
                S.act(e1, s1, AF.Exp)
                S.act(e2, s2, AF.Exp)
                S.tt(e2, e2, e1, ALU.subtract)
                S.ts(nlam[:], e2, -0.2, None, ALU.add)
                S.ts(subg[:], subg[:], 0.8, None, ALU.mult)
                sk = sb([128, 8], F32, "sk", ps_)
                S.dma(sk[:], I["sink"].partition_broadcast(128))
                S.act(esink[:], sk[:], AF.Exp)

        def sin_safe(out, pre, np_, n, tmp1, tmp2):
            S.act(tmp1, pre, AF.Sin, scale=0.25)
            S.act(tmp2, pre, AF.Sin, scale=0.25, bias=halfpi[0:np_, :])
            S.tt(tmp2, tmp1, tmp2, ALU.mult)
            S.tt(tmp1, tmp1, tmp1, ALU.mult)
            S.ts(tmp1, tmp1, -8.0, 4.0, ALU.mult, ALU.add)
            S.tt(out, tmp1, tmp2, ALU.mult)

        halfpi = sb([128, 1], F32, "halfpi")
        S.memset(halfpi[:], math.pi / 2)

        def prologue_hyena():
            S.tag = "hyp"
            with phase() as ps_:
                ft = sb([33, L], F32, "ft", ps_)
                w1 = sb([33, 64], F32, "w1", ps_)
                w2 = sb([64, 64], F32, "w2", ps_)
                w3 = sb([65, 2048], F32, "w3", ps_)
                b1 = sb([64, 1], F32, "b1", ps_)
                b2 = sb([64, 1], F32, "b2", ps_)
                fr = sb([64, 1], F32, "fr", ps_)
                h1 = sb([64, L], F32, "h1", ps_)
                h2 = sb([65, L], F32, "h2", ps_)
                t1 = sb([64, 512], F32, "t1", ps_)
                t2 = sb([64, 512], F32, "t2", ps_)
                t3 = sb([64, 512], F32, "t3", ps_)
                S.dma(ft[:], I["featsT"])
                S.dma(w1[:], I["hw1"])
                S.dma(w2[:], I["hw2"])
                S.dma(w3[0:64, :], I["hw3"])
                S.dma(w3[64:65, :], I["hb3"])
                S.dma(b1[:], I["hb1"])
                S.dma(b2[:], I["hb2"])
                S.dma(fr[:], I["hfreq"])
                S.tt(b1[:], b1[:], fr[:], ALU.mult)
                S.tt(b2[:], b2[:], fr[:], ALU.mult)
                S.memset(h2[64:65, :], 1.0)
                for c in range(4):
                    cs = slice(c * 512, (c + 1) * 512)
                    ps = PS[c % 2]
                    S.mm(ps[0:64, :], w1[:, :], ft[:, cs])
                    S.ts(t3[:], ps[0:64, :], fr[:, 0:1], b1[:, 0:1], ALU.mult, ALU.add)
                    sin_safe(h1[:, cs], t3[:], 64, 512, t1[:], t2[:])
                for c in range(4):
                    cs = slice(c * 512, (c + 1) * 512)
                    ps = PS[2 + c % 2]
                    S.mm(ps[0:64, :], w2[:, :], h1[:, cs])
                    S.ts(t3[:], ps[0:64, :], fr[:, 0:1], b2[:, 0:1], ALU.mult, ALU.add)
                    sin_safe(h2[0:64, cs], t3[:], 64, 512, t1[:], t2[:])
                decb = RR([sb([128, 512], F32, "dec", ps_) for _ in range(3)])
                Pm = [sb([128, 16, 512], BF, "Pm", ps_) for _ in range(2)]
                Mm = [sb([128, 16, 512], BF, "Mm", ps_) for _ in range(2)]
                hfb = RR([sb([128, 512], F32, "hf", ps_) for _ in range(2)])
                hbb = RR([sb([128, 512], F32, "hb", ps_) for _ in range(2)])
                for n in range(2):
                    for a in range(16):
                        psf = PS[4 + 2 * (a % 2)]
                        psb = PS[5 + 2 * (a % 2)]
                        hf = hfb.get(); hb = hbb.get()
                        dec_a = decb.get()
                        S.dma(dec_a[:], I["decay"][a * 128:(a + 1) * 128, :])
                        S.mm(psf[:, :], h2[:, a * 128:(a + 1) * 128], w3[:, n * 512:(n + 1) * 512])
                        S.mm(psb[:, :], h2[:, a * 128:(a + 1) * 128], w3[:, 1024 + n * 512:1024 + (n + 1) * 512])
                        S.tt(hf[:], psf[:, :], dec_a[:], ALU.mult)
                        S.tt(hb[:], psb[:, :], dec_a[:], ALU.mult)
                        if a == 0:
                            S.memset(hb[0:1, :], 0.0)
                        S.tt(Pm[n][:, a, :], hf[:], hb[:], ALU.add)
                        S.tt(Mm[n][:, a, :], hf[:], hb[:], ALU.subtract)
                fmb = RR([sb([128, 16, 128], BF, "fm", ps_) for _ in range(3)])
                ob = RR([sb([128, 512], F32, "so", ps_) for _ in range(3)])
                for n in range(2):
                    for ftile in range(32):
                        fm = fmb.get()
                        S.dma(fm[:].rearrange("p a f -> p (a f)"), I["FM"][ftile])
                        src = Pm[n] if ftile < 16 else Mm[n]
                        ps = PS[6 + ftile % 2]
                        S.mmacc(ps[:, :], [(fm[:, a, :], src[:, a, :]) for a in range(16)])
                        o = ob.get()
                        S.copy(o[:], ps[:, :], eng="act")
                        if ftile < 16:
                            S.dma(SPEC[n, 0, ftile * 128:(ftile + 1) * 128, :], o[:])
                            if ftile > 0:
                                S.dma(SPEC[n, 2, ftile * 128:(ftile + 1) * 128, :], o[:])
                            else:
                                S.dma(SPEC[n, 2, 1:128, :], o[1:128, :])
                        else:
                            if ftile == 16:
                                ps2 = PS[4]
                                S.mmacc(ps2[:, :], [(fm[:, a, :], Pm[n][:, a, :]) for a in range(16)])
                                o2 = ob.get()
                                S.copy(o2[0:1, :], ps2[0:1, :], eng="act")
                                S.dma(SPEC[n, 2, 0:1, :], o2[0:1, :])
                                S.memset(o[0:1, :], 0.0)
                            S.dma(SPEC[n, 1, (ftile - 16) * 128:(ftile - 15) * 128, :], o[:])

        def xtile_src(layer, s, tt):
            if layer == 0:
                if tt < 2:
                    return I["ctx"][s, tt * 128:(tt + 1) * 128, :]
                return I["x"][s, (tt - 2) * 128:(tt - 1) * 128, :]
            return XD[s, tt * 128:(tt + 1) * 128, :]

        def norm_transpose(src_fn, tts, acol, bcol_, HT, stack, tok_base=0):
            S.tag = "norm"
            xb = RR([sb([128, D], F32, "xb", stack) for _ in range(6)])
            xnb = RR([sb([128, D], F32, "xn", stack) for _ in range(8)])
            junk = sb([128, D], BF, "junk", stack)
            psr = RR([PS[0], PS[1], PS[2], PS[3]])
            for g0 in range(0, len(tts), 4):
                grp = tts[g0:g0 + 4]
                xns = []
                for tt in grp:
                    x = xb.get()
                    S.dma(x[:], src_fn(tt))
                    ss = scol()
                    S.act(junk[:], x[:], AF.Square, accum_out=ss)
                    r = rstd_from(ss, D)
                    xn = xnb.get()
                    S.ts(xn[:], x[:], r, None, ALU.mult)
                    xns.append(xn)
                for k in range(8):
                    ps = psr.get()
                    for i, xn in enumerate(xns):
                        S.transpose(ps[:, i * 128:(i + 1) * 128], xn[:, k * 128:(k + 1) * 128], ident[:])
                    i = 0
                    while i < len(grp):
                        j = i
                        while j + 1 < len(grp) and (grp[j + 1] < 2) == (grp[i] < 2):
                            j += 1
                        t0 = grp[i] * 128 - tok_base
                        if k % 2 == 0:
                            S.act(HT[:, k, t0:t0 + (j - i + 1) * 128], ps[:, i * 128:(j + 1) * 128], AF.Identity,
                                  scale=acol(grp[i])[:, k:k + 1], bias=bcol_(grp[i])[:, k:k + 1])
                        else:
                            S.ts(HT[:, k, t0:t0 + (j - i + 1) * 128], ps[:, i * 128:(j + 1) * 128],
                                 acol(grp[i])[:, k:k + 1], bcol_(grp[i])[:, k:k + 1], ALU.mult, ALU.add)
                        i = j + 1

        def load_w(dst, wb_ap, r0, nk, c0, ncols, q="sp"):
            S.dma(dst, wb_ap[r0:r0 + nk * 128, c0:c0 + ncols].rearrange("(k p) n -> p k n", p=128), q=q)

        CHUNKS0 = [(0, 256)] + [(256 + 512 * i, 512) for i in range(4)]

        def rope_apply(dst, ps, psr, tab, prange, tok0, n, tmpa, tmpb):
            p0, p1 = prange
            S.tt(tmpa[p0:p1, 0:n], ps[p0:p1, 0:n], tab[p0:p1, 0, tok0:tok0 + n], ALU.mult)
            S.tt(tmpb[p0:p1, 0:n], psr[p0:p1, 0:n], tab[p0:p1, 1, tok0:tok0 + n], ALU.mult)
            S.tt(dst, tmpa[p0:p1, 0:n], tmpb[p0:p1, 0:n], ALU.add, eng="pool")

        def residual_update(s, layer, which, tt, psy, xsrc, gtile, dst, stack_bufs):
            xb, tb, junk = stack_bufs
            s1 = scol(); s2 = scol()
            S.act(junk[:, 0:512], psy[0][:, :], AF.Square, accum_out=s1)
            S.act(junk[:, 512:1024], psy[1][:, :], AF.Square, accum_out=s2)
            S.tt(s1, s1, s2, ALU.add)
            r = rstd_from(s1, D)
            x = xsrc
            t = tb.get()
            for h in range(2):
                hs = slice(h * 512, (h + 1) * 512)
                S.stt(t[:, hs], psy[h][:, :], r, gtile[:, hs], ALU.mult, ALU.mult)
                S.tt(t[:, hs], t[:, hs], x[:, hs], ALU.add, eng="pool")
            S.dma(dst, t[:])

        def mixer0(s, HT, OT, stack):
            r_l, r_c = s, NS
            with phase() as st:
                CQN = sb([128, 3, NT], BF, "CQN", st)
                CKVN = sb([128, 2, NT], BF, "CKVN", st)
                KRT = sb([128, NT], BF, "KRT", st)
                ropeM = sb([128, 2, L], F32, "ropeM", st)
                S.dma(ropeM[64:96], I["ropeM"][64:96])
                tmpa = sb([128, 512], F32, "tmpa", st)
                tmpb = sb([128, 512], F32, "tmpb", st)
                S.tag = "lat"
                with phase() as st2:
                    wl = sb([128, 8, 640], BF, "wl", st2)
                    wkr = sb([128, 8, 64], BF, "wkr", st2)
                    load_w(wl[:], WB["w0in"], 0, 8, 0, 640)
                    load_w(wkr[:], WB["w0in"], 0, 8, 3200, 64)
                    cf = [sb([128, 512], F32, "cf", st2) for _ in range(5)]
                    sq = [sb([128, 512], BF, "sq", st2) for _ in range(5)]
                    rs = sb([128, 512], F32, "rs", st2)
                    for (c0, n) in CHUNKS0:
                        for (base, nt_, gcol, dst, nf) in ((0, 3, qng, CQN, 384), (3, 2, kvng, CKVN, 256)):
                            for ct in range(nt_):
                                ps = PS[ct]
                                S.mmacc(ps[:, 0:n], [(wl[:, k, (base + ct) * 128:(base + ct + 1) * 128], HT[:, k, c0:c0 + n]) for k in range(8)])
                                S.copy(cf[base + ct][:, 0:n], ps[:, 0:n], eng="act")
                                S.tt(sq[base + ct][:, 0:n], cf[base + ct][:, 0:n], cf[base + ct][:, 0:n], ALU.mult, eng="pool")
                            pss = PS[3]
                            S.mmacc(pss[:, 0:n], [(ones_bf[:, :], sq[base + ct][:, 0:n]) for ct in range(nt_)])
                            S.act(rs[:, 0:n], pss[:, 0:n], AF.Sqrt, bias=EPS, scale=1.0 / nf)
                            S.recip(rs[:, 0:n], rs[:, 0:n])
                            for ct in range(nt_):
                                S.stt(dst[:, ct, c0:c0 + n], cf[base + ct][:, 0:n], gcol[:, ct:ct + 1], rs[:, 0:n], ALU.mult, ALU.mult)
                        pk = PS[4]; pkr = PS[5]
                        S.mmacc(pk[64:96, 0:n], [(wkr[:, k, 0:32], HT[:, k, c0:c0 + n]) for k in range(8)])
                        if c0 == 0:
                            S.copy(KRT[64:96, 0:n], pk[64:96, 0:n], eng="act")
                        else:
                            S.mmacc(pkr[64:96, 0:n], [(wkr[:, k, 32:64], HT[:, k, c0:c0 + n]) for k in range(8)])
                            rope_apply(KRT[64:96, c0:c0 + n], pk, pkr, ropeM, (64, 96), c0 - C, n, tmpa, tmpb)
                S.tag = "mla"
                with phase() as st2:
                    wuq = sb([128, 3, 1024], BF, "wuq", st2)
                    wukv = sb([128, 2, 1024], BF, "wukv", st2)
                    load_w(wuq[:], WB["w_uq"], 0, 3, 0, 1024)
                    load_w(wukv[:], WB["w_ukv"], 0, 2, 0, 1024)
                    QHb = RR([sb([128, NT], BF, "QH", st2) for _ in range(2)])
                    KHb = RR([sb([128, NT], BF, "KH", st2) for _ in range(2)])
                    VHb = [sb([128, 18, 128], BF, "VH", st2) for _ in range(2)]
                    for v in VHb:
                        S.memset(v[:, :, 64:128], 1.0)
                    VHr = RR(VHb)
                    PTb = RR([sb([128, 512], BF, "PT", st2) for _ in range(6)])
                    dtmp = RR([sb([64, 512], F32, "dtmp", st2) for _ in range(2)])
                    scale = 96 ** -0.5
                    hb_ = {}

                    def mla_proj(h):
                        QH = QHb.get(); KH = KHb.get(); VH = VHr.get()
                        hb_[h] = (QH, KH, VH)
                        for (c0, n) in CHUNKS0:
                            pq = PS[0]; pqr = PS[1]; pk = PS[2]
                            S.mmacc(pq[0:96, 0:n], [(wuq[:, k, h * 96:(h + 1) * 96], CQN[:, k, c0:c0 + n]) for k in range(3)])
                            S.mmacc(pk[0:64, 0:n], [(wukv[:, k, h * 64:(h + 1) * 64], CKVN[:, k, c0:c0 + n]) for k in range(2)])
                            S.copy(KH[0:64, c0:c0 + n], pk[0:64, 0:n], eng="act")
                            if c0 == 0:
                                S.copy(QH[0:96, 0:n], pq[0:96, 0:n], eng="act")
                            else:
                                S.mmacc(pqr[64:96, 0:n], [(wuq[:, k, 768 + h * 32:768 + (h + 1) * 32], CQN[:, k, c0:c0 + n]) for k in range(3)])
                                S.copy(QH[0:64, c0:c0 + n], pq[0:64, 0:n], eng="act")
                                rope_apply(QH[64:96, c0:c0 + n], pq, pqr, ropeM, (64, 96), c0 - C, n, tmpa, tmpb)
                        S.copy(KH[64:96, :], KRT[64:96, :], eng="pool")
                        for t8 in range(0, 18, 8):
                            nn = min(8, 18 - t8)
                            psv = PS[1]
                            for i in range(nn):
                                tt = t8 + i
                                S.mmacc(psv[:, i * 64:(i + 1) * 64], [(CKVN[:, k, tt * 128:(tt + 1) * 128], wukv[:, k, 512 + h * 64:512 + (h + 1) * 64]) for k in range(2)])
                            S.copy(VH[:, t8:t8 + nn, 0:64], psv[:, 0:nn * 64].rearrange("p (a b) -> p a b", b=64))

                    def mla_attn(h):
                        QH, KH, VH = hb_.pop(h)
                        units = []
                        for ci, (c0, n) in enumerate(CHUNKS0):
                            nk = 2 if c0 == 0 else 18
                            for kt in range(nk):
                                units.append((ci, c0, n, kt, nk))
                        psS = RR([PS[3], PS[4], PS[5]])
                        psO = RR([PS[6], PS[7]])
                        cur_o = {}
                        pend = []

                        def do_pv(u, pt):
                            ci, c0, n, kt, nk = u
                            if kt == 0:
                                cur_o[ci] = psO.get()
                            po = cur_o[ci]
                            S.mm(po[:, 0:n], VH[:, kt, :], pt[:, 0:n], start=(kt == 0), stop=(kt == nk - 1))
                            if kt == nk - 1:
                                dt = dtmp.get()
                                S.copy(dt[0:64, 0:n], po[64:128, 0:n], eng="act")
                                S.recip(dt[0:64, 0:n], dt[0:64, 0:n])
                                S.tt(OT[(h % 2) * 64:(h % 2) * 64 + 64, h // 2, c0:c0 + n], po[0:64, 0:n], dt[0:64, 0:n], ALU.mult)

                        for u in units:
                            ci, c0, n, kt, nk = u
                            pss = psS.get()
                            S.mm(pss[:, 0:n], KH[0:96, kt * 128:(kt + 1) * 128], QH[0:96, c0:c0 + n])
                            pt = PTb.get()
                            S.act(pt[:, 0:n], pss[:, 0:n], AF.Exp, scale=scale)
                            pend.append((u, pt))
                            if len(pend) > 2:
                                do_pv(*pend.pop(0))
                        while pend:
                            do_pv(*pend.pop(0))
                    mla_proj(0)
                    for h in range(8):
                        if h + 1 < 8:
                            mla_proj(h + 1)
                        mla_attn(h)
            S.tag = "diff"
            with phase() as st2:
                tmpa = sb([128, 512], F32, "tmpa", st2)
                tmpb = sb([128, 512], F32, "tmpb", st2)
                ropeD = sb([128, 2, L], F32, "ropeD", st2)
                S.dma(ropeD[:], I["ropeD"])
                wdb = RR([sb([128, 8, 5, 128], BF, "wd", st2) for _ in range(2)])
                DQb = RR([sb([128, NT], BF, "DQ", st2) for _ in range(2)])
                DKb = RR([sb([128, NT], BF, "DK", st2) for _ in range(2)])
                DKzb = RR([[sb([128, NT], BF, "DKz", st2) for _ in range(2)] for _ in range(2)])
                for pair_ in DKzb.items:
                    S.memset(pair_[0][64:128, :], 0.0, eng="pool")
                    S.memset(pair_[1][0:64, :], 0.0, eng="pool")
                DVb = RR([sb([128, 18, 128], BF, "DV", st2) for _ in range(2)])
                PTb = RR([sb([128, 512], BF, "PT", st2) for _ in range(6)])
                esets = [(sb([128, 512], F32, "a1", st2), sb([128, 512], F32, "a2", st2), sb([128, 512], F32, "r1", st2),
                          sb([128, 512], F32, "r2", st2), sb([128, 512], BF, "sqd", st2)) for _ in range(2)]
                scale = 64 ** -0.5
                dh_ = {}

                def diff_proj(h):
                    wd = wdb.get()
                    for i, cb in enumerate((640, 1152, 1664, 2176, 2688)):
                        load_w(wd[:, :, i, :], WB["w0in"], 0, 8, cb + h * 128, 128)
                    DQ = DQb.get(); DK = DKb.get(); DV = DVb.get(); DKz = DKzb.get()
                    for (c0, n) in CHUNKS0:
                        for (dst, wi) in ((DQ, 0), (DK, 2)):
                            pa = PS[0]; pb = PS[1]
                            S.mmacc(pa[:, 0:n], [(wd[:, k, wi, :], HT[:, k, c0:c0 + n]) for k in range(8)])
                            if c0 == 0:
                                S.copy(dst[:, 0:n], pa[:, 0:n], eng="act")
                            else:
                                S.mmacc(pb[:, 0:n], [(wd[:, k, wi + 1, :], HT[:, k, c0:c0 + n]) for k in range(8)])
                                rope_apply(dst[:, c0:c0 + n], pa, pb, ropeD, (0, 128), c0 - C, n, tmpa, tmpb)
                    for t4 in range(0, 18, 4):
                        ps = PS[1]
                        nn = min(4, 18 - t4)
                        for i in range(nn):
                            tt = t4 + i
                            S.mmacc(ps[:, i * 128:(i + 1) * 128], [(HT[:, k, tt * 128:(tt + 1) * 128], wd[:, k, 4, :]) for k in range(8)])
                        S.copy(DV[:, t4:t4 + nn, :], ps[:, 0:nn * 128].rearrange("p (a b) -> p a b", b=128), eng="act")
                    S.copy(DKz[0][0:64, :], DK[0:64, :], eng="pool")
                    S.copy(DKz[1][64:128, :], DK[64:128, :], eng="pool")
                    dh_[h] = (DQ, DV, DKz)

                def diff_attn(h):
                    DQ, DV, DKz = dh_.pop(h)
                    deferred = []
                    for ci, (c0, n) in enumerate(CHUNKS0):
                        nk = 2 if c0 == 0 else 18
                        psSj = [PS[3], PS[4]]
                        pos = [(PS[5], PS[6]), (PS[7], PS[2])]
                        a1, a2, r1, r2, sqd = esets[ci % 2]
                        acc = [a1, a2]
                        pend = []

                        def do_pv(kt, j, pt, n=n, nk=nk):
                            po, pd = pos[j]
                            S.mm(po[:, 0:n], DV[:, kt, :], pt[:, 0:n], start=(kt == 0), stop=(kt == nk - 1))
                            S.mm(pd[:, 0:n], ones_bf[:, :], pt[:, 0:n], start=(kt == 0), stop=(kt == nk - 1))

                        for kt in range(nk):
                            for j in range(2):
                                pss = psSj[j]
                                S.mm(pss[:, 0:n], DKz[j][:, kt * 128:(kt + 1) * 128], DQ[:, c0:c0 + n])
                                pt = PTb.get()
                                S.act(pt[:, 0:n], pss[:, 0:n], AF.Exp, scale=scale)
                                pend.append((kt, j, pt))
                            while len(pend) > 2:
                                do_pv(*pend.pop(0))
                            if kt == min(3, nk - 1) and deferred:
                                deferred.pop(0)()
                        while pend:
                            do_pv(*pend.pop(0))
                        for j in range(2):
                            po, pd = pos[j]
                            rr_ = r1 if j == 0 else r2
                            S.recip(rr_[:, 0:n], pd[:, 0:n])
                            S.tt(acc[j][:, 0:n], po[:, 0:n], rr_[:, 0:n], ALU.mult)
                        S.stt(a1[:, 0:n], a2[:, 0:n], nlam[:, 0:1], a1[:, 0:n], ALU.mult, ALU.add)
                        S.tt(sqd[:, 0:n], a1[:, 0:n], a1[:, 0:n], ALU.mult, eng="pool")

                        def fin(a1=a1, r1=r1, sqd=sqd, c0=c0, n=n):
                            pss = PS[0]
                            S.mm(pss[:, 0:n], ones_bf[:, :], sqd[:, 0:n])
                            S.act(r1[:, 0:n], pss[:, 0:n], AF.Sqrt, bias=EPS, scale=1.0 / 128)
                            S.recip(r1[:, 0:n], r1[:, 0:n])
                            S.stt(OT[:, 4 + h, c0:c0 + n], a1[:, 0:n], subg[:, 0:1], r1[:, 0:n], ALU.mult, ALU.mult)
                        deferred.append(fin)
                    while deferred:
                        deferred.pop(0)()

                diff_proj(0)
                for h in range(4):
                    if h + 1 < 4:
                        diff_proj(h + 1)
                    diff_attn(h)

        def wout_residual(s, layer, OT, tts, tok_base):
            S.tag = "wout"
            with phase() as st:
                wo = sb([128, 8, D], BF, "wo", st)
                load_w(wo[:], WB["w_out"], layer * D, 8, 0, D)
                G = {}
                for which, r in (("l", s), ("c", NS)):
                    if which == "c" and layer == 1:
                        continue
                    G[which] = sb([128, D], F32, "G", st)
                    S.dma(G[which][:], GSC[layer, r, 0:1, :].partition_broadcast(128))
                bufs = (RR([sb([128, D], F32, "xr", st) for _ in range(4)]), RR([sb([128, D], F32, "tr", st) for _ in range(3)]),
                        sb([128, D], BF, "junk", st))
                psr = RR([(PS[0], PS[1]), (PS[2], PS[3]), (PS[4], PS[5])])
                xq = []
                PF = 2

                def pref(i_):
                    if i_ < len(tts):
                        x_ = bufs[0].get()
                        S.dma(x_[:], xtile_src(layer, s, tts[i_]))
                        xq.append(x_)
                for i_ in range(PF):
                    pref(i_)
                for i_, tt in enumerate(tts):
                    pref(i_ + PF)
                    psy = psr.get()
                    t0 = tt * 128 - tok_base
                    for h in range(2):
                        S.mmacc(psy[h][:, :], [(OT[:, k, t0:t0 + 128], wo[:, k, h * 512:(h + 1) * 512]) for k in range(8)])
                    residual_update(s, layer, 0, tt, psy, xq.pop(0), G["c" if tt < 2 else "l"],
                                    XD[s, tt * 128:(tt + 1) * 128, :], bufs)

        def ffn(s, layer, last, chk=lambda n: None):
            tts = list(range(18)) if layer == 0 else list(range(2, 18))
            tok_base = 0 if layer == 0 else C
            ntok = NT - tok_base
            with phase() as st:
                HT = sb([128, 8, ntok], BF, "HT2", st)
                with phase() as st2:
                    norm_transpose(lambda tt: XD[s, tt * 128:(tt + 1) * 128, :], tts,
                                   lambda tt: modA[:, layer, 2, NS if tt < 2 else s, :],
                                   lambda tt: modA[:, layer, 3, NS if tt < 2 else s, :], HT, st2, tok_base)
                chk("ffn_norm")
                S.tag = "ffn"
                wdn = sb([128, 22, D], BF, "wdn", st)
                load_w(wdn[:], WB["w_down"], layer * DFF, 22, 0, D)
                G = {}
                for which, r in (("l", s), ("c", NS)):
                    if which == "c" and layer == 1:
                        continue
                    G[which] = sb([128, D], F32, "G3", st)
                    S.dma(G[which][:], GSC[layer, r, 1:2, :].partition_broadcast(128))
                GT = sb([128, 22, 1024], BF, "GT", st)
                HALO = sb([128, 44, 2], F32, "HALO", st)
                wub = RR([sb([128, 8, 2, 128], BF, "wu", st) for _ in range(3)])
                Yb = RR([sb([128, 1026], F32, "Y", st) for _ in range(4)])
                Ub = RR([sb([128, 1024], F32, "U", st) for _ in range(4)])
                bufs = (RR([sb([128, D], F32, "xr", st) for _ in range(2)]), RR([sb([128, D], F32, "tr", st) for _ in range(2)]),
                        sb([128, D], BF, "junk", st))
                scs = []
                if layer == 0:
                    scs.append((0, 256, 0))
                scs += [(256, 1024, 1), (1280, 1024, 2)]
                psU = RR([PS[0], PS[1], PS[2], PS[3]])
                psH = RR([PS[4], PS[5]])
                psD = RR([(PS[6], PS[7]), (PS[4], PS[5])])
                chk("ffn_load")
                for (t0, n, kind) in scs:
                    a0 = t0 - tok_base
                    wq_ = []

                    def wpref(j_):
                        if j_ < 22:
                            w_ = wub.get()
                            load_w(w_[:, :, 0, :], WB["w_up"], layer * D, 8, j_ * 128, 128)
                            load_w(w_[:, :, 1, :], WB["w_up"], layer * D, 8, DFF + j_ * 128, 128)
                            wq_.append(w_)
                    wpref(0)
                    wpref(1)
                    for j in range(22):
                        wpref(j + 2)
                        wu = wq_.pop(0)
                        Us = []
                        for ag in range(2):
                            Y = Yb.get(); U = Ub.get()
                            jj = ag * 22 + j
                            for c in range(0, n, 512):
                                nn = min(512, n - c)
                                ps = psU.get()
                                S.mmacc(ps[:, 0:nn], [(wu[:, k, ag, :], HT[:, k, a0 + c:a0 + c + nn]) for k in range(8)])
                                S.copy(Y[:, 1 + c:1 + c + nn], ps[:, 0:nn], eng="act")
                                S.act(U[:, c:c + nn], ps[:, 0:nn], AF.Identity, scale=fcw[:, layer, jj, 1:2], bias=fcb[:, layer, jj:jj + 1])
                            if kind == 1:
                                ph = psH.get()
                                S.mmacc(ph[:, 0:2], [(wu[:, k, ag, :], HT[:, k, a0 + n - 1:a0 + n + 1]) for k in range(8)])
                                S.copy(Y[:, n + 1:n + 2], ph[:, 1:2], eng="act")
                                S.copy(HALO[:, jj, 0:1], ph[:, 0:1], eng="act")
                                S.memset(Y[:, 0:1], 0.0, eng="pool")
                            elif kind == 2:
                                S.copy(Y[:, 0:1], HALO[:, jj, 0:1], eng="pool")
                                S.memset(Y[:, n + 1:n + 2], 0.0, eng="pool")
                            else:
                                S.memset(Y[:, 0:1], 0.0, eng="pool")
                                S.memset(Y[:, n + 1:n + 2], 0.0, eng="pool")
                            S.stt(U[:, 0:n], Y[:, 0:n], fcw[:, layer, jj, 0:1], U[:, 0:n], ALU.mult, ALU.add)
                            S.stt(U[:, 0:n], Y[:, 2:n + 2], fcw[:, layer, jj, 2:3], U[:, 0:n], ALU.mult, ALU.add)
                            Us.append(U)
                        S.act(Us[1][:, 0:n], Us[1][:, 0:n], AF.Silu)
                        S.tt(GT[:, j, 0:n], Us[0][:, 0:n], Us[1][:, 0:n], ALU.mult, eng="pool")
                    chk("ffn_up")
                    xq = []

                    def pref(ti_):
                        if ti_ < n // 128:
                            x_ = bufs[0].get()
                            tt_ = t0 // 128 + ti_
                            S.dma(x_[:], XD[s, tt_ * 128:(tt_ + 1) * 128, :])
                            xq.append(x_)
                    pref(0)
                    for ti in range(n // 128):
                        pref(ti + 1)
                        tt = t0 // 128 + ti
                        psy = psD.get()
                        for h in range(2):
                            S.mmacc(psy[h][:, :], [(GT[:, j, ti * 128:(ti + 1) * 128], wdn[:, j, h * 512:(h + 1) * 512]) for j in range(22)])
                        if last:
                            dst = OUT[s, (tt - 2) * 128:(tt - 1) * 128, :]
                        else:
                            dst = XD[s, tt * 128:(tt + 1) * 128, :]
                        residual_update(s, layer, 1, tt, psy, xq.pop(0), G["c" if tt < 2 else "l"], dst, bufs)
                        chk("ffn_res")

        def mixer1(s, HT, OT, Z):
            S.tag = "m1hyproj"
            with phase() as st2:
                wub = RR([sb([128, 8, 128], BF, "wu1", st2) for _ in range(4)])
                Yb = RR([sb([128, L + 2], F32, "Yh", st2) for _ in range(2)])
                Ub = RR([sb([128, L], F32, "Uh", st2) for _ in range(3)])
                wq_ = []

                def wpref(j_):
                    if j_ < 12:
                        w_ = wub.get()
                        load_w(w_[:], WB["w1in"], 0, 8, 1408 + j_ * 128, 128)
                        wq_.append(w_)
                wpref(0)
                wpref(1)
                for j in range(12):
                    wpref(j + 2)
                    wu = wq_.pop(0)
                    Y = Yb.get()
                    S.memset(Y[:, 0:1], 0.0)
                    S.memset(Y[:, L + 1:L + 2], 0.0)
                    dst = Z[:, j, :] if j < 4 else Ub.get()[:, :]
                    for c in range(4):
                        ps = PS[5 + c % 2]
                        tk = C + c * 512
                        S.mmacc(ps[:, :], [(wu[:, k, :], HT[:, k, tk:tk + 512]) for k in range(8)])
                        S.copy(Y[:, 1 + c * 512:1 + (c + 1) * 512], ps[:, :], eng="act")
                        S.ts(dst[:, c * 512:(c + 1) * 512], ps[:, :], hcw[:, j, 1:2], hcb[:, j:j + 1], ALU.mult, ALU.add)
                    S.stt(dst, Y[:, 0:L], hcw[:, j, 0:1], dst, ALU.mult, ALU.add)
                    S.stt(dst, Y[:, 2:L + 2], hcw[:, j, 2:3], dst, ALU.mult, ALU.add)
                    if j >= 4:
                        jj = j - 4
                        S.dma(X12[jj // 4, (jj % 4) * 128:(jj % 4 + 1) * 128, :], dst)
            S.tag = "m1winproj"
            with phase() as st1:
                tmpa = sb([128, 512], F32, "tmpa", st1)
                tmpb = sb([128, 512], F32, "tmpb", st1)
                QW = sb([64, 8, L], BF, "QW", st1)
                KW = sb([64, 2, NT], BF, "KW", st1)
                VW = sb([128, 18, 2, 128], BF, "VW", st1)
                S.memset(VW[:, :, :, 64:128], 1.0)
                with phase() as st2:
                    wqb = RR([sb([128, 8, 2, 64], BF, "wq", st2) for _ in range(3)])
                    wk = sb([128, 8, 384], BF, "wk", st2)
                    rtb = RR([sb([128, 2, 512], F32, "rt", st2) for _ in range(1)])
                    load_w(wk[:], WB["w1in"], 0, 8, 1024, 384)
                    for c in range(4):
                        rt = rtb.get()
                        S.dma(rt[0:64], I["ropeD"][0:64, :, c * 512:(c + 1) * 512])
                        tk = C + c * 512
                        for hd in range(8):
                            pa = PS[0]; pb = PS[1]
                            wq = wqb.get()
                            load_w(wq[:, :, 0, :], WB["w1in"], 0, 8, hd * 64, 64)
                            load_w(wq[:, :, 1, :], WB["w1in"], 0, 8, 512 + hd * 64, 64)
                            S.mmacc(pa[0:64, :], [(wq[:, k, 0, :], HT[:, k, tk:tk + 512]) for k in range(8)])
                            S.mmacc(pb[0:64, :], [(wq[:, k, 1, :], HT[:, k, tk:tk + 512]) for k in range(8)])
                            rope_apply(QW[0:64, hd, c * 512:(c + 1) * 512], pa, pb, rt, (0, 64), 0, 512, tmpa, tmpb)
                        for kh in range(2):
                            pa = PS[2]; pb = PS[3]
                            S.mmacc(pa[0:64, :], [(wk[:, k, kh * 64:(kh + 1) * 64], HT[:, k, tk:tk + 512]) for k in range(8)])
                            S.mmacc(pb[0:64, :], [(wk[:, k, 128 + kh * 64:128 + (kh + 1) * 64], HT[:, k, tk:tk + 512]) for k in range(8)])
                            rope_apply(KW[0:64, kh, tk:tk + 512], pa, pb, rt, (0, 64), 0, 512, tmpa, tmpb)
                    for kh in range(2):
                        pa = PS[2]
                        S.mmacc(pa[0:64, 0:C], [(wk[:, k, kh * 64:(kh + 1) * 64], HT[:, k, 0:C]) for k in range(8)])
                        S.copy(KW[0:64, kh, 0:C], pa[0:64, 0:C], eng="act")
                    for t4 in range(0, 18, 4):
                        ps = PS[4]
                        nn = min(4, 18 - t4)
                        for i in range(nn):
                            tt = t4 + i
                            S.mmacc(ps[:, i * 128:(i + 1) * 128], [(HT[:, k, tt * 128:(tt + 1) * 128], wk[:, k, 256:384]) for k in range(8)])
                        for kh in range(2):
                            S.copy(VW[:, t4:t4 + nn, kh, 0:64],
                                   ps[:, 0:nn * 128].rearrange("p (a b) -> p a b", b=128)[:, :, kh * 64:(kh + 1) * 64], eng="act")
                S.tag = "m1win"
                with phase() as st2:
                    wm = sb([128, 2, 512], BF, "wm", st2)
                    S.dma(wm[:], I["wmask"])
                    esx = sb([128, 2, 512], F32, "esx", st2)
                    for kh_ in range(2):
                        for g_ in range(4):
                            S.ts(esx[:, kh_, g_ * 128:(g_ + 1) * 128], zeros512[:, 0:128], esink[:, kh_ * 4 + g_:kh_ * 4 + g_ + 1], None, ALU.add)
                    PTb = RR([sb([128, 512], BF, "PTw", st2) for _ in range(6)])
                    dtmp = RR([sb([64, 512], F32, "dtw", st2) for _ in range(2)])
                    psS = RR([PS[0], PS[1], PS[2], PS[3]])
                    psO = RR([PS[4], PS[5]])
                    scale = 64 ** -0.5
                    for kh in range(2):
                        for nb in range(16):
                            kts = [(0, None), (1, None)]
                            if nb > 0:
                                kts.append((2 + nb - 1, 0))
                            kts.append((2 + nb, None))
                            if nb < 15:
                                kts.append((2 + nb + 1, 1))
                            po = psO.get()
                            pts = []
                            for (kt, mk) in kts:
                                pss = psS.get()
                                S.mm(pss[:, :], KW[0:64, kh, kt * 128:(kt + 1) * 128], QW[0:64, kh * 4:(kh + 1) * 4, nb * 128:(nb + 1) * 128])
                                pt = PTb.get()
                                S.act(pt[:, :], pss[:, :], AF.Exp, scale=scale)
                                if mk is not None:
                                    S.tt(pt[:, :], pt[:, :], wm[:, mk, :], ALU.mult, eng="pool")
                                pts.append((kt, pt))
                            for i, (kt, pt) in enumerate(pts):
                                S.mm(po[:, :], VW[:, kt, kh, :], pt[:, :], start=(i == 0), stop=(i == len(pts) - 1))
                            dt = dtmp.get()
                            S.tt(dt[0:64, :], po[64:128, :], esx[64:128, kh, :], ALU.add)
                            S.recip(dt[0:64, :], dt[0:64, :])
                            pov = po[0:64, :].rearrange("p (g q) -> p g q", g=4)
                            dtv = dt[0:64, :].rearrange("p (g q) -> p g q", g=4)
                            for g2 in range(2):
                                S.tt(OT[g2 * 64:g2 * 64 + 64, kh * 2:kh * 2 + 2, nb * 128:(nb + 1) * 128], pov[:, g2::2, :], dtv[:, g2::2, :], ALU.mult)

        def hyena(s, Z, OT, st):
            S.tag = "hyena"
            ZT = sb([128, 16, 512], BF, "ZT", st)
            Yf = sb([128, 32, 512], BF, "Yf", st)
            fmb = RR([sb([128, 16, 128], BF, "fmh", st) for _ in range(4)])
            imb = RR([sb([128, 8, 512], BF, "imh", st) for _ in range(2)])
            spb = RR([sb([128, 3, 512], F32, "sp", st) for _ in range(2)])
            xb = RR([sb([128, 512], F32, "x12", st) for _ in range(2)])
            tA = sb([128, 512], F32, "tA", st)
            tB = sb([128, 512], F32, "tB", st)
            tC = sb([128, 512], F32, "tC", st)
            zb = sb([128, 128], BF, "zb", st)
            for n in range(2):
                for a in range(16):
                    ps = PS[a % 2]
                    for ct in range(4):
                        S.transpose(ps[:, ct * 128:(ct + 1) * 128], Z[:, ct, a * 128:(a + 1) * 128], ident[:])
                    S.copy(ZT[:, a, :], ps[:, :], eng="act")
                for i in range(16):
                    fr_ = fmb.get(); fi_ = fmb.get()
                    S.dma(fr_[:].rearrange("p a f -> p (a f)"), I["FM"][i])
                    S.dma(fi_[:].rearrange("p a f -> p (a f)"), I["FM"][16 + i])
                    sp = spb.get()
                    S.dma(sp[:], SPEC[n, :, i * 128:(i + 1) * 128, :].rearrange("w p c -> p w c"))
                    pr = PS[2 + 2 * (i % 2)]
                    pi = PS[3 + 2 * (i % 2)]
                    S.mmacc(pr[:, :], [(fr_[:, a, :], ZT[:, a, :]) for a in range(16)])
                    S.mmacc(pi[:, :], [(fi_[:, a, :], ZT[:, a, :]) for a in range(16)])
                    S.tt(tA[:], pr[:, :], sp[:, 0, :], ALU.mult)
                    S.tt(tB[:], pi[:, :], sp[:, 1, :], ALU.mult)
                    S.tt(Yf[:, i, :], tA[:], tB[:], ALU.subtract, eng="pool")
                    S.tt(tC[:], pr[:, :], sp[:, 1, :], ALU.mult)
                    S.tt(tB[:], pi[:, :], sp[:, 2, :], ALU.mult)
                    S.tt(Yf[:, 16 + i, :], tC[:], tB[:], ALU.add, eng="pool")
                for c in range(4):
                    pss = [PS[4 + ct] for ct in range(4)]
                    for f8 in range(4):
                        im = imb.get()
                        S.dma(im[:].rearrange("p a t -> p (a t)"), I["IM"][f8, c])
                        for ct in range(4):
                            for a in range(8):
                                fa = f8 * 8 + a
                                S.mm(pss[ct][:, :], Yf[:, fa, ct * 128:(ct + 1) * 128], im[:, a, :], start=(fa == 0), stop=(fa == 31))
                    for ct in range(4):
                        x = xb.get()
                        S.dma(x[:], X12[n, ct * 128:(ct + 1) * 128, c * 512:(c + 1) * 512])
                        zs = Z[:, ct, c * 512:(c + 1) * 512]
                        S.stt(tA[:], zs, hbias[:, n, ct:ct + 1], pss[ct][:, :], ALU.mult, ALU.add)
                        if n == 0:
                            S.tt(zs, tA[:], x[:], ALU.mult)
                        else:
                            S.tt(OT[:, 4 + ct, c * 512:(c + 1) * 512], tA[:], x[:], ALU.mult)

        def main_body(chk):
            for s in range(NS):
                with phase() as st:
                    HT = sb([128, 8, NT], BF, "HT", st)
                    OT = sb([128, 8, NT], BF, "OT", st)
                    with phase() as st2:
                        norm_transpose(lambda tt: xtile_src(0, s, tt), list(range(18)),
                                       lambda tt: modA[:, 0, 0, NS if tt < 2 else s, :],
                                       lambda tt: modA[:, 0, 1, NS if tt < 2 else s, :], HT, st2)
                    chk("norm0")
                    mixer0(s, HT, OT, st)
                    chk("mixer0")
                    if dbg and "ot0" in DBG and s == 0:
                        with phase() as std:
                            otf = sb([128, 8, NT], F32, "otf", std)
                            S.copy(otf[:], OT[:])
                            S.dma(DBG["ot0"].rearrange("(k p) t -> p k t", p=128), otf[:])
                    wout_residual(s, 0, OT, list(range(18)), 0)
                    chk("wout0")
                ffn(s, 0, False, chk)
                chk("ffn0")
                S.barrier()
                if dbg and "xd0" in DBG and s == 0:
                    for tt_ in range(18):
                        S.dma(DBG["xd0"][tt_ * 128:(tt_ + 1) * 128, :], XD[0, tt_ * 128:(tt_ + 1) * 128, :])
                    S.barrier()
                with phase() as st:
                    OT = sb([128, 8, L], BF, "OT1", st)
                    Z = sb([128, 4, L], F32, "Z", st)
                    with phase() as stH:
                        HT = sb([128, 8, NT], BF, "HT1", stH)
                        with phase() as st2:
                            norm_transpose(lambda tt: xtile_src(1, s, tt), list(range(18)),
                                           lambda tt: modA[:, 1, 0, NS if tt < 2 else s, :],
                                           lambda tt: modA[:, 1, 1, NS if tt < 2 else s, :], HT, st2)
                        chk("norm1")
                        mixer1(s, HT, OT, Z)
                        chk("mixer1")
                    with phase() as stY:
                        hyena(s, Z, OT, stY)
                        chk("hyena")
                    if dbg and "ot1" in DBG and s == 0:
                        otf = sb([128, 8, L], F32, "otf", st)
                        S.copy(otf[:], OT[:])
                        S.dma(DBG["ot1"].rearrange("(k p) t -> p k t", p=128), otf[:])
                        S.barrier()
                    wout_residual(s, 1, OT, list(range(2, 18)), C)
                    S.barrier()
                ffn(s, 1, True)
                S.barrier()

        class _Stop(Exception):
            pass

        def chk(name):
            if stop == name:
                raise _Stop()

        try:
            with phase() as pc_:
                prologue_cast(pc_)
                S.default_q = "act"
                pmode["engs"] = ("pe", "act", "dve")
                pmode["keep"] = ("cf_", "cb_", "wb_", "w0in", "w_uq", "w_ukv", "w1in", "w_out", "w_up", "w_down")
                prologue_mod()
                chk("mod")
                prologue_hyena()
                chk("hyp")
                pmode["engs"] = None
                pmode["keep"] = ()
                S.default_q = "sp"
            main_body(chk)
        except _Stop:
            pass
        S.finish()
    return nc


def kernel(**inputs):
    NCORE = 8
    NS = 4
    nc = build_program(NS)
    in_maps = [_host_layout(inputs, NS, c) for c in range(NCORE)]
    res = run_bass_kernel_spmd(nc, in_maps, core_ids=list(range(NCORE)))
    out = np.concatenate([np.asarray(r["y"], np.float32) for r in res.results], axis=0)
    return out
```
